# Optimizing a Trainium2 kernel written in Bass

```python
import jax, jax.numpy as jnp
from jax import lax
import numpy as np

D_MODEL = 1024
BATCH = 16
SEQ = 256
DEPTH = 2
DEC_BATCH = 4
DEC_SEQ = 2048
PAST_LEN = 256

GRID_W = 64
N_EVEN = (DEPTH + 1) // 2
N_ODD = DEPTH // 2
HEAD_DIM = 64
N_HEADS = 8
N_KV_HEADS = 2
GROUP = N_HEADS // N_KV_HEADS
ATT_WIDTH = N_HEADS * HEAD_DIM
KV_WIDTH = N_KV_HEADS * HEAD_DIM
WINDOW = 128
BLK = 128
ROPE_BASE = 10000.0
CONV_CH = 512
CONV_K = 31
IN_EVEN = ATT_WIDTH + 2 * KV_WIDTH + 2 * CONV_CH
MIX_EVEN = ATT_WIDTH + CONV_CH
LRU_WIDTH = 1024
LRU_BLOCKS = 8
LRU_BW = LRU_WIDTH // LRU_BLOCKS
LRU_CONV_K = 4
LRU_C = 8.0
D_FF = 4 * D_MODEL
EPS = 1e-6
NEG_INF = -1e30

kernel_name = "hybrid_dit_swa_conformer_rglru_step"


def _rmsnorm(x, g):
    xf = x.astype(jnp.float32)
    y = xf * lax.rsqrt(jnp.mean(xf * xf, axis=-1, keepdims=True) + EPS)
    return (y * g.astype(jnp.float32)).astype(x.dtype)


def _layernorm(x, g, b):
    xf = x.astype(jnp.float32)
    mu = jnp.mean(xf, axis=-1, keepdims=True)
    var = jnp.mean(jnp.square(xf - mu), axis=-1, keepdims=True)
    y = (xf - mu) * lax.rsqrt(var + EPS)
    return (y * g.astype(jnp.float32) + b.astype(jnp.float32)).astype(x.dtype)


def _adaln(cond, w_mod, b_mod):
    return jnp.split(jax.nn.silu(cond) @ w_mod + b_mod, 6, axis=-1)


def _modulate(x, g, shift, scale):
    return _rmsnorm(x, g) * (1 + scale) + shift


def _sq_relu_mlp(h, w1, w2):
    return jnp.square(jax.nn.relu(h @ w1)) @ w2


def _dwconv(x, w, b, pad_l, pad_r):
    y = lax.conv_general_dilated(
        x, w[:, None, :].astype(x.dtype), window_strides=(1,),
        padding=[(pad_l, pad_r)], dimension_numbers=('NWC', 'WIO', 'NWC'),
        feature_group_count=x.shape[-1])
    return y + b


def _rope_axis(x, pos):
    n = x.shape[-1] // 2
    inv = ROPE_BASE ** (-jnp.arange(n, dtype=jnp.float32) / n)
    ang = pos.astype(jnp.float32)[:, None] * inv[None, :]
    cos = jnp.cos(ang)[:, None, :]
    sin = jnp.sin(ang)[:, None, :]
    x1, x2 = x[..., :n], x[..., n:]
    return jnp.concatenate([x1 * cos - x2 * sin, x2 * cos + x1 * sin], axis=-1)


def _axial_rope(x):
    T = x.shape[1]
    rows = T // GRID_W
    row = jnp.repeat(jnp.arange(rows), GRID_W)
    col = jnp.tile(jnp.arange(GRID_W), rows)
    xf = x.astype(jnp.float32)
    h = HEAD_DIM // 2
    out = jnp.concatenate([_rope_axis(xf[..., :h], row), _rope_axis(xf[..., h:], col)], axis=-1)
    return out.astype(x.dtype)


def _attend(q, k, v, mask, sink):
    B, Q = q.shape[0], q.shape[1]
    s = jnp.einsum('bqkgd,bskd->bkgqs', q, k).astype(jnp.float32) * (HEAD_DIM ** -0.5)
    if mask is not None:
        s = jnp.where(mask, s, NEG_INF)
    sink_col = jnp.broadcast_to(sink.astype(jnp.float32)[None, :, :, None, None],
                                (B, N_KV_HEADS, GROUP, Q, 1))
    p = jax.nn.softmax(jnp.concatenate([s, sink_col], axis=-1), axis=-1)[..., :-1]
    return jnp.einsum('bkgqs,bskd->bqkgd', p.astype(v.dtype), v)


def _context_attention(q, k, v, sink):
    B, C = q.shape[0], q.shape[1]
    nq = C // BLK
    qb = q.reshape(B, nq, BLK, N_KV_HEADS, GROUP, HEAD_DIM).transpose(1, 0, 2, 3, 4, 5)
    o = lax.map(lambda qi: _attend(qi, k, v, None, sink), qb)
    return o.transpose(1, 0, 2, 3, 4, 5).reshape(B, C, ATT_WIDTH)


def _latent_attention(q, k, v, ck, cv, sink):
    B, T = q.shape[0], q.shape[1]
    nb = T // BLK
    C = ck.shape[1]
    qb = q.reshape(B, nb, BLK, N_KV_HEADS, GROUP, HEAD_DIM).transpose(1, 0, 2, 3, 4, 5)
    kp = jnp.pad(k, ((0, 0), (BLK, BLK), (0, 0), (0, 0)))
    vp = jnp.pad(v, ((0, 0), (BLK, BLK), (0, 0), (0, 0)))
    a_idx = jnp.arange(BLK)[:, None]
    b_idx = jnp.arange(3 * BLK)[None, :]
    ctx_mask = jnp.ones((BLK, C), dtype=bool)

    def block(args):
        n, qi = args
        kw = lax.dynamic_slice_in_dim(kp, n * BLK, 3 * BLK, axis=1)
        vw = lax.dynamic_slice_in_dim(vp, n * BLK, 3 * BLK, axis=1)
        i = n * BLK + a_idx
        j = n * BLK - BLK + b_idx
        m = (jnp.abs(i - j) <= WINDOW) & (j >= 0) & (j < T)
        return _attend(qi, jnp.concatenate([kw, ck], axis=1), jnp.concatenate([vw, cv], axis=1),
                       jnp.concatenate([m, ctx_mask], axis=1), sink)

    o = lax.map(block, (jnp.arange(nb), qb))
    return o.transpose(1, 0, 2, 3, 4, 5).reshape(B, T, ATT_WIDTH)


def _conformer_conv(u, w, b, g, beta):
    a, gate = jnp.split(u, 2, axis=-1)
    z = _dwconv(a * jax.nn.sigmoid(gate), w, b, CONV_K // 2, CONV_K // 2)
    return jax.nn.silu(_layernorm(z, g, beta))


def _even_split(h, w_in):
    B, T = h.shape[0], h.shape[1]
    q, k, v, u = jnp.split(h @ w_in, [ATT_WIDTH, ATT_WIDTH + KV_WIDTH, ATT_WIDTH + 2 * KV_WIDTH], axis=-1)
    return (q.reshape(B, T, N_HEADS, HEAD_DIM), k.reshape(B, T, N_KV_HEADS, HEAD_DIM),
            v.reshape(B, T, N_KV_HEADS, HEAD_DIM), u)


def _even_ctx(h, w_in, w_out, sink, cw, cb, cg, cbeta):
    q, k, v, u = _even_split(h, w_in)
    o_att = _context_attention(q, k, v, sink.reshape(N_KV_HEADS, GROUP))
    o_conv = _conformer_conv(u, cw, cb, cg, cbeta)
    return jnp.concatenate([o_att, o_conv], axis=-1) @ w_out, k, v


def _even_lat(h, ck, cv, w_in, w_out, sink, cw, cb, cg, cbeta):
    q, k, v, u = _even_split(h, w_in)
    q = _axial_rope(q)
    k = _axial_rope(k)
    o_att = _latent_attention(q, k, v, ck, cv, sink.reshape(N_KV_HEADS, GROUP))
    o_conv = _conformer_conv(u, cw, cb, cg, cbeta)
    return jnp.concatenate([o_att, o_conv], axis=-1) @ w_out


def _lru_coeffs(xc, wa, ba, wx, bx, lam):
    B, T = xc.shape[0], xc.shape[1]
    xb = xc.reshape(B, T, LRU_BLOCKS, LRU_BW)
    r = jax.nn.sigmoid(jnp.einsum('btnc,ncd->btnd', xb, wa.astype(jnp.float32)).reshape(B, T, LRU_WIDTH)
                       + ba.astype(jnp.float32))
    i = jax.nn.sigmoid(jnp.einsum('btnc,ncd->btnd', xb, wx.astype(jnp.float32)).reshape(B, T, LRU_WIDTH)
                       + bx.astype(jnp.float32))
    log_a = -LRU_C * r * jax.nn.softplus(-lam.astype(jnp.float32))
    a = jnp.exp(log_a)
    return a, jnp.sqrt(-jnp.expm1(2.0 * log_a)) * (i * xc)


def _linear_scan(a, b, h0, reverse):
    idx = -1 if reverse else 0
    b = b.at[:, idx].add(a[:, idx] * h0)

    def comb(l, r):
        al, bl = l
        ar, br = r
        return al * ar, ar * bl + br

    _, h = lax.associative_scan(comb, (a, b), reverse=reverse, axis=1)
    return h


def _odd_core(h, h0, w_in, w_out, cw, cb, wa, ba, wx, bx, lam):
    gate_br, rec = jnp.split(h @ w_in, 2, axis=-1)
    xc = _dwconv(rec, cw, cb, 1, 2).astype(jnp.float32)
    a_f, b_f = _lru_coeffs(xc, wa[0], ba[0], wx[0], bx[0], lam[0])
    h_f = _linear_scan(a_f, b_f, h0[:, 0].astype(jnp.float32), False)
    a_b, b_b = _lru_coeffs(xc, wa[1], ba[1], wx[1], bx[1], lam[1])
    h_b = _linear_scan(a_b, b_b, h0[:, 1].astype(jnp.float32), True)
    y = (h_f + h_b).astype(h.dtype) * jax.nn.gelu(gate_br)
    return y @ w_out, h_f, h_b


def setup_inputs(seed: int = 0) -> dict:
    key = jax.random.key(seed)
    ks = jax.random.split(key, 32)

    def nrm(k, shape, scale):
        return jax.random.normal(k, shape, jnp.float32) * scale

    u = jax.random.uniform(ks[28], (N_ODD, 2, LRU_WIDTH), jnp.float32, minval=0.9, maxval=0.999)
    a_init = u ** (1.0 / LRU_C)
    return {
        "x_prompt": nrm(ks[0], (BATCH, SEQ, D_MODEL), 1.0),
        "x_sample": nrm(ks[1], (DEC_BATCH, DEC_SEQ, D_MODEL), 1.0),
        "c": nrm(ks[2], (DEC_BATCH, D_MODEL), 1.0),
        "cache_k": nrm(ks[3], (DEC_BATCH, N_EVEN, PAST_LEN, N_KV_HEADS, HEAD_DIM), 1.0),
        "cache_v": nrm(ks[4], (DEC_BATCH, N_EVEN, PAST_LEN, N_KV_HEADS, HEAD_DIM), 1.0),
        "state_lru": nrm(ks[5], (DEC_BATCH, N_ODD, 2, LRU_WIDTH), 0.5),
        "c_ctx": nrm(ks[6], (D_MODEL,), 1.0),
        "w_mod": nrm(ks[7], (DEPTH, D_MODEL, 6 * D_MODEL), D_MODEL ** -0.5),
        "b_mod": nrm(ks[8], (DEPTH, 6 * D_MODEL), 0.02),
        "norm_mix": 1.0 + nrm(ks[9], (DEPTH, D_MODEL), 0.1),
        "norm_ffn": 1.0 + nrm(ks[10], (DEPTH, D_MODEL), 0.1),
        "w_ff1": nrm(ks[11], (DEPTH, D_MODEL, D_FF), D_MODEL ** -0.5),
        "w_ff2": nrm(ks[12], (DEPTH, D_FF, D_MODEL), D_FF ** -0.5),
        "att_in": nrm(ks[13], (N_EVEN, D_MODEL, IN_EVEN), D_MODEL ** -0.5),
        "att_out": nrm(ks[14], (N_EVEN, MIX_EVEN, D_MODEL), MIX_EVEN ** -0.5),
        "att_sink": nrm(ks[15], (N_EVEN, N_HEADS), 0.5),
        "conv_w": nrm(ks[16], (N_EVEN, CONV_K, CONV_CH), CONV_K ** -0.5),
        "conv_b": nrm(ks[17], (N_EVEN, CONV_CH), 0.02),
        "conv_norm_g": 1.0 + nrm(ks[18], (N_EVEN, CONV_CH), 0.1),
        "conv_norm_b": nrm(ks[19], (N_EVEN, CONV_CH), 0.02),
        "lru_in": nrm(ks[20], (N_ODD, D_MODEL, 2 * LRU_WIDTH), D_MODEL ** -0.5),
        "lru_out": nrm(ks[21], (N_ODD, LRU_WIDTH, D_MODEL), LRU_WIDTH ** -0.5),
        "lru_conv_w": nrm(ks[22], (N_ODD, LRU_CONV_K, LRU_WIDTH), LRU_CONV_K ** -0.5),
        "lru_conv_b": nrm(ks[23], (N_ODD, LRU_WIDTH), 0.02),
        "lru_wa": nrm(ks[24], (N_ODD, 2, LRU_BLOCKS, LRU_BW, LRU_BW), LRU_BW ** -0.5),
        "lru_ba": nrm(ks[25], (N_ODD, 2, LRU_WIDTH), 0.02),
        "lru_wx": nrm(ks[26], (N_ODD, 2, LRU_BLOCKS, LRU_BW, LRU_BW), LRU_BW ** -0.5),
        "lru_bx": nrm(ks[27], (N_ODD, 2, LRU_WIDTH), 0.02),
        "lru_lam": jnp.log(a_init) - jnp.log1p(-a_init),
        "final_norm": 1.0 + nrm(ks[29], (D_MODEL,), 0.1),
    }


def reference(x_prompt, x_sample, c, cache_k, cache_v, state_lru, c_ctx,
              w_mod, b_mod, norm_mix, norm_ffn, w_ff1, w_ff2,
              att_in, att_out, att_sink, conv_w, conv_b, conv_norm_g, conv_norm_b,
              lru_in, lru_out, lru_conv_w, lru_conv_b, lru_wa, lru_ba, lru_wx, lru_bx, lru_lam,
              final_norm):
    y_p = x_prompt
    y_s = x_sample
    cond_ctx = c_ctx[None, None, :]
    cond_lat = c[:, None, :]
    new_k, new_v, new_h = [], [], []
    for li in range(DEPTH):
        mp = _adaln(cond_ctx, w_mod[li], b_mod[li])
        ms = _adaln(cond_lat, w_mod[li], b_mod[li])
        hp = _modulate(y_p, norm_mix[li], mp[0], mp[1])
        hs = _modulate(y_s, norm_mix[li], ms[0], ms[1])
        if li % 2 == 0:
            e = li // 2
            ep = (att_in[e], att_out[e], att_sink[e], conv_w[e], conv_b[e], conv_norm_g[e], conv_norm_b[e])
            op, kp, vp = _even_ctx(hp, *ep)
            new_k.append(kp)
            new_v.append(vp)
            os_ = _even_lat(hs, cache_k[:, e], cache_v[:, e], *ep)
        else:
            o = li // 2
            opar = (lru_in[o], lru_out[o], lru_conv_w[o], lru_conv_b[o],
                    lru_wa[o], lru_ba[o], lru_wx[o], lru_bx[o], lru_lam[o])
            h0 = jnp.zeros((y_p.shape[0], 2, LRU_WIDTH), jnp.float32)
            op, hf_ctx, hb_ctx = _odd_core(hp, h0, *opar)
            new_h.append(jnp.stack([hf_ctx[:, -1], hb_ctx[:, 0]], axis=1).astype(y_p.dtype))
            os_, _, _ = _odd_core(hs, state_lru[:, o], *opar)
        y_p = y_p + mp[2] * op
        y_s = y_s + ms[2] * os_
        y_p = y_p + mp[5] * _sq_relu_mlp(_modulate(y_p, norm_ffn[li], mp[3], mp[4]), w_ff1[li], w_ff2[li])
        y_s = y_s + ms[5] * _sq_relu_mlp(_modulate(y_s, norm_ffn[li], ms[3], ms[4]), w_ff1[li], w_ff2[li])
    y_prompt = _rmsnorm(y_p, final_norm)
    y_sample = _rmsnorm(y_s, final_norm)
    new_cache_k = jnp.stack(new_k, axis=1)
    new_cache_v = jnp.stack(new_v, axis=1)
    new_state_lru = jnp.stack(new_h, axis=1)
    return (y_prompt, y_sample, new_cache_k, new_cache_v, new_state_lru)
```

```python
import numpy as np
from contextlib import ExitStack
import concourse.bass as bass
import concourse.mybir as mybir
from concourse.bass_utils import run_bass_kernel_spmd

F32 = mybir.dt.float32
BF16 = mybir.dt.bfloat16
AF = mybir.ActivationFunctionType
ALU = mybir.AluOpType

NT = 2560
NB = 5
EPS = 1e-6
STAGE = 99
import os as _os
SUB = _os.environ.get('MK_SUB', 'ABCDEK')
ATT = _os.environ.get('MK_ATT', 'PSQXMVN')
DEBUG = {}


class Tl:
    __slots__ = ("name", "ap", "lw", "rd", "sem", "cnt", "excl")

    def __init__(self, name, ap):
        self.excl = False
        self.name = name
        self.ap = ap
        self.lw = None
        self.rd = []
        self.sem = None
        self.cnt = 0


class Op:
    __slots__ = ("eng", "fn", "needs", "sig", "val", "dma", "dsem", "dval")

    def __init__(self, eng, fn):
        self.eng = eng
        self.fn = fn
        self.needs = []
        self.sig = False
        self.val = 0
        self.dma = False
        self.dsem = None
        self.dval = 0


class KB:
    ENGS = ("pe", "act", "dve", "pool", "sp")

    def __init__(self, nc, es):
        self.nc = nc
        self.es = es
        self.q = {e: [] for e in self.ENGS}
        self.esem = {e: es.enter_context(nc.semaphore("s_" + e)) for e in self.ENGS}
        self.dma_tiles = []
        self.nsem = 0
        self.psum_tiles = []
        self.psum_i = 0
        self.acc_i = 0
        self.nalloc = 0

    def at(self, name, shape, dt, off):
        self.nalloc += 1
        h = self.nc.alloc_sbuf_tensor_at("%s_%d" % (name, self.nalloc), list(shape), dt, offset=off)
        return h

    def init_psum(self):
        for i in range(8):
            h = self.es.enter_context(self.nc.psum_tensor("ps%d" % i, [128, 512], F32))
            self.psum_tiles.append(Tl("ps%d" % i, h[:, :]))
            self.psum_tiles[-1].excl = True

    def psum(self):
        t = self.psum_tiles[self.psum_i % 6]
        self.psum_i += 1
        return t

    def psum_acc(self):
        t = self.psum_tiles[6 + self.acc_i % 2]
        self.acc_i += 1
        return t

    def add(self, eng, fn, reads=(), writes=(), dma_tile=None):
        op = Op(eng, fn)
        deps = {}
        for t in reads:
            if t.lw is not None:
                deps[id(t.lw)] = (t.lw, "raw")
            if t.excl:
                for r in t.rd:
                    if r.eng != eng and id(r) not in deps:
                        deps[id(r)] = (r, "rar")
        for t in writes:
            if t.lw is not None and id(t.lw) not in deps:
                deps[id(t.lw)] = (t.lw, "waw")
            for r in t.rd:
                if id(r) not in deps:
                    deps[id(r)] = (r, "war")
        for p, kind in deps.values():
            if p.dma or p.eng != eng or eng != "pe":
                op.needs.append(p)
        if dma_tile is not None:
            op.dma = True
            if dma_tile.sem is None:
                dma_tile.sem = self.es.enter_context(self.nc.semaphore("d%d" % self.nsem))
                self.nsem += 1
                self.dma_tiles.append(dma_tile)
            dma_tile.cnt += 16
            op.dsem = dma_tile.sem
            op.dval = dma_tile.cnt
        for t in reads:
            t.rd.append(op)
        for t in writes:
            t.lw = op
            t.rd = []
        self.q[eng].append(op)
        return op

    def barrier(self):
        lasts = [self.q[e][-1] for e in self.ENGS if self.q[e] and self.q[e][-1].dma is not None]
        pend = [(t.sem, t.cnt) for t in self.dma_tiles]
        for e in self.ENGS:
            op = Op(e, None)
            op.needs = [p for p in lasts if (not p.dma) and p.eng != e]
            op.val = list(pend)
            op.dma = None
            self.q[e].append(op)

    def finalize(self):
        nc = self.nc
        for e in self.ENGS:
            for i, op in enumerate(self.q[e]):
                op.dval = op.dval if op.dma else i
        for e in self.ENGS:
            for op in self.q[e]:
                last = {}
                keep = []
                for p in op.needs:
                    if p.dma:
                        keep.append(p)
                    elif p.eng not in last or last[p.eng].dval < p.dval:
                        last[p.eng] = p
                for p in last.values():
                    p.sig = True
                    keep.append(p)
                op.needs = keep
        for e in self.ENGS:
            c = 0
            for op in self.q[e]:
                if op.dma is False and op.sig:
                    c += 1
                    op.val = c
        final = [(t.sem, t.cnt) for t in self.dma_tiles]
        esem = self.esem
        q = self.q

        def run(ename, eng):
            known = {}

            def wait(s, v):
                if known.get(id(s), 0) >= v:
                    return
                eng.wait_ge(s, v)
                known[id(s)] = v

            for op in q[ename]:
                waits = {}
                for p in op.needs:
                    if p.dma:
                        s, v = p.dsem, p.dval
                    else:
                        s, v = esem[p.eng], p.val
                    if id(s) not in waits or waits[id(s)][1] < v:
                        waits[id(s)] = (s, v)
                for s, v in waits.values():
                    wait(s, v)
                if op.dma is None:
                    for s, v in op.val:
                        wait(s, v)
                    continue
                ins = op.fn(eng)
                if op.dma:
                    ins.then_inc(op.dsem, 16)
                elif op.sig:
                    ins.then_inc(esem[ename], 1)
            if ename == "sp":
                for s, v in final:
                    wait(s, v)

        with nc.Block() as block:
            @block.tensor
            def _(eng):
                run("pe", eng)

            @block.scalar
            def _(eng):
                run("act", eng)

            @block.vector
            def _(eng):
                run("dve", eng)

            @block.gpsimd
            def _(eng):
                run("pool", eng)

            @block.sync
            def _(eng):
                run("sp", eng)

    def dma(self, eng, out_ap, in_ap, reads=(), writes=(), tile=None, **kw):
        return self.add(eng, lambda e: e.dma_start(out=out_ap, in_=in_ap, **kw),
                        reads=reads, writes=writes, dma_tile=tile)

    def mm(self, out_ap, lhsT, rhs, start, stop, reads=(), writes=()):
        return self.add("pe", lambda e: e.matmul(out_ap, lhsT, rhs, start=start, stop=stop),
                        reads=reads, writes=writes)

    def act(self, out_ap, in_ap, func, reads=(), writes=(), bias=None, scale=None):
        kw = {}
        if bias is not None:
            kw["bias"] = bias
        if scale is not None:
            kw["scale"] = scale
        return self.add("act", lambda e: e.activation(out_ap, in_ap, func, **kw),
                        reads=reads, writes=writes)

    def tt(self, out_ap, in0, in1, op, reads=(), writes=(), eng="dve"):
        return self.add(eng, lambda e: e.tensor_tensor(out_ap, in0, in1, op),
                        reads=reads, writes=writes)

    def ts(self, out_ap, in0, s1, s2, op0, op1=None, reads=(), writes=(), eng="dve"):
        if op1 is None:
            return self.add(eng, lambda e: e.tensor_scalar(out_ap, in0, s1, None, op0),
                            reads=reads, writes=writes)
        return self.add(eng, lambda e: e.tensor_scalar(out_ap, in0, s1, s2, op0, op1),
                        reads=reads, writes=writes)

    def stt(self, out_ap, in0, scalar, in1, op0, op1, reads=(), writes=(), eng="dve"):
        return self.add(eng, lambda e: e.scalar_tensor_tensor(out_ap, in0, scalar, in1, op0, op1),
                        reads=reads, writes=writes)

    def copy(self, out_ap, in_ap, reads=(), writes=(), eng="dve"):
        return self.add(eng, lambda e: e.tensor_copy(out_ap, in_ap), reads=reads, writes=writes)

    def memset(self, out_ap, val, writes=(), eng="dve"):
        return self.add(eng, lambda e: e.memset(out_ap, val), writes=writes)

    def recip(self, out_ap, in_ap, reads=(), writes=()):
        return self.add("dve", lambda e: e.reciprocal(out_ap, in_ap), reads=reads, writes=writes)


PV = {}
_o = 0
for _n, _w in [("b_mod", 96), ("norm_mix", 16), ("norm_ffn", 16), ("final", 8), ("conv_w", 124),
               ("conv_b", 4), ("cng", 4), ("cnb", 4), ("lcw", 32), ("lcb", 8), ("lba", 16),
               ("lbx", 16), ("lam", 16), ("st0", 16), ("sink", 8), ("cond", 16), ("eps", 1),
               ("one", 1)]:
    PV[_n] = (_o, _w)
    _o += _w
NPV = _o

BASE = 16640
TOP = 229376


def build_program(stage=99):
    nc = bass.Bass("TRN2", target_bir_lowering=False)

    def din(name, shape):
        return nc.dram_tensor(name, list(shape), F32, kind="ExternalInput").ap()

    def dout(name, shape):
        return nc.dram_tensor(name, list(shape), F32, kind="ExternalOutput").ap()

    xT = din("xT", [1024, NT])
    pv_d = din("pv", [128, NPV])
    w_mod = din("w_mod", [2, 1024, 6144])
    w_qkv = din("w_qkv", [1024, 1152])
    w_u = din("w_u", [1024, 1024])
    w_ao = din("w_ao", [1024, 1024])
    w_ff1 = din("w_ff1", [2, 1024, 4096])
    w_ff2 = din("w_ff2", [2, 4096, 1024])
    lru_in = din("lru_in", [1024, 2048])
    lru_out = din("lru_out", [1024, 1024])
    lru_g = din("lru_g", [128, 32, 128])
    ckT = din("ckT", [128, 4, 256])
    cV = din("cV", [256, 128])
    tabs = din("tabs", [128, 2, 2048])
    cmats = din("cmats", [128, 4, 128])
    yT = dout("yT", [1024, NT])
    ko = dout("ko", [128, 2, 512])
    vo = dout("vo", [512, 128])
    so = dout("so", [128, 32])
    dbg = {}

    with ExitStack() as es:
        k = KB(nc, es)
        k.init_psum()
        off = [BASE]

        def alloc(name, shape, dt, base=None):
            nb = int(np.prod(shape[1:])) * (2 if dt == BF16 else 4)
            nb = (nb + 63) // 64 * 64
            if base is None:
                o = off[0]
                off[0] += nb
                assert off[0] <= TOP, (name, off[0])
                return k.at(name, shape, dt, o)
            return k.at(name, shape, dt, base)

        Yh = alloc("Y", [128, 8, NT], F32)
        Yt = [[Tl("Y%d_%d" % (c, b), Yh[:, c, b * 512:(b + 1) * 512]) for b in range(NB)] for c in range(8)]
        pvh = alloc("pv", [128, NPV], F32)
        pvt = Tl("pv", pvh[:, :])
        modh = alloc("mod", [128, 2, 48, 2], F32)
        modt = Tl("mod", modh[:, :, :, :])
        ABh = alloc("AB", [128, 2, 2, 8, 2], F32)
        ABt = Tl("AB", ABh[:, :, :, :, :])
        cmh = alloc("cm", [128, 4, 128], BF16)
        cmt = Tl("cm", cmh[:, :, :])
        onesh = alloc("ones", [128, 2, 128], BF16)
        onest = Tl("ones", onesh[:, :, :])
        esh = alloc("es", [128, 8], F32)
        est = Tl("es", esh[:, :])
        scb_h = alloc("scb", [128, 8, 2], BF16)
        scbt = Tl("scb", scb_h[:, :, :])
        PH = off[0]

        def pvs(name, i=0, n=1):
            o, w = PV[name]
            return pvh[:, o + i:o + i + n]

        k.dma("sp", pvh[:, :], pv_d, writes=[pvt], tile=pvt)
        for c in range(8):
            for b in range(NB):
                k.dma("sp", Yt[c][b].ap, xT[c * 128:(c + 1) * 128, b * 512:(b + 1) * 512],
                      writes=[Yt[c][b]], tile=Yt[c][b])
        k.dma("pool", cmh[:, :, :], cmats, writes=[cmt], tile=cmt)
        k.memset(onesh[:, 0, :], 1.0 / 1024, writes=[onest])
        k.memset(onesh[:, 1, :], 1.0 / 512, writes=[onest])
        k.act(esh[:, :], pvs("sink", 0, 8), AF.Exp, reads=[pvt], writes=[est])
        o_c = PV["cond"][0]
        k.act(scb_h[:, :, :], pvh[:, o_c:o_c + 16].rearrange("p (c r) -> p c r", r=2), AF.Silu,
              reads=[pvt], writes=[scbt])

        off[0] = PH
        wm = [alloc("wm%d" % i, [128, 8, 1024], BF16) for i in range(2)]
        wmt = [Tl("wm%d" % i, wm[i][:, :, :]) for i in range(2)]
        def adaln_finish(l, ps):
            ob = PV["b_mod"][0] + l * 48
            k.tt(modh[:, l, :, :], ps.ap[:, 0:96].rearrange("p (f r) -> p f r", r=2),
                 pvh[:, ob:ob + 48].unsqueeze(2).to_broadcast([128, 48, 2]), ALU.add,
                 reads=[ps, pvt], writes=[modt])
            for m, (gname, sj) in enumerate([("norm_mix", 1), ("norm_ffn", 4)]):
                og = PV[gname][0] + l * 8
                k.stt(ABh[:, l, m, :, :], modh[:, l, sj * 8:(sj + 1) * 8, :], 1.0,
                      pvh[:, og:og + 8].unsqueeze(2).to_broadcast([128, 8, 2]), ALU.add, ALU.mult,
                      reads=[modt, pvt], writes=[ABt])

        def adaln_evac(l, j, ps, c0):
            ob = PV["b_mod"][0] + l * 48 + j * 8
            k.tt(modh[:, l, j * 8:(j + 1) * 8, :], ps.ap[:, c0:c0 + 16].rearrange("p (f r) -> p f r", r=2),
                 pvh[:, ob:ob + 8].unsqueeze(2).to_broadcast([128, 8, 2]), ALU.add,
                 reads=[ps, pvt], writes=[modt])

        def adaln_AB(l, m):
            gname, sj = [("norm_mix", 1), ("norm_ffn", 4)][m]
            og = PV[gname][0] + l * 8
            k.stt(ABh[:, l, m, :, :], modh[:, l, sj * 8:(sj + 1) * 8, :], 1.0,
                  pvh[:, og:og + 8].unsqueeze(2).to_broadcast([128, 8, 2]), ALU.add, ALU.mult,
                  reads=[modt, pvt], writes=[ABt])

        ps = k.psum()
        for j in range(2):
            k.dma("pool", wm[j][:, :, :],
                  w_mod[0, :, j * 1024:(j + 1) * 1024].rearrange("(c p) n -> p c n", p=128),
                  writes=[wmt[j]], tile=wmt[j])
            for fc in range(8):
                col = (j * 8 + fc) * 2
                for kc in range(8):
                    k.mm(ps.ap[:, col:col + 2], wm[j][:, kc, fc * 128:(fc + 1) * 128],
                         scb_h[:, kc, :], kc == 0, kc == 7, reads=[wmt[j], scbt], writes=[ps])
        for j in range(2):
            adaln_evac(0, j, ps, j * 16)
        adaln_AB(0, 0)

        def modp(l, j, c, r):
            return modh[:, l, j * 8 + c, r:r + 1]

        def Ap(l, m, c, r):
            return ABh[:, l, m, c, r:r + 1]

        k.barrier()

        def rms_stats(b, rs_t, sq_h, sq_t):
            ps = k.psum()
            for c in range(8):
                k.act(sq_h[:, c % 2, :], Yt[c][b].ap, AF.Square, reads=[Yt[c][b]], writes=[sq_t[c % 2]])
                k.mm(ps.ap, onesh[:, 0, :], sq_h[:, c % 2, :], c == 0, c == 7,
                     reads=[onest, sq_t[c % 2]], writes=[ps])
            k.act(rs_t.ap, ps.ap, AF.Ln, reads=[ps, pvt], writes=[rs_t], bias=pvs("eps"), scale=1.0)
            k.act(rs_t.ap, rs_t.ap, AF.Exp, reads=[rs_t], writes=[rs_t], scale=-0.5)

        def modulate(b, l, m, rs_t, tmp_h, tmp_t, hb_h, hb_t):
            r = 0 if b == 0 else 1
            jb = 0 if m == 0 else 3
            for c in range(8):
                i = c % 2
                k.tt(tmp_h[:, i, :], Yt[c][b].ap, rs_t.ap, ALU.mult, reads=[Yt[c][b], rs_t], writes=[tmp_t[i]])
                k.act(hb_h[:, c, :], tmp_h[:, i, :], AF.Identity, reads=[tmp_t[i], ABt, modt], writes=[hb_t],
                      scale=Ap(l, m, c, r), bias=modp(l, jb, c, r))

        def norm_temps():
            sq_h = alloc("sq", [128, 2, 512], BF16)
            sq_t = [Tl("sq%d" % i, sq_h[:, i, :]) for i in range(2)]
            rs_h = alloc("rs", [128, 512], F32)
            rs_t = Tl("rs", rs_h[:, :])
            tmp_h = alloc("ntmp", [128, 2, 512], F32)
            tmp_t = [Tl("nt%d" % i, tmp_h[:, i, :]) for i in range(2)]
            return sq_h, sq_t, rs_t, tmp_h, tmp_t

        def ffn(l):
            off[0] = PH
            h2h = alloc("h2", [128, 8, NT], BF16)
            h2t = [Tl("h2_%d" % b, h2h[:, :, b * 512:(b + 1) * 512]) for b in range(NB)]
            hidh = alloc("hid", [128, 4, NT], BF16)
            hidt = [Tl("hid_%d" % b, hidh[:, :, b * 512:(b + 1) * 512]) for b in range(NB)]
            w1 = [alloc("w1_%d" % i, [128, 8, 512], BF16) for i in range(2)]
            w1t = [Tl("w1_%d" % i, w1[i][:, :, :]) for i in range(2)]
            w2 = [alloc("w2_%d" % i, [128, 4, 1024], BF16) for i in range(2)]
            w2t = [Tl("w2_%d" % i, w2[i][:, :, :]) for i in range(2)]
            rl_h = alloc("rl", [128, 2, 512], BF16)
            rl_t = [Tl("rl%d" % i, rl_h[:, i, :]) for i in range(2)]
            sq_h, sq_t, rs_t, tmp_h, tmp_t = norm_temps()
            if l == 0:
                wm1 = alloc("wm1x", [128, 8, 1024], BF16)
                wm1t = Tl("wm1x", wm1[:, :, :])
                pa = k.psum_tiles[6]
            def _normf(b):
                rms_stats(b, rs_t, sq_h, sq_t)
                modulate(b, l, 1, rs_t, tmp_h, tmp_t, h2h[:, :, b * 512:(b + 1) * 512], h2t[b])
            _normf(0)
            n = 0
            for j in range(8):
                i = j % 2
                k.dma("pool", w1[i][:, :, :],
                      w_ff1[l, :, j * 512:(j + 1) * 512].rearrange("(c p) n -> p c n", p=128),
                      writes=[w1t[i]], tile=w1t[i])
                k.dma("pool", w2[i][:, :, :],
                      w_ff2[l, j * 512:(j + 1) * 512, :].rearrange("(c p) n -> p c n", p=128),
                      writes=[w2t[i]], tile=w2t[i])
                if l == 0 and j < 6:
                    k.dma("pool", wm1[:, :, :],
                          w_mod[1, :, j * 1024:(j + 1) * 1024].rearrange("(c p) n -> p c n", p=128),
                          writes=[wm1t], tile=wm1t)
                for b in range(NB):
                    if j == 0 and b + 1 < NB:
                        _normf(b + 1)
                    for hc in range(4):
                        ps = k.psum()
                        for kc in range(8):
                            k.mm(ps.ap, w1[i][:, kc, hc * 128:(hc + 1) * 128], h2h[:, kc, b * 512:(b + 1) * 512],
                                 kc == 0, kc == 7, reads=[w1t[i], h2t[b]], writes=[ps])
                        ri = n % 2
                        n += 1
                        k.act(rl_h[:, ri, :], ps.ap, AF.Relu, reads=[ps], writes=[rl_t[ri]])
                        k.tt(hidh[:, hc, b * 512:(b + 1) * 512], rl_h[:, ri, :], rl_h[:, ri, :], ALU.mult,
                             reads=[rl_t[ri]], writes=[hidt[b]])
                for b in range(NB):
                    r = 0 if b == 0 else 1
                    for oc in range(8):
                        ps = k.psum()
                        for kc in range(4):
                            k.mm(ps.ap, w2[i][:, kc, oc * 128:(oc + 1) * 128], hidh[:, kc, b * 512:(b + 1) * 512],
                                 kc == 0, kc == 3, reads=[w2t[i], hidt[b]], writes=[ps])
                        k.stt(Yt[oc][b].ap, ps.ap, modp(l, 5, oc, r), Yt[oc][b].ap, ALU.mult, ALU.add,
                              reads=[ps, modt, Yt[oc][b]], writes=[Yt[oc][b]])
                if l == 0 and j < 6:
                    for fc in range(8):
                        col = (j * 8 + fc) * 2
                        for kc in range(8):
                            k.mm(pa.ap[:, col:col + 2], wm1[:, kc, fc * 128:(fc + 1) * 128],
                                 scb_h[:, kc, :], kc == 0, kc == 7, reads=[wm1t, scbt], writes=[pa])
                    if j == 5:
                        adaln_finish(1, pa)
            k.barrier()

        off[0] = PH
        GW = 286 + 286 + 2078
        OFF_QKV = off[0]
        qTh = alloc("qT", [128, 4, NT], BF16)
        qTt = [Tl("qT%d" % b, qTh[:, :, b * 512:(b + 1) * 512]) for b in range(NB)]
        kTh = alloc("kT", [128, 4, NT], BF16)
        kTt = [Tl("kT%d" % b, kTh[:, :, b * 512:(b + 1) * 512]) for b in range(NB)]
        Vh = alloc("Vaug", [128, 20, 2, 192], BF16)
        Vt = [Tl("V%d" % b, Vh[:, b * 4:(b + 1) * 4, :, :]) for b in range(NB)]
        OFF_ATT = off[0]
        attTh = alloc("attT", [128, 4, NT], BF16)
        attt = [Tl("att%d" % b, attTh[:, :, b * 512:(b + 1) * 512]) for b in range(NB)]
        gluh = alloc("glu", [128, 4, GW], BF16)
        OFF_T = off[0]
        off[0] = OFF_QKV
        cvTh = alloc("cvT", [128, 4, NT], BF16)
        cvt = [Tl("cv%d" % b, cvTh[:, :, b * 512:(b + 1) * 512]) for b in range(NB)]
        OFF_CV = off[0]

        def gcol(t):
            if t < 256:
                return 15 + t
            if t < 512:
                return 286 + 15 + (t - 256)
            return 572 + 15 + (t - 512)
        glut = [Tl("glu%d" % b, gluh[:, :, gcol(b * 512):gcol(b * 512) + 512]) for b in range(NB)]
        glupad = Tl("glupad", gluh[:, :, :])
        off[0] = OFF_ATT
        k.memset(Vh[:, :, :, 64:128], 1.0, writes=Vt)

        wq = alloc("wqkv", [128, 8, 1152], BF16)
        wqt = Tl("wqkv", wq[:, :, :])
        k.dma("pool", wq[:, :, :], w_qkv.rearrange("(c p) n -> p c n", p=128), writes=[wqt], tile=wqt)
        tabh = alloc("tabs", [128, 2, 2048], BF16)
        tabt = Tl("tabs", tabh[:, :, :])
        k.dma("pool", tabh[:, :, :], tabs, writes=[tabt], tile=tabt)
        hbs = [alloc("hb%d" % i, [128, 8, 512], BF16) for i in range(2)]
        hbts = [Tl("hb%d" % i, hbs[i][:, :, :]) for i in range(2)]
        sq_h, sq_t, rs_t, tmp_h, tmp_t = norm_temps()
        xbh = alloc("xb", [128, 512], BF16)
        xbt = Tl("xb", xbh[:, :])
        t1h = alloc("t1", [128, 2, 512], F32)
        t1t = [Tl("t1_%d" % i, t1h[:, i, :]) for i in range(2)]
        koh = alloc("koS", [128, 2, 512], F32)
        kot = Tl("koS", koh[:, :, :])
        voh = alloc("voS", [128, 4, 128], F32)
        vot = Tl("voS", voh[:, :, :])
        def _norm1(b):
            rms_stats(b, rs_t, sq_h, sq_t)
            modulate(b, 0, 0, rs_t, tmp_h, tmp_t, hbs[b % 2], hbts[b % 2])
        nb1 = NB if stage >= 1 else 0
        if nb1:
            _norm1(0)
        for b in range(nb1):
            hbh, hbt = hbs[b % 2], hbts[b % 2]
            if b + 1 < nb1:
                _norm1(b + 1)
            cols = slice(b * 512, (b + 1) * 512)
            for oc in range(8 if 'B' in SUB else 0):
                if b > 0 and 'C' not in SUB:
                    continue
                ps = k.psum()
                for kc in range(8):
                    k.mm(ps.ap, wq[:, kc, oc * 128:(oc + 1) * 128], hbh[:, kc, :], kc == 0, kc == 7,
                         reads=[wqt, hbt], writes=[ps])
                if oc < 4:
                    dst_ap, dst_t = qTh[:, oc, cols], qTt[b]
                else:
                    dst_ap, dst_t = kTh[:, oc - 4, cols], kTt[b]
                if b == 0:
                    if 'E' in SUB:
                        k.act(dst_ap, ps.ap, AF.Copy, reads=[ps], writes=[dst_t])
                    if oc in (4, 6) and 'K' in SUB:
                        k.copy(koh[:, (oc - 4) // 2, :], ps.ap, reads=[ps], writes=[kot])
                else:
                    tc_ = slice((b - 1) * 512, b * 512)
                    k.act(xbh[:, :], ps.ap, AF.Copy, reads=[ps], writes=[xbt])
                    ps2 = k.psum()
                    k.mm(ps2.ap, cmh[:, 0, :], xbh[:, :], True, True, reads=[cmt, xbt], writes=[ps2])
                    k.tt(t1h[:, 0, :], ps.ap, tabh[:, 0, tc_], ALU.mult, reads=[ps, tabt], writes=[t1t[0]])
                    k.tt(t1h[:, 1, :], ps2.ap, tabh[:, 1, tc_], ALU.mult, reads=[ps2, tabt], writes=[t1t[1]])
                    k.tt(dst_ap, t1h[:, 0, :], t1h[:, 1, :], ALU.add, reads=[t1t[0], t1t[1]], writes=[dst_t])
            for tt_ in range(4 if 'D' in SUB else 0):
                ps = k.psum()
                for kc in range(8):
                    k.mm(ps.ap[:, 0:128], hbh[:, kc, tt_ * 128:(tt_ + 1) * 128], wq[:, kc, 1024:1152],
                         kc == 0, kc == 7, reads=[wqt, hbt], writes=[ps])
                k.act(Vh[:, b * 4 + tt_, :, 0:64], ps.ap[:, 0:128].rearrange("p (a d) -> p a d", a=2), AF.Copy,
                      reads=[ps], writes=[Vt[b]])
                k.act(Vh[:, b * 4 + tt_, :, 128:192], ps.ap[:, 0:128].rearrange("p (a d) -> p a d", a=2), AF.Copy,
                      reads=[ps], writes=[Vt[b]])
                if b == 0:
                    k.copy(voh[:, tt_, :], ps.ap[:, 0:128], reads=[ps], writes=[vot])
            if b == 0 and 'D' in SUB:
                k.dma("sp", ko, koh[:, :, :], reads=[kot], tile=kot)
                k.dma("sp", vo.rearrange("(t p) d -> p t d", p=128), voh[:, :, :], reads=[vot], tile=vot)
        k.barrier()
        if stage >= 2:
            k.memset(gluh[:, :, :], 0.0, writes=[glupad] + glut)
            off[0] = OFF_ATT
            wu = alloc("wu", [128, 8, 1024], BF16)
            wut = Tl("wu", wu[:, :, :])
            k.dma("pool", wu[:, :, :], w_u.rearrange("(c p) n -> p c n", p=128), writes=[wut], tile=wut)
            sgh = alloc("sg", [128, 2, 512], F32)
            sgt = [Tl("sg%d" % i, sgh[:, i, :]) for i in range(2)]
            assert off[0] <= OFF_ATT + 20480, off[0]
            off[0] = OFF_T
            hbs = [alloc("hb%d" % i, [128, 8, 512], BF16) for i in range(2)]
            hbts = [Tl("hb%d" % i, hbs[i][:, :, :]) for i in range(2)]
            sq_h, sq_t, rs_t, tmp_h, tmp_t = norm_temps()
            def _norm2(b):
                rms_stats(b, rs_t, sq_h, sq_t)
                modulate(b, 0, 0, rs_t, tmp_h, tmp_t, hbs[b % 2], hbts[b % 2])
            _norm2(0)
            for b in range(NB):
                hbh, hbt = hbs[b % 2], hbts[b % 2]
                if b + 1 < NB:
                    _norm2(b + 1)
                for c in range(4):
                    psa = k.psum()
                    psg = k.psum()
                    for kc in range(8):
                        k.mm(psa.ap, wu[:, kc, c * 128:(c + 1) * 128], hbh[:, kc, :], kc == 0, kc == 7,
                             reads=[wut, hbt], writes=[psa])
                    for kc in range(8):
                        k.mm(psg.ap, wu[:, kc, 512 + c * 128:512 + (c + 1) * 128], hbh[:, kc, :], kc == 0, kc == 7,
                             reads=[wut, hbt], writes=[psg])
                    k.act(sgh[:, c % 2, :], psg.ap, AF.Sigmoid, reads=[psg], writes=[sgt[c % 2]])
                    if b == 0:
                        for s_ in range(2):
                            g0 = gcol(s_ * 256)
                            k.tt(gluh[:, c, g0:g0 + 256], psa.ap[:, s_ * 256:(s_ + 1) * 256],
                                 sgh[:, c % 2, s_ * 256:(s_ + 1) * 256], ALU.mult,
                                 reads=[psa, sgt[c % 2]], writes=[glut[b]])
                    else:
                        g0 = gcol(b * 512)
                        k.tt(gluh[:, c, g0:g0 + 512], psa.ap, sgh[:, c % 2, :], ALU.mult,
                             reads=[psa, sgt[c % 2]], writes=[glut[b]])
            k.barrier()
        if stage >= 3:
            off[0] = OFF_T
            ckh = alloc("ckT", [128, 4, 256], BF16)
            ckt = Tl("ckT", ckh[:, :, :])
            k.dma("pool", ckh[:, :, :], ckT, writes=[ckt], tile=ckt)
            cVh = alloc("cV", [128, 2, 2, 192], BF16)
            cVt = Tl("cV", cVh[:, :, :, :])
            k.memset(cVh[:, :, :, 64:128], 1.0, writes=[cVt])
            for t_ in range(2):
                for a_ in range(2):
                    for c0 in (0, 128):
                        k.dma("pool", cVh[:, t_, a_, c0:c0 + 64], cV[t_ * 128:(t_ + 1) * 128, a_ * 64:(a_ + 1) * 64],
                              writes=[cVt], tile=cVt)
            pth = alloc("pT", [128, 3, 512], BF16)
            ptt = [Tl("pT%d" % i, pth[:, i, :]) for i in range(3)]
            dnh = alloc("dn", [128, 2, 512], F32)
            dnt = [Tl("dn%d" % i, dnh[:, i, :]) for i in range(2)]
            k.memset(dnh[:, :, :], 1.0, writes=dnt)
            wmx = alloc("wm0x", [128, 8, 1024], BF16)
            wmxt = Tl("wm0x", wmx[:, :, :])

            def adaln0_dma(j):
                k.dma("pool", wmx[:, :, :],
                      w_mod[0, :, j * 1024:(j + 1) * 1024].rearrange("(c p) n -> p c n", p=128),
                      writes=[wmxt], tile=wmxt)

            def adaln0_compute(j):
                pa_ = k.psum()
                for fc in range(8):
                    for kc in range(8):
                        k.mm(pa_.ap[:, fc * 2:fc * 2 + 2], wmx[:, kc, fc * 128:(fc + 1) * 128],
                             scb_h[:, kc, :], kc == 0, kc == 7, reads=[wmxt, scbt], writes=[pa_])
                adaln_evac(0, j, pa_, 0)
                if j == 4:
                    adaln_AB(0, 1)
                if j < 5:
                    adaln0_dma(j + 1)
            adaln0_dma(2)
            pi = [0]
            units = []

            def attend(qcols, g, keylist, b_out):
                units.append(dict(qcols=qcols, g=g, kl=keylist, b_out=b_out))

            def S_stage(u, ki):
                qcols, g = u["qcols"], u["g"]
                kind, idx, mk = u["kl"][ki]
                ps = k.psum()
                for hl in range(4):
                    h = g * 4 + hl
                    if kind == "tok":
                        lhs = kTh[:, g * 2 + h % 2, idx * 128:(idx + 1) * 128]
                        rd = [kTt[idx // 4]]
                    else:
                        lhs = ckh[:, g * 2 + h % 2, idx * 128:(idx + 1) * 128]
                        rd = [ckt]
                    k.mm(ps.ap[:, hl * 128:(hl + 1) * 128], lhs, qTh[:, h // 2, qcols], True, True,
                         reads=rd + [qTt[qcols.start // 512]], writes=[ps])
                return ps

            def XP_stage(u, ki, ps):
                g = u["g"]
                kind, idx, mk = u["kl"][ki]
                nk = len(u["kl"])
                po = u["po"]
                i = pi[0] % 3
                pi[0] += 1
                k.act(pth[:, i, :], ps.ap, AF.Exp, reads=[ps], writes=[ptt[i]], scale=0.125)
                if mk is not None:
                    k.tt(pth[:, i, :].rearrange("p (h q) -> p h q", h=4),
                         pth[:, i, :].rearrange("p (h q) -> p h q", h=4),
                         cmh[:, mk, :].unsqueeze(1).to_broadcast([128, 4, 128]), ALU.mult,
                         reads=[ptt[i], cmt], writes=[ptt[i]])
                for hl in range(4):
                    vo_ = 0 if hl % 2 == 0 else 64
                    if kind == "tok":
                        lhs = Vh[:, idx, g, vo_:vo_ + 128]
                        rd = [Vt[idx // 4]]
                    else:
                        lhs = cVh[:, idx, g, vo_:vo_ + 128]
                        rd = [cVt]
                    k.mm(po.ap[:, hl * 128:(hl + 1) * 128], lhs, pth[:, i, hl * 128:(hl + 1) * 128],
                         ki == 0 and hl == 0, ki == nk - 1 and hl == 3, reads=rd + [ptt[i]], writes=[po])

            def A1(u):
                u["po"] = k.psum_acc()
                nk = len(u["kl"])
                pss = [S_stage(u, 0)]
                if nk > 1:
                    pss.append(S_stage(u, 1))
                for ki in range(nk):
                    if ki + 2 < nk:
                        pss.append(S_stage(u, ki + 2))
                    XP_stage(u, ki, pss[ki])

            def A2(u, ui):
                qcols, g, b_out, po = u["qcols"], u["g"], u["b_out"], u["po"]
                di = ui % 2
                for hl in range(4):
                    h = g * 4 + hl
                    ro = (h % 2) * 64
                    rd_ = 64 - ro
                    cs = slice(hl * 128, (hl + 1) * 128)
                    k.ts(dnh[ro:ro + 64, di, cs], po.ap[rd_:rd_ + 64, cs], esh[rd_:rd_ + 64, h:h + 1], None, ALU.add,
                         reads=[po, est], writes=[dnt[di]])
                k.act(dnh[:, di, :], dnh[:, di, :], AF.Ln, reads=[dnt[di]], writes=[dnt[di]])
                k.act(dnh[:, di, :], dnh[:, di, :], AF.Exp, reads=[dnt[di]], writes=[dnt[di]], scale=-1.0)
                for hl in range(4):
                    h = g * 4 + hl
                    ro = (h % 2) * 64
                    cs = slice(hl * 128, (hl + 1) * 128)
                    k.tt(attTh[ro:ro + 64, h // 2, qcols], po.ap[ro:ro + 64, cs], dnh[ro:ro + 64, di, cs], ALU.mult,
                         reads=[po, dnt[di]], writes=[attt[b_out]])

            for s_ in range(2):
                for qb in range(2):
                    t0 = s_ * 256 + qb * 128
                    for g in range(2):
                        attend(slice(t0, t0 + 128), g, [("tok", s_ * 2, None), ("tok", s_ * 2 + 1, None)], 0)
            for qb in range(16):
                t0 = 512 + qb * 128
                kl = []
                if qb > 0:
                    kl.append(("tok", 4 + qb - 1, 2))
                kl.append(("tok", 4 + qb, None))
                if qb < 15:
                    kl.append(("tok", 4 + qb + 1, 3))
                kl += [("ctx", 0, None), ("ctx", 1, None)]
                for g in range(2):
                    attend(slice(t0, t0 + 128), g, kl, 1 + qb // 4)
            A1(units[0])
            for ui in range(len(units)):
                if ui + 1 < len(units):
                    A1(units[ui + 1])
                A2(units[ui], ui)
                if ui in (7, 15, 23, 31):
                    adaln0_compute(2 + (ui - 7) // 8)
            k.barrier()
        if stage >= 4:
            off[0] = OFF_CV
            dgh = alloc("diag", [128, 4, 31, 128], BF16)
            assert off[0] <= OFF_ATT, off[0]
            off[0] = OFF_T
            dgt = Tl("diag", dgh[:, :, :, :])
            o_w = PV["conv_w"][0]
            for c in range(4):
                for kk in range(31):
                    k.ts(dgh[:, c, kk, :], cmh[:, 1, :], pvh[:, o_w + c * 31 + kk:o_w + c * 31 + kk + 1], None, ALU.mult,
                         reads=[cmt, pvt], writes=[dgt])
            zs = [alloc("z%d" % i, [128, 4, 512], F32) for i in range(2)]
            zts = [Tl("z%d" % i, zs[i][:, :, :]) for i in range(2)]
            zbh = alloc("zb", [128, 4, 512], BF16)
            zbt = Tl("zb", zbh[:, :, :])
            zqh = alloc("zq", [128, 4, 512], BF16)
            zqt = Tl("zq", zqh[:, :, :])
            assert off[0] <= TOP
            off[0] = OFF_CV + 31744
            muh = alloc("mu", [128, 512], F32)
            mut = Tl("mu", muh[:, :])
            vrh = alloc("vr", [128, 512], F32)
            vrt = Tl("vr", vrh[:, :])
            assert off[0] <= OFF_ATT, off[0]
            ob_ = PV["conv_b"][0]
            og_ = PV["cng"][0]
            obb = PV["cnb"][0]

            def convmm(b):
                zh, zt = zs[b % 2], zts[b % 2]
                subs = [(0, 256), (256, 256)] if b == 0 else [(0, 512)]
                for c in range(4):
                    ps = k.psum()
                    for (s0, n) in subs:
                        g0 = gcol(b * 512 + s0) - 15
                        for kk in range(31):
                            k.mm(ps.ap[:, s0:s0 + n], dgh[:, c, kk, :], gluh[:, c, g0 + kk:g0 + kk + n],
                                 kk == 0, kk == 30, reads=[dgt, glupad] + glut, writes=[ps])
                    k.act(zh[:, c, :], ps.ap, AF.Identity, reads=[ps, pvt], writes=[zt],
                          bias=pvh[:, ob_ + c:ob_ + c + 1], scale=1.0)

            def convln(b):
                zh, zt = zs[b % 2], zts[b % 2]
                k.copy(zbh[:, :, :], zh[:, :, :], reads=[zt], writes=[zbt])
                k.tt(zqh[:, :, :], zh[:, :, :], zh[:, :, :], ALU.mult, reads=[zt], writes=[zqt])
                pm = k.psum()
                pq = k.psum()
                for c in range(4):
                    k.mm(pm.ap, onesh[:, 1, :], zbh[:, c, :], c == 0, c == 3, reads=[onest, zbt], writes=[pm])
                for c in range(4):
                    k.mm(pq.ap, onesh[:, 1, :], zqh[:, c, :], c == 0, c == 3, reads=[onest, zqt], writes=[pq])
                k.copy(muh[:, :], pm.ap, reads=[pm], writes=[mut])
                k.tt(vrh[:, :], muh[:, :], muh[:, :], ALU.mult, reads=[mut], writes=[vrt])
                k.tt(vrh[:, :], pq.ap, vrh[:, :], ALU.subtract, reads=[pq, vrt], writes=[vrt])
                k.act(vrh[:, :], vrh[:, :], AF.Ln, reads=[vrt, pvt], writes=[vrt], bias=pvs("eps"), scale=1.0)
                k.act(vrh[:, :], vrh[:, :], AF.Exp, reads=[vrt], writes=[vrt], scale=-0.5)
                for c in range(4):
                    k.tt(zh[:, c, :], zh[:, c, :], muh[:, :], ALU.subtract, reads=[zt, mut], writes=[zt])
                    k.tt(zh[:, c, :], zh[:, c, :], vrh[:, :], ALU.mult, reads=[zt, vrt], writes=[zt])
                    k.act(cvTh[:, c, b * 512:(b + 1) * 512], zh[:, c, :], AF.Silu, reads=[zt, pvt], writes=[cvt[b]],
                          scale=pvh[:, og_ + c:og_ + c + 1], bias=pvh[:, obb + c:obb + c + 1])

            convmm(0)
            for b in range(NB):
                if b + 1 < NB:
                    convmm(b + 1)
                convln(b)
            k.barrier()
        if stage >= 5:
            off[0] = OFF_T
            wo = alloc("wo", [128, 8, 1024], BF16)
            wot = Tl("wo", wo[:, :, :])
            k.dma("pool", wo[:, :, :], w_ao.rearrange("(c p) n -> p c n", p=128), writes=[wot], tile=wot)
            for b in range(NB):
                r = 0 if b == 0 else 1
                cols = slice(b * 512, (b + 1) * 512)
                for oc in range(8):
                    ps = k.psum()
                    for kc in range(8):
                        rhs = attTh[:, kc, cols] if kc < 4 else cvTh[:, kc - 4, cols]
                        k.mm(ps.ap, wo[:, kc, oc * 128:(oc + 1) * 128], rhs, kc == 0, kc == 7,
                             reads=[wot, attt[b], cvt[b]], writes=[ps])
                    k.stt(Yt[oc][b].ap, ps.ap, modp(0, 2, oc, r), Yt[oc][b].ap, ALU.mult, ALU.add,
                          reads=[ps, modt, Yt[oc][b]], writes=[Yt[oc][b]])
            k.barrier()
        if stage >= 6:
            ffn(0)

        if stage >= 7:
            off[0] = PH
            RW = 259 + 259 + 2051
            rseg = [(1, 256, 0), (260, 256, 256), (519, 2048, 512)]
            rech = alloc("rec", [128, 8, RW], BF16)
            rect = Tl("rec", rech[:, :, :])
            ggh = alloc("gg", [128, 8, NT], BF16)
            ggt = [Tl("gg%d" % c, ggh[:, c, :]) for c in range(8)]
            PH3 = off[0]
            k.memset(rech[:, :, :], 0.0, writes=[rect])

            def rcol(t):
                if t < 256:
                    return 1 + t
                if t < 512:
                    return 260 + (t - 256)
                return 519 + (t - 512)
            for half in range(2):
                off[0] = PH3
                wl = alloc("wl", [128, 8, 1024], BF16)
                wlt = Tl("wl", wl[:, :, :])
                k.dma("pool", wl[:, :, :], lru_in[:, half * 1024:(half + 1) * 1024].rearrange("(c p) n -> p c n", p=128),
                      writes=[wlt], tile=wlt)
                hbs = [alloc("hb%d" % i, [128, 8, 512], BF16) for i in range(2)]
                hbts = [Tl("hb%d" % i, hbs[i][:, :, :]) for i in range(2)]
                sq_h, sq_t, rs_t, tmp_h, tmp_t = norm_temps()
                def _norm3(b, hbs=hbs, hbts=hbts, rs_t=rs_t, sq_h=sq_h, sq_t=sq_t, tmp_h=tmp_h, tmp_t=tmp_t):
                    rms_stats(b, rs_t, sq_h, sq_t)
                    modulate(b, 1, 0, rs_t, tmp_h, tmp_t, hbs[b % 2], hbts[b % 2])
                _norm3(0)
                for b in range(NB):
                    hbh, hbt = hbs[b % 2], hbts[b % 2]
                    if b + 1 < NB:
                        _norm3(b + 1)
                    for oc in range(8):
                        ps = k.psum()
                        for kc in range(8):
                            k.mm(ps.ap, wl[:, kc, oc * 128:(oc + 1) * 128], hbh[:, kc, :], kc == 0, kc == 7,
                                 reads=[wlt, hbt], writes=[ps])
                        if half == 0:
                            k.act(ggh[:, oc, b * 512:(b + 1) * 512], ps.ap, AF.Gelu_apprx_tanh, reads=[ps], writes=[ggt[oc]])
                        elif b == 0:
                            for s_ in range(2):
                                r0 = rcol(s_ * 256)
                                k.act(rech[:, oc, r0:r0 + 256], ps.ap[:, s_ * 256:(s_ + 1) * 256], AF.Copy,
                                      reads=[ps], writes=[rect])
                        else:
                            r0 = rcol(b * 512)
                            k.act(rech[:, oc, r0:r0 + 512], ps.ap, AF.Copy, reads=[ps], writes=[rect])
                k.barrier()
            off[0] = PH3
            wgh = alloc("wg", [128, 32, 128], BF16)
            wgt = Tl("wg", wgh[:, :, :])
            k.dma("pool", wgh[:, :, :], lru_g, writes=[wgt], tile=wgt)
            sch = alloc("sc", [128, 2, 16], F32)
            sct = Tl("sc", sch[:, :, :])
            o_l = PV["lam"][0]
            k.act(sch[:, 0, :], pvh[:, o_l:o_l + 16], AF.Exp, reads=[pvt], writes=[sct], scale=-1.0)
            k.act(sch[:, 0, :], sch[:, 0, :], AF.Ln, reads=[sct, pvt], writes=[sct], bias=pvs("one"), scale=1.0)
            k.ts(sch[:, 1, :], sch[:, 0, :], -4.0, None, ALU.mult, reads=[sct], writes=[sct])
            k.ts(sch[:, 0, :], sch[:, 0, :], -8.0, None, ALU.mult, reads=[sct], writes=[sct])
            hbh_ = alloc("hbias", [128, 48], F32)
            hbit = Tl("hbias", hbh_[:, :])
            k.ts(hbh_[:, 0:16], pvh[:, PV["lba"][0]:PV["lba"][0] + 16], 0.5, None, ALU.mult, reads=[pvt], writes=[hbit])
            k.ts(hbh_[:, 16:32], pvh[:, PV["lbx"][0]:PV["lbx"][0] + 16], 0.5, None, ALU.mult, reads=[pvt], writes=[hbit])
            k.ts(hbh_[:, 32:40], pvh[:, PV["lcb"][0]:PV["lcb"][0] + 8], 0.5, None, ALU.mult, reads=[pvt], writes=[hbit])
            k.memset(hbh_[:, 40:48], 0.5, writes=[hbit])
            dgl = alloc("dgl", [128, 4, 128], BF16)
            dglt = Tl("dgl", dgl[:, :, :])
            SW = 512
            XS = []
            for i in range(3):
                d_ = {}
                for nm, dt_ in (("xc", F32), ("xcb", BF16)):
                    h_ = alloc("%s%d" % (nm, i), [128, SW], dt_)
                    d_[nm] = h_
                    d_[nm + "_t"] = Tl("%s%d" % (nm, i), h_[:, :])
                XS.append(d_)
            sets = []
            for i in range(2):
                d_ = {}
                for nm, dt_ in (("A", F32), ("I", F32), ("T", F32), ("hbk", F32)):
                    h_ = alloc("%s%d" % (nm, i), [128, SW], dt_)
                    d_[nm] = h_
                    d_[nm + "_t"] = Tl("%s%d" % (nm, i), h_[:, :])
                sets.append(d_)
            Hfh = alloc("Hf", [128, 2048], F32)
            Hft = Tl("Hf", Hfh[:, :])
            carh = alloc("car", [128, 2], F32)
            cart = Tl("car", carh[:, :])
            soh = alloc("so", [128, 8, 2, 2], F32)
            sot = Tl("so", soh[:, :, :, :])
            o_cw = PV["lcw"][0]
            o_cb = PV["lcb"][0]
            o_s0 = PV["st0"][0]

            def rev(ap_h, col_last, n):
                a_ = ap_h[:, col_last:col_last + 1]
                return bass.AP(a_.tensor, a_.offset, [list(a_.ap[0]), [-1, n]])

            steps = []
            for c in range(8):
                steps.append(dict(c=c, d=0, si=-1, sub=0, nsub=1, sl=512, rs0=1, tok0=0, slen=512))
                steps.append(dict(c=c, d=1, si=-1, sub=0, nsub=1, sl=512, rs0=1, tok0=0, slen=512))
                for si, (rs0, slen, tok0) in enumerate(rseg):
                    if si < 2:
                        continue
                    sl = min(slen, SW)
                    nsub = slen // sl
                    for sub in range(nsub):
                        steps.append(dict(c=c, d=0, si=si, sub=sub, nsub=nsub, sl=sl, rs0=rs0, tok0=tok0, slen=slen))
                    for sub in reversed(range(nsub)):
                        steps.append(dict(c=c, d=1, si=si, sub=sub, nsub=nsub, sl=sl, rs0=rs0, tok0=tok0, slen=slen))
            cur_c = [-1]

            def F(kk):
                st = steps[kk]
                c, d, n = st["c"], st["d"], st["sl"]
                r0 = st["rs0"] + st["sub"] * n
                if c != cur_c[0]:
                    cur_c[0] = c
                    for j in range(4):
                        k.ts(dgl[:, j, :], cmh[:, 1, :], pvh[:, o_cw + c * 4 + j:o_cw + c * 4 + j + 1], None, ALU.mult,
                             reads=[cmt, pvt], writes=[dglt])
                X = XS[kk % 3]
                pc = k.psum()
                for j in range(4):
                    if st["si"] == -1:
                        a_ = rech[:, c, r0 - 1 + j:r0 + j]
                        rhs_ = bass.AP(a_.tensor, a_.offset, [list(a_.ap[0]), [259, 2], [1, 256]])
                        out_ = pc.ap[:, 0:512].rearrange("p (s t) -> p s t", s=2)
                    else:
                        rhs_ = rech[:, c, r0 - 1 + j:r0 - 1 + j + n]
                        out_ = pc.ap[:, 0:n]
                    k.mm(out_, dgl[:, j, :], rhs_, j == 0, j == 3, reads=[dglt, rect], writes=[pc])
                k.ts(X["xc"][:, 0:n], pc.ap[:, 0:n], hbh_[:, 40:41], hbh_[:, 32 + c:33 + c], ALU.mult, ALU.add,
                     reads=[pc, hbit], writes=[X["xc_t"]])
                k.ts(X["xcb"][:, 0:n], pc.ap[:, 0:n], pvh[:, o_cb + c:o_cb + c + 1], None, ALU.add,
                     reads=[pc, pvt], writes=[X["xcb_t"]])
                pr = k.psum()
                pi_ = k.psum()
                k.mm(pr.ap[:, 0:n], wgh[:, (0 * 2 + d) * 8 + c, :], X["xcb"][:, 0:n], True, True,
                     reads=[wgt, X["xcb_t"]], writes=[pr])
                k.mm(pi_.ap[:, 0:n], wgh[:, (1 * 2 + d) * 8 + c, :], X["xcb"][:, 0:n], True, True,
                     reads=[wgt, X["xcb_t"]], writes=[pi_])
                st["pr"], st["pi"] = pr, pi_

            def M(kk):
                st = steps[kk]
                c, d, n = st["c"], st["d"], st["sl"]
                S = sets[kk % 2]
                pr, pi_ = st["pr"], st["pi"]
                dc = d * 8 + c
                k.act(S["A"][:, 0:n], pr.ap[:, 0:n], AF.Tanh, reads=[pr, hbit], writes=[S["A_t"]],
                      bias=hbh_[:, dc:dc + 1], scale=0.5)
                k.act(S["I"][:, 0:n], pi_.ap[:, 0:n], AF.Tanh, reads=[pi_, hbit], writes=[S["I_t"]],
                      bias=hbh_[:, 16 + dc:17 + dc], scale=0.5)
                k.act(S["T"][:, 0:n], S["A"][:, 0:n], AF.Exp, reads=[S["A_t"], sct], writes=[S["T_t"]],
                      scale=sch[:, 0, dc:dc + 1], bias=sch[:, 0, dc:dc + 1])
                k.act(S["A"][:, 0:n], S["A"][:, 0:n], AF.Exp, reads=[S["A_t"], sct], writes=[S["A_t"]],
                      scale=sch[:, 1, dc:dc + 1], bias=sch[:, 1, dc:dc + 1])

            def M2(kk):
                st = steps[kk]
                n = st["sl"]
                S = sets[kk % 2]
                k.act(S["T"][:, 0:n], S["T"][:, 0:n], AF.Sqrt, reads=[S["T_t"], pvt], writes=[S["T_t"]],
                      bias=pvs("one"), scale=-1.0)

            def scan(out_ap, a_ap, b_ap, init, reads, writes):
                k.add("dve", lambda e: e.tensor_tensor_scan(out_ap, a_ap, b_ap, init, ALU.mult, ALU.add),
                      reads=reads, writes=writes)

            def B(kk):
                st = steps[kk]
                c, d, n, si, sub, nsub = st["c"], st["d"], st["sl"], st["si"], st["sub"], st["nsub"]
                S = sets[kk % 2]
                X = XS[kk % 3]
                k.stt(S["I"][:, 0:n], S["I"][:, 0:n], 1.0, X["xc"][:, 0:n], ALU.add, ALU.mult,
                      reads=[S["I_t"], X["xc_t"]], writes=[S["I_t"]])
                k.tt(S["T"][:, 0:n], S["T"][:, 0:n], S["I"][:, 0:n], ALU.mult, reads=[S["T_t"], S["I_t"]], writes=[S["T_t"]])
                if si == -1 and d == 0:
                    for q_ in range(2):
                        cs_ = slice(q_ * 256, (q_ + 1) * 256)
                        scan(Hfh[:, cs_], S["A"][:, cs_], S["T"][:, cs_], 0.0, [S["A_t"], S["T_t"], Hft], [Hft])
                        k.copy(soh[:, c, q_, 0:1], Hfh[:, q_ * 256 + 255:q_ * 256 + 256], reads=[Hft], writes=[sot])
                elif si == -1:
                    for q_ in range(2):
                        lo = q_ * 256
                        scan(rev(S["hbk"], lo + 255, 256), rev(S["A"], lo + 255, 256), rev(S["T"], lo + 255, 256), 0.0,
                             [S["A_t"], S["T_t"]], [S["hbk_t"]])
                        k.copy(soh[:, c, q_, 1:2], S["hbk"][:, lo:lo + 1], reads=[S["hbk_t"]], writes=[sot])
                    k.tt(S["hbk"][:, 0:512], S["hbk"][:, 0:512], Hfh[:, 0:512], ALU.add,
                         reads=[S["hbk_t"], Hft], writes=[S["hbk_t"]])
                    k.tt(ggh[:, c, 0:512], S["hbk"][:, 0:512], ggh[:, c, 0:512], ALU.mult,
                         reads=[S["hbk_t"], ggt[c]], writes=[ggt[c]])
                elif d == 0:
                    if sub == 0:
                        init = pvh[:, o_s0 + c * 2:o_s0 + c * 2 + 1] if si == 2 else 0.0
                    else:
                        init = Hfh[:, sub * n - 1:sub * n]
                    scan(Hfh[:, sub * n:(sub + 1) * n], S["A"][:, 0:n], S["T"][:, 0:n], init,
                         [S["A_t"], S["T_t"], pvt, Hft], [Hft])
                    if si < 2 and sub == nsub - 1:
                        k.copy(soh[:, c, si, 0:1], Hfh[:, st["slen"] - 1:st["slen"]], reads=[Hft], writes=[sot])
                else:
                    if sub == nsub - 1:
                        init = pvh[:, o_s0 + c * 2 + 1:o_s0 + c * 2 + 2] if si == 2 else 0.0
                    else:
                        init = carh[:, 0:1]
                    scan(rev(S["hbk"], n - 1, n), rev(S["A"], n - 1, n), rev(S["T"], n - 1, n), init,
                         [S["A_t"], S["T_t"], pvt, cart], [S["hbk_t"]])
                    k.copy(carh[:, 0:1], S["hbk"][:, 0:1], reads=[S["hbk_t"]], writes=[cart])
                    if si < 2:
                        k.copy(soh[:, c, si, 1:2], S["hbk"][:, 0:1], reads=[S["hbk_t"]], writes=[sot])
                    k.tt(S["hbk"][:, 0:n], S["hbk"][:, 0:n], Hfh[:, sub * n:(sub + 1) * n], ALU.add,
                         reads=[S["hbk_t"], Hft], writes=[S["hbk_t"]])
                    t0 = st["tok0"] + sub * n
                    k.tt(ggh[:, c, t0:t0 + n], S["hbk"][:, 0:n], ggh[:, c, t0:t0 + n], ALU.mult,
                         reads=[S["hbk_t"], ggt[c]], writes=[ggt[c]])

            NS = len(steps)
            F(0)
            for kk in range(0, NS, 2):
                if kk + 1 < NS:
                    F(kk + 1)
                M(kk)
                if kk + 2 < NS:
                    F(kk + 2)
                if kk + 1 < NS:
                    M(kk + 1)
                M2(kk)
                if kk + 1 < NS:
                    M2(kk + 1)
                B(kk)
                if kk + 1 < NS:
                    B(kk + 1)
            k.dma("sp", so, soh[:, :, :, :].rearrange("p a b c -> p (a b c)"), reads=[sot], tile=sot)
            k.barrier()
            off[0] = PH3
            wo = alloc("wlo", [128, 8, 1024], BF16)
            wot = Tl("wlo", wo[:, :, :])
            k.dma("pool", wo[:, :, :], lru_out.rearrange("(c p) n -> p c n", p=128), writes=[wot], tile=wot)
            for b in range(NB):
                r = 0 if b == 0 else 1
                for oc in range(8):
                    ps = k.psum()
                    for kc in range(8):
                        k.mm(ps.ap, wo[:, kc, oc * 128:(oc + 1) * 128], ggh[:, kc, b * 512:(b + 1) * 512],
                             kc == 0, kc == 7, reads=[wot, ggt[kc]], writes=[ps])
                    k.stt(Yt[oc][b].ap, ps.ap, modp(1, 2, oc, r), Yt[oc][b].ap, ALU.mult, ALU.add,
                          reads=[ps, modt, Yt[oc][b]], writes=[Yt[oc][b]])
            k.barrier()
        if stage >= 8:
            ffn(1)
        off[0] = PH
        sq_h, sq_t, rs_t, tmp_h, tmp_t = norm_temps()
        rs2h = alloc("rs2", [128, 512], F32)
        rss = [rs_t, Tl("rs2", rs2h[:, :])]
        o_f = PV["final"][0]
        if stage >= 9:
            rms_stats(0, rss[0], sq_h, sq_t)
        for b in range(NB):
            if stage >= 9 and b + 1 < NB:
                rms_stats(b + 1, rss[(b + 1) % 2], sq_h, sq_t)
            for c in range(8):
                if stage >= 9:
                    k.stt(Yt[c][b].ap, Yt[c][b].ap, pvh[:, o_f + c:o_f + c + 1], rss[b % 2].ap, ALU.mult, ALU.mult,
                          reads=[Yt[c][b], pvt, rss[b % 2]], writes=[Yt[c][b]])
                k.dma("sp", yT[c * 128:(c + 1) * 128, b * 512:(b + 1) * 512], Yt[c][b].ap,
                      reads=[Yt[c][b]], tile=Yt[c][b])
        k.finalize()
    return nc


def _pp(v, nch):
    return np.ascontiguousarray(np.asarray(v, np.float32).reshape(nch, 128).T)


def _rope_tables():
    t = np.arange(2048)
    row = (t // 64).astype(np.float32)
    col = (t % 64).astype(np.float32)
    inv = (10000.0 ** (-np.arange(16, dtype=np.float32) / 16)).astype(np.float32)
    cos = np.zeros((128, 2048), np.float32)
    sin = np.zeros((128, 2048), np.float32)
    for p in range(128):
        d = p % 64
        pos = row if d < 32 else col
        ang = (pos * inv[d % 16]).astype(np.float32)
        cos[p] = np.cos(ang)
        sin[p] = np.sin(ang)
    return np.stack([cos, sin], axis=1)


def _const_mats():
    rot = np.zeros((128, 128), np.float32)
    for fp in range(128):
        d = fp % 32
        if d < 16:
            rot[fp + 16, fp] = -1.0
        else:
            rot[fp - 16, fp] = 1.0
    ident = np.eye(128, dtype=np.float32)
    p = np.arange(128)[:, None]
    c = np.arange(128)[None, :]
    mprev = (c <= p).astype(np.float32)
    mnext = (p <= c).astype(np.float32)
    return np.stack([rot, ident, mprev, mnext], axis=1)


_NC_CACHE = {}


def kernel(x_prompt, x_sample, c, cache_k, cache_v, state_lru, c_ctx,
           w_mod, b_mod, norm_mix, norm_ffn, w_ff1, w_ff2,
           att_in, att_out, att_sink, conv_w, conv_b, conv_norm_g, conv_norm_b,
           lru_in, lru_out, lru_conv_w, lru_conv_b, lru_wa, lru_ba, lru_wx, lru_bx, lru_lam,
           final_norm, _stage=None):
    stage = STAGE if _stage is None else _stage
    f = lambda a: np.asarray(a, np.float32)
    x_prompt, x_sample, c, cache_k, cache_v, state_lru, c_ctx = map(f, (x_prompt, x_sample, c, cache_k, cache_v, state_lru, c_ctx))
    att_in = f(att_in)[0]
    zc = np.zeros((1024, 64), np.float32)
    k0, k1 = att_in[:, 512:576], att_in[:, 576:640]
    w_qkv = np.ascontiguousarray(np.concatenate(
        [att_in[:, 0:512], k0, zc, zc, k0, k1, zc, zc, k1, att_in[:, 640:768]], axis=1))
    w_u = np.ascontiguousarray(att_in[:, 768:1792])
    lru_g = np.ascontiguousarray(np.stack([f(lru_wa)[0], f(lru_wx)[0]], 0).reshape(32, 128, 128).transpose(1, 0, 2))
    tabs = _rope_tables()
    cm = _const_mats()
    shared = {
        "w_mod": f(w_mod), "w_qkv": w_qkv, "w_u": w_u, "w_ao": f(att_out)[0], "w_ff1": f(w_ff1), "w_ff2": f(w_ff2),
        "lru_in": f(lru_in)[0], "lru_out": f(lru_out)[0], "lru_g": lru_g, "tabs": tabs, "cmats": cm,
    }
    pv_common = {
        "b_mod": np.concatenate([_pp(f(b_mod)[l], 48) for l in range(2)], 1),
        "norm_mix": np.concatenate([_pp(f(norm_mix)[l], 8) for l in range(2)], 1),
        "norm_ffn": np.concatenate([_pp(f(norm_ffn)[l], 8) for l in range(2)], 1),
        "final": _pp(f(final_norm), 8),
        "conv_w": np.ascontiguousarray(f(conv_w)[0].reshape(31, 4, 128).transpose(2, 1, 0)).reshape(128, 124),
        "conv_b": _pp(f(conv_b)[0], 4), "cng": _pp(f(conv_norm_g)[0], 4), "cnb": _pp(f(conv_norm_b)[0], 4),
        "lcw": np.ascontiguousarray(f(lru_conv_w)[0].reshape(4, 8, 128).transpose(2, 1, 0)).reshape(128, 32),
        "lcb": _pp(f(lru_conv_b)[0], 8),
        "lba": np.concatenate([_pp(f(lru_ba)[0, d], 8) for d in range(2)], 1),
        "lbx": np.concatenate([_pp(f(lru_bx)[0, d], 8) for d in range(2)], 1),
        "lam": np.concatenate([_pp(f(lru_lam)[0, d], 8) for d in range(2)], 1),
        "sink": np.broadcast_to(f(att_sink)[0][None, :], (128, 8)),
        "eps": np.full((128, 1), EPS, np.float32), "one": np.ones((128, 1), np.float32),
    }
    in_maps = []
    for core in range(8):
        s = core // 2
        xcat = np.concatenate([x_prompt[2 * core], x_prompt[2 * core + 1], x_sample[s]], axis=0)
        pv = np.zeros((128, NPV), np.float32)
        for name, val in pv_common.items():
            o, w = PV[name]
            pv[:, o:o + w] = val
        o, w = PV["st0"]
        st = state_lru[s, 0]
        pv[:, o:o + w] = np.stack([_pp(st[0], 8), _pp(st[1], 8)], axis=2).reshape(128, 16)
        o, w = PV["cond"]
        pv[:, o:o + w] = np.stack([_pp(c_ctx, 8), _pp(c[s], 8)], axis=2).reshape(128, 16)
        ck = cache_k[s, 0]
        zk = np.zeros((64, 256), np.float32)
        ckT = np.stack([np.concatenate([ck[:, 0, :].T, zk], 0), np.concatenate([zk, ck[:, 0, :].T], 0),
                        np.concatenate([ck[:, 1, :].T, zk], 0), np.concatenate([zk, ck[:, 1, :].T], 0)], axis=1)
        m = dict(shared)
        m.update({"xT": np.ascontiguousarray(xcat.T), "pv": pv, "ckT": np.ascontiguousarray(ckT),
                  "cV": np.ascontiguousarray(cache_v[s, 0].reshape(256, 128))})
        in_maps.append(m)
    if stage not in _NC_CACHE:
        _NC_CACHE[stage] = build_program(stage)
    nc = _NC_CACHE[stage]
    res = run_bass_kernel_spmd(nc, in_maps, core_ids=list(range(8)))
    y_prompt = np.zeros((16, 256, 1024), np.float32)
    y_sample = np.zeros((4, 2048, 1024), np.float32)
    nk = np.zeros((16, 1, 256, 2, 64), np.float32)
    nv = np.zeros((16, 1, 256, 2, 64), np.float32)
    nh = np.zeros((16, 1, 2, 1024), np.float32)
    for core in range(8):
        r = res.results[core]
        y = r["yT"].T
        y_prompt[2 * core] = y[0:256]
        y_prompt[2 * core + 1] = y[256:512]
        if core % 2 == 0:
            y_sample[core // 2] = y[512:]
        ko = r["ko"]
        vo = r["vo"]
        so = r["so"].reshape(128, 8, 2, 2)
        for j in range(2):
            for g in range(2):
                nk[2 * core + j, 0, :, g, :] = ko[0:64, g, j * 256:(j + 1) * 256].T
            nv[2 * core + j, 0] = vo[j * 256:(j + 1) * 256].reshape(256, 2, 64)
            for d in range(2):
                nh[2 * core + j, 0, d] = so[:, :, j, d].T.reshape(1024)
    return (y_prompt, y_sample, nk, nv, nh)
```

```python
import numpy as np
from contextlib import ExitStack
import concourse.bass as bass
import concourse.mybir as mybir
from concourse.bass_utils import run_bass_kernel_spmd

F32 = mybir.dt.float32
BF16 = mybir.dt.bfloat16
AF = mybir.ActivationFunctionType
ALU = mybir.AluOpType

NT = 2560
NB = 5
EPS = 1e-6
STAGE = 99
import os as _os
SUB = _os.environ.get('MK_SUB', 'ABCDEK')
ATT = _os.environ.get('MK_ATT', 'PSQXMVN')
DEBUG = {}


class Tl:
    __slots__ = ("name", "ap", "lw", "rd", "sem", "cnt", "excl")

    def __init__(self, name, ap):
        self.excl = False
        self.name = name
        self.ap = ap
        self.lw = None
        self.rd = []
        self.sem = None
        self.cnt = 0


class Op:
    __slots__ = ("eng", "fn", "needs", "sig", "val", "dma", "dsem", "dval")

    def __init__(self, eng, fn):
        self.eng = eng
        self.fn = fn
        self.needs = []
        self.sig = False
        self.val = 0
        self.dma = False
        self.dsem = None
        self.dval = 0


class KB:
    ENGS = ("pe", "act", "dve", "pool", "sp")

    def __init__(self, nc, es):
        self.nc = nc
        self.es = es
        self.q = {e: [] for e in self.ENGS}
        self.esem = {e: es.enter_context(nc.semaphore("s_" + e)) for e in self.ENGS}
        self.dma_tiles = []
        self.nsem = 0
        self.psum_tiles = []
        self.psum_i = 0
        self.acc_i = 0
        self.nalloc = 0

    def at(self, name, shape, dt, off):
        self.nalloc += 1
        h = self.nc.alloc_sbuf_tensor_at("%s_%d" % (name, self.nalloc), list(shape), dt, offset=off)
        return h

    def init_psum(self):
        for i in range(8):
            h = self.es.enter_context(self.nc.psum_tensor("ps%d" % i, [128, 512], F32))
            self.psum_tiles.append(Tl("ps%d" % i, h[:, :]))
            self.psum_tiles[-1].excl = True

    def psum(self):
        t = self.psum_tiles[self.psum_i % 6]
        self.psum_i += 1
        return t

    def psum_acc(self):
        t = self.psum_tiles[6 + self.acc_i % 2]
        self.acc_i += 1
        return t

    def add(self, eng, fn, reads=(), writes=(), dma_tile=None):
        op = Op(eng, fn)
        deps = {}
        for t in reads:
            if t.lw is not None:
                deps[id(t.lw)] = (t.lw, "raw")
            if t.excl:
                for r in t.rd:
                    if r.eng != eng and id(r) not in deps:
                        deps[id(r)] = (r, "rar")
        for t in writes:
            if t.lw is not None and id(t.lw) not in deps:
                deps[id(t.lw)] = (t.lw, "waw")
            for r in t.rd:
                if id(r) not in deps:
                    deps[id(r)] = (r, "war")
        for p, kind in deps.values():
            if p.dma or p.eng != eng or eng != "pe":
                op.needs.append(p)
        if dma_tile is not None:
            op.dma = True
            if dma_tile.sem is None:
                dma_tile.sem = self.es.enter_context(self.nc.semaphore("d%d" % self.nsem))
                self.nsem += 1
                self.dma_tiles.append(dma_tile)
            dma_tile.cnt += 16
            op.dsem = dma_tile.sem
            op.dval = dma_tile.cnt
        for t in reads:
            t.rd.append(op)
        for t in writes:
            t.lw = op
            t.rd = []
        self.q[eng].append(op)
        return op

    def barrier(self):
        lasts = [self.q[e][-1] for e in self.ENGS if self.q[e] and self.q[e][-1].dma is not None]
        pend = [(t.sem, t.cnt) for t in self.dma_tiles]
        for e in self.ENGS:
            op = Op(e, None)
            op.needs = [p for p in lasts if (not p.dma) and p.eng != e]
            op.val = list(pend)
            op.dma = None
            self.q[e].append(op)

    def finalize(self):
        nc = self.nc
        for e in self.ENGS:
            for i, op in enumerate(self.q[e]):
                op.dval = op.dval if op.dma else i
        for e in self.ENGS:
            for op in self.q[e]:
                last = {}
                keep = []
                for p in op.needs:
                    if p.dma:
                        keep.append(p)
                    elif p.eng not in last or last[p.eng].dval < p.dval:
                        last[p.eng] = p
                for p in last.values():
                    p.sig = True
                    keep.append(p)
                op.needs = keep
        for e in self.ENGS:
            c = 0
            for op in self.q[e]:
                if op.dma is False and op.sig:
                    c += 1
                    op.val = c
        final = [(t.sem, t.cnt) for t in self.dma_tiles]
        esem = self.esem
        q = self.q

        def run(ename, eng):
            known = {}

            def wait(s, v):
                if known.get(id(s), 0) >= v:
                    return
                eng.wait_ge(s, v)
                known[id(s)] = v

            for op in q[ename]:
                waits = {}
                for p in op.needs:
                    if p.dma:
                        s, v = p.dsem, p.dval
                    else:
                        s, v = esem[p.eng], p.val
                    if id(s) not in waits or waits[id(s)][1] < v:
                        waits[id(s)] = (s, v)
                for s, v in waits.values():
                    wait(s, v)
                if op.dma is None:
                    for s, v in op.val:
                        wait(s, v)
                    continue
                ins = op.fn(eng)
                if op.dma:
                    ins.then_inc(op.dsem, 16)
                elif op.sig:
                    ins.then_inc(esem[ename], 1)
            if ename == "sp":
                for s, v in final:
                    wait(s, v)

        with nc.Block() as block:
            @block.tensor
            def _(eng):
                run("pe", eng)

            @block.scalar
            def _(eng):
                run("act", eng)

            @block.vector
            def _(eng):
                run("dve", eng)

            @block.gpsimd
            def _(eng):
                run("pool", eng)

            @block.sync
            def _(eng):
                run("sp", eng)

    def dma(self, eng, out_ap, in_ap, reads=(), writes=(), tile=None, **kw):
        return self.add(eng, lambda e: e.dma_start(out=out_ap, in_=in_ap, **kw),
                        reads=reads, writes=writes, dma_tile=tile)

    def mm(self, out_ap, lhsT, rhs, start, stop, reads=(), writes=()):
        return self.add("pe", lambda e: e.matmul(out_ap, lhsT, rhs, start=start, stop=stop),
                        reads=reads, writes=writes)

    def act(self, out_ap, in_ap, func, reads=(), writes=(), bias=None, scale=None):
        kw = {}
        if bias is not None:
            kw["bias"] = bias
        if scale is not None:
            kw["scale"] = scale
        return self.add("act", lambda e: e.activation(out_ap, in_ap, func, **kw),
                        reads=reads, writes=writes)

    def tt(self, out_ap, in0, in1, op, reads=(), writes=(), eng="dve"):
        return self.add(eng, lambda e: e.tensor_tensor(out_ap, in0, in1, op),
                        reads=reads, writes=writes)

    def ts(self, out_ap, in0, s1, s2, op0, op1=None, reads=(), writes=(), eng="dve"):
        if op1 is None:
            return self.add(eng, lambda e: e.tensor_scalar(out_ap, in0, s1, None, op0),
                            reads=reads, writes=writes)
        return self.add(eng, lambda e: e.tensor_scalar(out_ap, in0, s1, s2, op0, op1),
                        reads=reads, writes=writes)

    def stt(self, out_ap, in0, scalar, in1, op0, op1, reads=(), writes=(), eng="dve"):
        return self.add(eng, lambda e: e.scalar_tensor_tensor(out_ap, in0, scalar, in1, op0, op1),
                        reads=reads, writes=writes)

    def copy(self, out_ap, in_ap, reads=(), writes=(), eng="dve"):
        return self.add(eng, lambda e: e.tensor_copy(out_ap, in_ap), reads=reads, writes=writes)

    def memset(self, out_ap, val, writes=(), eng="dve"):
        return self.add(eng, lambda e: e.memset(out_ap, val), writes=writes)

    def recip(self, out_ap, in_ap, reads=(), writes=()):
        return self.add("dve", lambda e: e.reciprocal(out_ap, in_ap), reads=reads, writes=writes)


PV = {}
_o = 0
for _n, _w in [("b_mod", 96), ("norm_mix", 16), ("norm_ffn", 16), ("final", 8), ("conv_w", 124),
               ("conv_b", 4), ("cng", 4), ("cnb", 4), ("lcw", 32), ("lcb", 8), ("lba", 16),
               ("lbx", 16), ("lam", 16), ("st0", 16), ("sink", 8), ("cond", 16), ("eps", 1),
               ("one", 1)]:
    PV[_n] = (_o, _w)
    _o += _w
NPV = _o

BASE = 16640
TOP = 229376


def build_program(stage=99):
    nc = bass.Bass("TRN2", target_bir_lowering=False)

    def din(name, shape):
        return nc.dram_tensor(name, list(shape), F32, kind="ExternalInput").ap()

    def dout(name, shape):
        return nc.dram_tensor(name, list(shape), F32, kind="ExternalOutput").ap()

    xT = din("xT", [1024, NT])
    pv_d = din("pv", [128, NPV])
    w_mod = din("w_mod", [2, 1024, 6144])
    w_qkv = din("w_qkv", [1024, 1152])
    w_u = din("w_u", [1024, 1024])
    w_ao = din("w_ao", [1024, 1024])
    w_ff1 = din("w_ff1", [2, 1024, 4096])
    w_ff2 = din("w_ff2", [2, 4096, 1024])
    lru_in = din("lru_in", [1024, 2048])
    lru_out = din("lru_out", [1024, 1024])
    lru_g = din("lru_g", [128, 32, 128])
    ckT = din("ckT", [128, 4, 256])
    cV = din("cV", [256, 128])
    tabs = din("tabs", [128, 2, 2048])
    cmats = din("cmats", [128, 4, 128])
    yT = dout("yT", [1024, NT])
    ko = dout("ko", [128, 2, 512])
    vo = dout("vo", [512, 128])
    so = dout("so", [128, 32])
    dbg = {}

    with ExitStack() as es:
        k = KB(nc, es)
        k.init_psum()
        off = [BASE]

        def alloc(name, shape, dt, base=None):
            nb = int(np.prod(shape[1:])) * (2 if dt == BF16 else 4)
            nb = (nb + 63) // 64 * 64
            if base is None:
                o = off[0]
                off[0] += nb
                assert off[0] <= TOP, (name, off[0])
                return k.at(name, shape, dt, o)
            return k.at(name, shape, dt, base)

        Yh = alloc("Y", [128, 8, NT], F32)
        Yt = [[Tl("Y%d_%d" % (c, b), Yh[:, c, b * 512:(b + 1) * 512]) for b in range(NB)] for c in range(8)]
        pvh = alloc("pv", [128, NPV], F32)
        pvt = Tl("pv", pvh[:, :])
        modh = alloc("mod", [128, 2, 48, 2], F32)
        modt = Tl("mod", modh[:, :, :, :])
        ABh = alloc("AB", [128, 2, 2, 8, 2], F32)
        ABt = Tl("AB", ABh[:, :, :, :, :])
        cmh = alloc("cm", [128, 4, 128], BF16)
        cmt = Tl("cm", cmh[:, :, :])
        onesh = alloc("ones", [128, 2, 128], BF16)
        onest = Tl("ones", onesh[:, :, :])
        esh = alloc("es", [128, 8], F32)
        est = Tl("es", esh[:, :])
        scb_h = alloc("scb", [128, 8, 2], BF16)
        scbt = Tl("scb", scb_h[:, :, :])
        PH = off[0]

        def pvs(name, i=0, n=1):
            o, w = PV[name]
            return pvh[:, o + i:o + i + n]

        k.dma("sp", pvh[:, :], pv_d, writes=[pvt], tile=pvt)
        for c in range(8):
            for b in range(NB):
                k.dma("sp", Yt[c][b].ap, xT[c * 128:(c + 1) * 128, b * 512:(b + 1) * 512],
                      writes=[Yt[c][b]], tile=Yt[c][b])
        k.dma("pool", cmh[:, :, :], cmats, writes=[cmt], tile=cmt)
        k.memset(onesh[:, 0, :], 1.0 / 1024, writes=[onest])
        k.memset(onesh[:, 1, :], 1.0 / 512, writes=[onest])
        k.act(esh[:, :], pvs("sink", 0, 8), AF.Exp, reads=[pvt], writes=[est])
        o_c = PV["cond"][0]
        k.act(scb_h[:, :, :], pvh[:, o_c:o_c + 16].rearrange("p (c r) -> p c r", r=2), AF.Silu,
              reads=[pvt], writes=[scbt])

        off[0] = PH
        wm = [alloc("wm%d" % i, [128, 8, 1024], BF16) for i in range(2)]
        wmt = [Tl("wm%d" % i, wm[i][:, :, :]) for i in range(2)]
        def adaln_finish(l, ps):
            ob = PV["b_mod"][0] + l * 48
            k.tt(modh[:, l, :, :], ps.ap[:, 0:96].rearrange("p (f r) -> p f r", r=2),
                 pvh[:, ob:ob + 48].unsqueeze(2).to_broadcast([128, 48, 2]), ALU.add,
                 reads=[ps, pvt], writes=[modt])
            for m, (gname, sj) in enumerate([("norm_mix", 1), ("norm_ffn", 4)]):
                og = PV[gname][0] + l * 8
                k.stt(ABh[:, l, m, :, :], modh[:, l, sj * 8:(sj + 1) * 8, :], 1.0,
                      pvh[:, og:og + 8].unsqueeze(2).to_broadcast([128, 8, 2]), ALU.add, ALU.mult,
                      reads=[modt, pvt], writes=[ABt])

        def adaln_evac(l, j, ps, c0):
            ob = PV["b_mod"][0] + l * 48 + j * 8
            k.tt(modh[:, l, j * 8:(j + 1) * 8, :], ps.ap[:, c0:c0 + 16].rearrange("p (f r) -> p f r", r=2),
                 pvh[:, ob:ob + 8].unsqueeze(2).to_broadcast([128, 8, 2]), ALU.add,
                 reads=[ps, pvt], writes=[modt])

        def adaln_AB(l, m):
            gname, sj = [("norm_mix", 1), ("norm_ffn", 4)][m]
            og = PV[gname][0] + l * 8
            k.stt(ABh[:, l, m, :, :], modh[:, l, sj * 8:(sj + 1) * 8, :], 1.0,
                  pvh[:, og:og + 8].unsqueeze(2).to_broadcast([128, 8, 2]), ALU.add, ALU.mult,
                  reads=[modt, pvt], writes=[ABt])

        ps = k.psum()
        for j in range(2):
            k.dma("pool", wm[j][:, :, :],
                  w_mod[0, :, j * 1024:(j + 1) * 1024].rearrange("(c p) n -> p c n", p=128),
                  writes=[wmt[j]], tile=wmt[j])
            for fc in range(8):
                col = (j * 8 + fc) * 2
                for kc in range(8):
                    k.mm(ps.ap[:, col:col + 2], wm[j][:, kc, fc * 128:(fc + 1) * 128],
                         scb_h[:, kc, :], kc == 0, kc == 7, reads=[wmt[j], scbt], writes=[ps])
        for j in range(2):
            adaln_evac(0, j, ps, j * 16)
        adaln_AB(0, 0)

        def modp(l, j, c, r):
            return modh[:, l, j * 8 + c, r:r + 1]

        def Ap(l, m, c, r):
            return ABh[:, l, m, c, r:r + 1]

        k.barrier()

        def rms_stats(b, rs_t, sq_h, sq_t, sq_dve=False):
            ps = k.psum()
            for c in range(8):
                if sq_dve:
                    k.tt(sq_h[:, c % 2, :], Yt[c][b].ap, Yt[c][b].ap, ALU.mult, reads=[Yt[c][b]], writes=[sq_t[c % 2]])
                else:
                    k.act(sq_h[:, c % 2, :], Yt[c][b].ap, AF.Square, reads=[Yt[c][b]], writes=[sq_t[c % 2]])
                k.mm(ps.ap, onesh[:, 0, :], sq_h[:, c % 2, :], c == 0, c == 7,
                     reads=[onest, sq_t[c % 2]], writes=[ps])
            k.act(rs_t.ap, ps.ap, AF.Ln, reads=[ps, pvt], writes=[rs_t], bias=pvs("eps"), scale=1.0)
            k.act(rs_t.ap, rs_t.ap, AF.Exp, reads=[rs_t], writes=[rs_t], scale=-0.5)

        def modulate(b, l, m, rs_t, tmp_h, tmp_t, hb_h, hb_t):
            r = 0 if b == 0 else 1
            jb = 0 if m == 0 else 3
            for c in range(8):
                i = c % 2
                k.tt(tmp_h[:, i, :], Yt[c][b].ap, rs_t.ap, ALU.mult, reads=[Yt[c][b], rs_t], writes=[tmp_t[i]])
                k.act(hb_h[:, c, :], tmp_h[:, i, :], AF.Identity, reads=[tmp_t[i], ABt, modt], writes=[hb_t],
                      scale=Ap(l, m, c, r), bias=modp(l, jb, c, r))

        def norm_temps():
            sq_h = alloc("sq", [128, 2, 512], BF16)
            sq_t = [Tl("sq%d" % i, sq_h[:, i, :]) for i in range(2)]
            rs_h = alloc("rs", [128, 512], F32)
            rs_t = Tl("rs", rs_h[:, :])
            tmp_h = alloc("ntmp", [128, 2, 512], F32)
            tmp_t = [Tl("nt%d" % i, tmp_h[:, i, :]) for i in range(2)]
            return sq_h, sq_t, rs_t, tmp_h, tmp_t

        def ffn(l):
            off[0] = PH
            h2h = alloc("h2", [128, 8, NT], BF16)
            h2t = [Tl("h2_%d" % b, h2h[:, :, b * 512:(b + 1) * 512]) for b in range(NB)]
            hidh = alloc("hid", [128, 4, NT], BF16)
            hidt = [Tl("hid_%d" % b, hidh[:, :, b * 512:(b + 1) * 512]) for b in range(NB)]
            w1 = [alloc("w1_%d" % i, [128, 8, 512], BF16) for i in range(2)]
            w1t = [Tl("w1_%d" % i, w1[i][:, :, :]) for i in range(2)]
            w2 = [alloc("w2_%d" % i, [128, 4, 1024], BF16) for i in range(2)]
            w2t = [Tl("w2_%d" % i, w2[i][:, :, :]) for i in range(2)]
            rl_h = alloc("rl", [128, 2, 512], BF16)
            rl_t = [Tl("rl%d" % i, rl_h[:, i, :]) for i in range(2)]
            sq_h, sq_t, rs_t, tmp_h, tmp_t = norm_temps()
            if l == 0:
                wm1 = alloc("wm1x", [128, 8, 1024], BF16)
                wm1t = Tl("wm1x", wm1[:, :, :])
                pa = k.psum_tiles[6]
            def _normf(b):
                rms_stats(b, rs_t, sq_h, sq_t)
                modulate(b, l, 1, rs_t, tmp_h, tmp_t, h2h[:, :, b * 512:(b + 1) * 512], h2t[b])
            _normf(0)
            n = 0
            for j in range(8):
                i = j % 2
                k.dma("pool", w1[i][:, :, :],
                      w_ff1[l, :, j * 512:(j + 1) * 512].rearrange("(c p) n -> p c n", p=128),
                      writes=[w1t[i]], tile=w1t[i])
                k.dma("pool", w2[i][:, :, :],
                      w_ff2[l, j * 512:(j + 1) * 512, :].rearrange("(c p) n -> p c n", p=128),
                      writes=[w2t[i]], tile=w2t[i])
                if l == 0 and j < 6:
                    k.dma("pool", wm1[:, :, :],
                          w_mod[1, :, j * 1024:(j + 1) * 1024].rearrange("(c p) n -> p c n", p=128),
                          writes=[wm1t], tile=wm1t)
                for b in range(NB):
                    if j == 0 and b + 1 < NB:
                        _normf(b + 1)
                    for hc in range(4):
                        ps = k.psum()
                        for kc in range(8):
                            k.mm(ps.ap, w1[i][:, kc, hc * 128:(hc + 1) * 128], h2h[:, kc, b * 512:(b + 1) * 512],
                                 kc == 0, kc == 7, reads=[w1t[i], h2t[b]], writes=[ps])
                        ri = n % 2
                        n += 1
                        k.act(rl_h[:, ri, :], ps.ap, AF.Relu, reads=[ps], writes=[rl_t[ri]])
                        k.tt(hidh[:, hc, b * 512:(b + 1) * 512], rl_h[:, ri, :], rl_h[:, ri, :], ALU.mult,
                             reads=[rl_t[ri]], writes=[hidt[b]])
                for b in range(NB):
                    r = 0 if b == 0 else 1
                    for oc in range(8):
                        ps = k.psum()
                        for kc in range(4):
                            k.mm(ps.ap, w2[i][:, kc, oc * 128:(oc + 1) * 128], hidh[:, kc, b * 512:(b + 1) * 512],
                                 kc == 0, kc == 3, reads=[w2t[i], hidt[b]], writes=[ps])
                        k.stt(Yt[oc][b].ap, ps.ap, modp(l, 5, oc, r), Yt[oc][b].ap, ALU.mult, ALU.add,
                              reads=[ps, modt, Yt[oc][b]], writes=[Yt[oc][b]])
                if l == 0 and j < 6:
                    for fc in range(8):
                        col = (j * 8 + fc) * 2
                        for kc in range(8):
                            k.mm(pa.ap[:, col:col + 2], wm1[:, kc, fc * 128:(fc + 1) * 128],
                                 scb_h[:, kc, :], kc == 0, kc == 7, reads=[wm1t, scbt], writes=[pa])
                    if j == 5:
                        adaln_finish(1, pa)
            k.barrier()

        off[0] = PH
        GW = 286 + 286 + 2078
        OFF_QKV = off[0]
        qTh = alloc("qT", [128, 4, NT], BF16)
        qTt = [Tl("qT%d" % b, qTh[:, :, b * 512:(b + 1) * 512]) for b in range(NB)]
        kTh = alloc("kT", [128, 4, NT], BF16)
        kTt = [Tl("kT%d" % b, kTh[:, :, b * 512:(b + 1) * 512]) for b in range(NB)]
        Vh = alloc("Vaug", [128, 20, 2, 192], BF16)
        Vt = [Tl("V%d" % b, Vh[:, b * 4:(b + 1) * 4, :, :]) for b in range(NB)]
        OFF_ATT = off[0]
        attTh = alloc("attT", [128, 4, NT], BF16)
        attt = [Tl("att%d" % b, attTh[:, :, b * 512:(b + 1) * 512]) for b in range(NB)]
        gluh = alloc("glu", [128, 4, GW], BF16)
        OFF_T = off[0]
        off[0] = OFF_QKV
        cvTh = alloc("cvT", [128, 4, NT], BF16)
        cvt = [Tl("cv%d" % b, cvTh[:, :, b * 512:(b + 1) * 512]) for b in range(NB)]
        OFF_CV = off[0]

        def gcol(t):
            if t < 256:
                return 15 + t
            if t < 512:
                return 286 + 15 + (t - 256)
            return 572 + 15 + (t - 512)
        glut = [Tl("glu%d" % b, gluh[:, :, gcol(b * 512):gcol(b * 512) + 512]) for b in range(NB)]
        glupad = Tl("glupad", gluh[:, :, :])
        off[0] = OFF_ATT
        k.memset(Vh[:, :, :, 64:128], 1.0, writes=Vt)

        wq = alloc("wqkv", [128, 8, 1152], BF16)
        wqt = Tl("wqkv", wq[:, :, :])
        k.dma("pool", wq[:, :, :], w_qkv.rearrange("(c p) n -> p c n", p=128), writes=[wqt], tile=wqt)
        tabh = alloc("tabs", [128, 2, 2048], BF16)
        tabt = Tl("tabs", tabh[:, :, :])
        k.dma("pool", tabh[:, :, :], tabs, writes=[tabt], tile=tabt)
        hbs = [alloc("hb%d" % i, [128, 8, 512], BF16) for i in range(2)]
        hbts = [Tl("hb%d" % i, hbs[i][:, :, :]) for i in range(2)]
        sq_h, sq_t, rs_t, tmp_h, tmp_t = norm_temps()
        xbh = alloc("xb", [128, 512], BF16)
        xbt = Tl("xb", xbh[:, :])
        t1h = alloc("t1", [128, 2, 512], F32)
        t1t = [Tl("t1_%d" % i, t1h[:, i, :]) for i in range(2)]
        koh = alloc("koS", [128, 2, 512], F32)
        kot = Tl("koS", koh[:, :, :])
        voh = alloc("voS", [128, 4, 128], F32)
        vot = Tl("voS", voh[:, :, :])
        def _norm1(b):
            rms_stats(b, rs_t, sq_h, sq_t)
            modulate(b, 0, 0, rs_t, tmp_h, tmp_t, hbs[b % 2], hbts[b % 2])
        nb1 = NB if stage >= 1 else 0
        if nb1:
            _norm1(0)
        for b in range(nb1):
            hbh, hbt = hbs[b % 2], hbts[b % 2]
            if b + 1 < nb1:
                _norm1(b + 1)
            cols = slice(b * 512, (b + 1) * 512)
            for oc in range(8 if 'B' in SUB else 0):
                if b > 0 and 'C' not in SUB:
                    continue
                ps = k.psum()
                for kc in range(8):
                    k.mm(ps.ap, wq[:, kc, oc * 128:(oc + 1) * 128], hbh[:, kc, :], kc == 0, kc == 7,
                         reads=[wqt, hbt], writes=[ps])
                if oc < 4:
                    dst_ap, dst_t = qTh[:, oc, cols], qTt[b]
                else:
                    dst_ap, dst_t = kTh[:, oc - 4, cols], kTt[b]
                if b == 0:
                    if 'E' in SUB:
                        k.act(dst_ap, ps.ap, AF.Copy, reads=[ps], writes=[dst_t])
                    if oc in (4, 6) and 'K' in SUB:
                        k.copy(koh[:, (oc - 4) // 2, :], ps.ap, reads=[ps], writes=[kot])
                else:
                    tc_ = slice((b - 1) * 512, b * 512)
                    k.act(xbh[:, :], ps.ap, AF.Copy, reads=[ps], writes=[xbt])
                    ps2 = k.psum()
                    k.mm(ps2.ap, cmh[:, 0, :], xbh[:, :], True, True, reads=[cmt, xbt], writes=[ps2])
                    k.tt(t1h[:, 0, :], ps.ap, tabh[:, 0, tc_], ALU.mult, reads=[ps, tabt], writes=[t1t[0]])
                    k.tt(t1h[:, 1, :], ps2.ap, tabh[:, 1, tc_], ALU.mult, reads=[ps2, tabt], writes=[t1t[1]])
                    k.tt(dst_ap, t1h[:, 0, :], t1h[:, 1, :], ALU.add, reads=[t1t[0], t1t[1]], writes=[dst_t])
            for tt_ in range(4 if 'D' in SUB else 0):
                ps = k.psum()
                for kc in range(8):
                    k.mm(ps.ap[:, 0:128], hbh[:, kc, tt_ * 128:(tt_ + 1) * 128], wq[:, kc, 1024:1152],
                         kc == 0, kc == 7, reads=[wqt, hbt], writes=[ps])
                k.act(Vh[:, b * 4 + tt_, :, 0:64], ps.ap[:, 0:128].rearrange("p (a d) -> p a d", a=2), AF.Copy,
                      reads=[ps], writes=[Vt[b]])
                k.act(Vh[:, b * 4 + tt_, :, 128:192], ps.ap[:, 0:128].rearrange("p (a d) -> p a d", a=2), AF.Copy,
                      reads=[ps], writes=[Vt[b]])
                if b == 0:
                    k.copy(voh[:, tt_, :], ps.ap[:, 0:128], reads=[ps], writes=[vot])
            if b == 0 and 'D' in SUB:
                k.dma("sp", ko, koh[:, :, :], reads=[kot], tile=kot)
                k.dma("sp", vo.rearrange("(t p) d -> p t d", p=128), voh[:, :, :], reads=[vot], tile=vot)
        k.barrier()
        if stage >= 2:
            k.memset(gluh[:, :, :], 0.0, writes=[glupad] + glut)
            off[0] = OFF_ATT
            wu = alloc("wu", [128, 8, 1024], BF16)
            wut = Tl("wu", wu[:, :, :])
            k.dma("pool", wu[:, :, :], w_u.rearrange("(c p) n -> p c n", p=128), writes=[wut], tile=wut)
            sgh = alloc("sg", [128, 2, 512], F32)
            sgt = [Tl("sg%d" % i, sgh[:, i, :]) for i in range(2)]
            assert off[0] <= OFF_ATT + 20480, off[0]
            off[0] = OFF_T
            hbs = [alloc("hb%d" % i, [128, 8, 512], BF16) for i in range(2)]
            hbts = [Tl("hb%d" % i, hbs[i][:, :, :]) for i in range(2)]
            sq_h, sq_t, rs_t, tmp_h, tmp_t = norm_temps()
            def _norm2(b):
                rms_stats(b, rs_t, sq_h, sq_t, sq_dve=True)
                modulate(b, 0, 0, rs_t, tmp_h, tmp_t, hbs[b % 2], hbts[b % 2])
            _norm2(0)
            for b in range(NB):
                hbh, hbt = hbs[b % 2], hbts[b % 2]
                if b + 1 < NB:
                    _norm2(b + 1)
                for c in range(4):
                    psa = k.psum()
                    psg = k.psum()
                    for kc in range(8):
                        k.mm(psa.ap, wu[:, kc, c * 128:(c + 1) * 128], hbh[:, kc, :], kc == 0, kc == 7,
                             reads=[wut, hbt], writes=[psa])
                    for kc in range(8):
                        k.mm(psg.ap, wu[:, kc, 512 + c * 128:512 + (c + 1) * 128], hbh[:, kc, :], kc == 0, kc == 7,
                             reads=[wut, hbt], writes=[psg])
                    k.act(sgh[:, c % 2, :], psg.ap, AF.Sigmoid, reads=[psg], writes=[sgt[c % 2]])
                    if b == 0:
                        for s_ in range(2):
                            g0 = gcol(s_ * 256)
                            k.tt(gluh[:, c, g0:g0 + 256], psa.ap[:, s_ * 256:(s_ + 1) * 256],
                                 sgh[:, c % 2, s_ * 256:(s_ + 1) * 256], ALU.mult,
                                 reads=[psa, sgt[c % 2]], writes=[glut[b]])
                    else:
                        g0 = gcol(b * 512)
                        k.tt(gluh[:, c, g0:g0 + 512], psa.ap, sgh[:, c % 2, :], ALU.mult,
                             reads=[psa, sgt[c % 2]], writes=[glut[b]])
            k.barrier()
        if stage >= 3:
            off[0] = OFF_T
            ckh = alloc("ckT", [128, 4, 256], BF16)
            ckt = Tl("ckT", ckh[:, :, :])
            k.dma("pool", ckh[:, :, :], ckT, writes=[ckt], tile=ckt)
            cVh = alloc("cV", [128, 2, 2, 192], BF16)
            cVt = Tl("cV", cVh[:, :, :, :])
            k.memset(cVh[:, :, :, 64:128], 1.0, writes=[cVt])
            for t_ in range(2):
                for a_ in range(2):
                    for c0 in (0, 128):
                        k.dma("pool", cVh[:, t_, a_, c0:c0 + 64], cV[t_ * 128:(t_ + 1) * 128, a_ * 64:(a_ + 1) * 64],
                              writes=[cVt], tile=cVt)
            pth = alloc("pT", [128, 3, 512], BF16)
            ptt = [Tl("pT%d" % i, pth[:, i, :]) for i in range(3)]
            dnh = alloc("dn", [128, 2, 512], F32)
            dnt = [Tl("dn%d" % i, dnh[:, i, :]) for i in range(2)]
            k.memset(dnh[:, :, :], 1.0, writes=dnt)
            wmx = alloc("wm0x", [128, 8, 1024], BF16)
            wmxt = Tl("wm0x", wmx[:, :, :])

            def adaln0_dma(j):
                k.dma("pool", wmx[:, :, :],
                      w_mod[0, :, j * 1024:(j + 1) * 1024].rearrange("(c p) n -> p c n", p=128),
                      writes=[wmxt], tile=wmxt)

            def adaln0_compute(j):
                pa_ = k.psum()
                for fc in range(8):
                    for kc in range(8):
                        k.mm(pa_.ap[:, fc * 2:fc * 2 + 2], wmx[:, kc, fc * 128:(fc + 1) * 128],
                             scb_h[:, kc, :], kc == 0, kc == 7, reads=[wmxt, scbt], writes=[pa_])
                adaln_evac(0, j, pa_, 0)
                if j == 4:
                    adaln_AB(0, 1)
                if j < 5:
                    adaln0_dma(j + 1)
            adaln0_dma(2)
            pi = [0]
            units = []

            def attend(qcols, g, keylist, b_out):
                units.append(dict(qcols=qcols, g=g, kl=keylist, b_out=b_out))

            def S_stage(u, ki):
                qcols, g = u["qcols"], u["g"]
                kind, idx, mk = u["kl"][ki]
                ps = k.psum()
                for hl in range(4):
                    h = g * 4 + hl
                    if kind == "tok":
                        lhs = kTh[:, g * 2 + h % 2, idx * 128:(idx + 1) * 128]
                        rd = [kTt[idx // 4]]
                    else:
                        lhs = ckh[:, g * 2 + h % 2, idx * 128:(idx + 1) * 128]
                        rd = [ckt]
                    k.mm(ps.ap[:, hl * 128:(hl + 1) * 128], lhs, qTh[:, h // 2, qcols], True, True,
                         reads=rd + [qTt[qcols.start // 512]], writes=[ps])
                return ps

            def XP_stage(u, ki, ps):
                g = u["g"]
                kind, idx, mk = u["kl"][ki]
                nk = len(u["kl"])
                po = u["po"]
                i = pi[0] % 3
                pi[0] += 1
                k.act(pth[:, i, :], ps.ap, AF.Exp, reads=[ps], writes=[ptt[i]], scale=0.125)
                if mk is not None:
                    k.tt(pth[:, i, :].rearrange("p (h q) -> p h q", h=4),
                         pth[:, i, :].rearrange("p (h q) -> p h q", h=4),
                         cmh[:, mk, :].unsqueeze(1).to_broadcast([128, 4, 128]), ALU.mult,
                         reads=[ptt[i], cmt], writes=[ptt[i]])
                for hl in range(4):
                    vo_ = 0 if hl % 2 == 0 else 64
                    if kind == "tok":
                        lhs = Vh[:, idx, g, vo_:vo_ + 128]
                        rd = [Vt[idx // 4]]
                    else:
                        lhs = cVh[:, idx, g, vo_:vo_ + 128]
                        rd = [cVt]
                    k.mm(po.ap[:, hl * 128:(hl + 1) * 128], lhs, pth[:, i, hl * 128:(hl + 1) * 128],
                         ki == 0 and hl == 0, ki == nk - 1 and hl == 3, reads=rd + [ptt[i]], writes=[po])

            def A1(u):
                u["po"] = k.psum_acc()
                nk = len(u["kl"])
                pss = [S_stage(u, 0)]
                if nk > 1:
                    pss.append(S_stage(u, 1))
                for ki in range(nk):
                    if ki + 2 < nk:
                        pss.append(S_stage(u, ki + 2))
                    XP_stage(u, ki, pss[ki])

            def A2(u, ui):
                qcols, g, b_out, po = u["qcols"], u["g"], u["b_out"], u["po"]
                di = ui % 2
                for hl in range(4):
                    h = g * 4 + hl
                    ro = (h % 2) * 64
                    rd_ = 64 - ro
                    cs = slice(hl * 128, (hl + 1) * 128)
                    k.ts(dnh[ro:ro + 64, di, cs], po.ap[rd_:rd_ + 64, cs], esh[rd_:rd_ + 64, h:h + 1], None, ALU.add,
                         reads=[po, est], writes=[dnt[di]])
                k.act(dnh[:, di, :], dnh[:, di, :], AF.Ln, reads=[dnt[di]], writes=[dnt[di]])
                k.act(dnh[:, di, :], dnh[:, di, :], AF.Exp, reads=[dnt[di]], writes=[dnt[di]], scale=-1.0)
                for hl in range(4):
                    h = g * 4 + hl
                    ro = (h % 2) * 64
                    cs = slice(hl * 128, (hl + 1) * 128)
                    k.tt(attTh[ro:ro + 64, h // 2, qcols], po.ap[ro:ro + 64, cs], dnh[ro:ro + 64, di, cs], ALU.mult,
                         reads=[po, dnt[di]], writes=[attt[b_out]])

            for s_ in range(2):
                for qb in range(2):
                    t0 = s_ * 256 + qb * 128
                    for g in range(2):
                        attend(slice(t0, t0 + 128), g, [("tok", s_ * 2, None), ("tok", s_ * 2 + 1, None)], 0)
            for qb in range(16):
                t0 = 512 + qb * 128
                kl = []
                if qb > 0:
                    kl.append(("tok", 4 + qb - 1, 2))
                kl.append(("tok", 4 + qb, None))
                if qb < 15:
                    kl.append(("tok", 4 + qb + 1, 3))
                kl += [("ctx", 0, None), ("ctx", 1, None)]
                for g in range(2):
                    attend(slice(t0, t0 + 128), g, kl, 1 + qb // 4)
            A1(units[0])
            for ui in range(len(units)):
                if ui + 1 < len(units):
                    A1(units[ui + 1])
                A2(units[ui], ui)
                if ui in (7, 15, 23, 31):
                    adaln0_compute(2 + (ui - 7) // 8)
            k.barrier()
        if stage >= 4:
            off[0] = OFF_CV
            dgh = alloc("diag", [128, 4, 31, 128], BF16)
            assert off[0] <= OFF_ATT, off[0]
            off[0] = OFF_T
            dgt = Tl("diag", dgh[:, :, :, :])
            o_w = PV["conv_w"][0]
            for c in range(4):
                for kk in range(31):
                    k.ts(dgh[:, c, kk, :], cmh[:, 1, :], pvh[:, o_w + c * 31 + kk:o_w + c * 31 + kk + 1], None, ALU.mult,
                         reads=[cmt, pvt], writes=[dgt])
            zs = [alloc("z%d" % i, [128, 4, 512], F32) for i in range(2)]
            zts = [Tl("z%d" % i, zs[i][:, :, :]) for i in range(2)]
            zbh = alloc("zb", [128, 4, 512], BF16)
            zbt = Tl("zb", zbh[:, :, :])
            zqh = alloc("zq", [128, 4, 512], BF16)
            zqt = Tl("zq", zqh[:, :, :])
            assert off[0] <= TOP
            off[0] = OFF_CV + 31744
            muh = alloc("mu", [128, 512], F32)
            mut = Tl("mu", muh[:, :])
            vrh = alloc("vr", [128, 512], F32)
            vrt = Tl("vr", vrh[:, :])
            assert off[0] <= OFF_ATT, off[0]
            ob_ = PV["conv_b"][0]
            og_ = PV["cng"][0]
            obb = PV["cnb"][0]

            def convmm(b):
                zh, zt = zs[b % 2], zts[b % 2]
                subs = [(0, 256), (256, 256)] if b == 0 else [(0, 512)]
                for c in range(4):
                    ps = k.psum()
                    for (s0, n) in subs:
                        g0 = gcol(b * 512 + s0) - 15
                        for kk in range(31):
                            k.mm(ps.ap[:, s0:s0 + n], dgh[:, c, kk, :], gluh[:, c, g0 + kk:g0 + kk + n],
                                 kk == 0, kk == 30, reads=[dgt, glupad] + glut, writes=[ps])
                    k.act(zh[:, c, :], ps.ap, AF.Identity, reads=[ps, pvt], writes=[zt],
                          bias=pvh[:, ob_ + c:ob_ + c + 1], scale=1.0)

            def convln(b):
                zh, zt = zs[b % 2], zts[b % 2]
                k.copy(zbh[:, :, :], zh[:, :, :], reads=[zt], writes=[zbt])
                k.tt(zqh[:, :, :], zh[:, :, :], zh[:, :, :], ALU.mult, reads=[zt], writes=[zqt])
                pm = k.psum()
                pq = k.psum()
                for c in range(4):
                    k.mm(pm.ap, onesh[:, 1, :], zbh[:, c, :], c == 0, c == 3, reads=[onest, zbt], writes=[pm])
                for c in range(4):
                    k.mm(pq.ap, onesh[:, 1, :], zqh[:, c, :], c == 0, c == 3, reads=[onest, zqt], writes=[pq])
                k.copy(muh[:, :], pm.ap, reads=[pm], writes=[mut])
                k.tt(vrh[:, :], muh[:, :], muh[:, :], ALU.mult, reads=[mut], writes=[vrt])
                k.tt(vrh[:, :], pq.ap, vrh[:, :], ALU.subtract, reads=[pq, vrt], writes=[vrt])
                k.act(vrh[:, :], vrh[:, :], AF.Ln, reads=[vrt, pvt], writes=[vrt], bias=pvs("eps"), scale=1.0)
                k.act(vrh[:, :], vrh[:, :], AF.Exp, reads=[vrt], writes=[vrt], scale=-0.5)
                for c in range(4):
                    k.tt(zh[:, c, :], zh[:, c, :], muh[:, :], ALU.subtract, reads=[zt, mut], writes=[zt])
                    k.tt(zh[:, c, :], zh[:, c, :], vrh[:, :], ALU.mult, reads=[zt, vrt], writes=[zt])
                    k.act(cvTh[:, c, b * 512:(b + 1) * 512], zh[:, c, :], AF.Silu, reads=[zt, pvt], writes=[cvt[b]],
                          scale=pvh[:, og_ + c:og_ + c + 1], bias=pvh[:, obb + c:obb + c + 1])

            convmm(0)
            for b in range(NB):
                if b + 1 < NB:
                    convmm(b + 1)
                convln(b)
            k.barrier()
        if stage >= 5:
            off[0] = OFF_T
            wo = alloc("wo", [128, 8, 1024], BF16)
            wot = Tl("wo", wo[:, :, :])
            k.dma("pool", wo[:, :, :], w_ao.rearrange("(c p) n -> p c n", p=128), writes=[wot], tile=wot)
            for b in range(NB):
                r = 0 if b == 0 else 1
                cols = slice(b * 512, (b + 1) * 512)
                for oc in range(8):
                    ps = k.psum()
                    for kc in range(8):
                        rhs = attTh[:, kc, cols] if kc < 4 else cvTh[:, kc - 4, cols]
                        k.mm(ps.ap, wo[:, kc, oc * 128:(oc + 1) * 128], rhs, kc == 0, kc == 7,
                             reads=[wot, attt[b], cvt[b]], writes=[ps])
                    k.stt(Yt[oc][b].ap, ps.ap, modp(0, 2, oc, r), Yt[oc][b].ap, ALU.mult, ALU.add,
                          reads=[ps, modt, Yt[oc][b]], writes=[Yt[oc][b]])
            k.barrier()
        if stage >= 6:
            ffn(0)

        if stage >= 7:
            off[0] = PH
            RW = 259 + 259 + 2051
            rseg = [(1, 256, 0), (260, 256, 256), (519, 2048, 512)]
            rech = alloc("rec", [128, 8, RW], BF16)
            rect = Tl("rec", rech[:, :, :])
            ggh = alloc("gg", [128, 8, NT], BF16)
            ggt = [Tl("gg%d" % c, ggh[:, c, :]) for c in range(8)]
            PH3 = off[0]
            k.memset(rech[:, :, :], 0.0, writes=[rect])

            def rcol(t):
                if t < 256:
                    return 1 + t
                if t < 512:
                    return 260 + (t - 256)
                return 519 + (t - 512)
            for half in range(2):
                off[0] = PH3
                wl = alloc("wl", [128, 8, 1024], BF16)
                wlt = Tl("wl", wl[:, :, :])
                k.dma("pool", wl[:, :, :], lru_in[:, half * 1024:(half + 1) * 1024].rearrange("(c p) n -> p c n", p=128),
                      writes=[wlt], tile=wlt)
                hbs = [alloc("hb%d" % i, [128, 8, 512], BF16) for i in range(2)]
                hbts = [Tl("hb%d" % i, hbs[i][:, :, :]) for i in range(2)]
                sq_h, sq_t, rs_t, tmp_h, tmp_t = norm_temps()
                def _norm3(b, hbs=hbs, hbts=hbts, rs_t=rs_t, sq_h=sq_h, sq_t=sq_t, tmp_h=tmp_h, tmp_t=tmp_t):
                    rms_stats(b, rs_t, sq_h, sq_t, sq_dve=True)
                    modulate(b, 1, 0, rs_t, tmp_h, tmp_t, hbs[b % 2], hbts[b % 2])
                _norm3(0)
                for b in range(NB):
                    hbh, hbt = hbs[b % 2], hbts[b % 2]
                    if b + 1 < NB:
                        _norm3(b + 1)
                    for oc in range(8):
                        ps = k.psum()
                        for kc in range(8):
                            k.mm(ps.ap, wl[:, kc, oc * 128:(oc + 1) * 128], hbh[:, kc, :], kc == 0, kc == 7,
                                 reads=[wlt, hbt], writes=[ps])
                        if half == 0:
                            k.act(ggh[:, oc, b * 512:(b + 1) * 512], ps.ap, AF.Gelu_apprx_tanh, reads=[ps], writes=[ggt[oc]])
                        elif b == 0:
                            for s_ in range(2):
                                r0 = rcol(s_ * 256)
                                k.act(rech[:, oc, r0:r0 + 256], ps.ap[:, s_ * 256:(s_ + 1) * 256], AF.Copy,
                                      reads=[ps], writes=[rect])
                        else:
                            r0 = rcol(b * 512)
                            k.act(rech[:, oc, r0:r0 + 512], ps.ap, AF.Copy, reads=[ps], writes=[rect])
                k.barrier()
            off[0] = PH3
            wgh = alloc("wg", [128, 32, 128], BF16)
            wgt = Tl("wg", wgh[:, :, :])
            k.dma("pool", wgh[:, :, :], lru_g, writes=[wgt], tile=wgt)
            sch = alloc("sc", [128, 2, 16], F32)
            sct = Tl("sc", sch[:, :, :])
            o_l = PV["lam"][0]
            k.act(sch[:, 0, :], pvh[:, o_l:o_l + 16], AF.Exp, reads=[pvt], writes=[sct], scale=-1.0)
            k.act(sch[:, 0, :], sch[:, 0, :], AF.Ln, reads=[sct, pvt], writes=[sct], bias=pvs("one"), scale=1.0)
            k.ts(sch[:, 1, :], sch[:, 0, :], -4.0, None, ALU.mult, reads=[sct], writes=[sct])
            k.ts(sch[:, 0, :], sch[:, 0, :], -8.0, None, ALU.mult, reads=[sct], writes=[sct])
            hbh_ = alloc("hbias", [128, 48], F32)
            hbit = Tl("hbias", hbh_[:, :])
            k.ts(hbh_[:, 0:16], pvh[:, PV["lba"][0]:PV["lba"][0] + 16], 0.5, None, ALU.mult, reads=[pvt], writes=[hbit])
            k.ts(hbh_[:, 16:32], pvh[:, PV["lbx"][0]:PV["lbx"][0] + 16], 0.5, None, ALU.mult, reads=[pvt], writes=[hbit])
            k.ts(hbh_[:, 32:40], pvh[:, PV["lcb"][0]:PV["lcb"][0] + 8], 0.5, None, ALU.mult, reads=[pvt], writes=[hbit])
            k.memset(hbh_[:, 40:48], 0.5, writes=[hbit])
            dgl = alloc("dgl", [128, 4, 128], BF16)
            dglt = Tl("dgl", dgl[:, :, :])
            SW = 512
            XS = []
            for i in range(3):
                d_ = {}
                for nm, dt_ in (("xc", F32), ("xcb", BF16)):
                    h_ = alloc("%s%d" % (nm, i), [128, SW], dt_)
                    d_[nm] = h_
                    d_[nm + "_t"] = Tl("%s%d" % (nm, i), h_[:, :])
                XS.append(d_)
            sets = []
            for i in range(2):
                d_ = {}
                for nm, dt_ in (("A", F32), ("I", F32), ("T", F32), ("hbk", F32)):
                    h_ = alloc("%s%d" % (nm, i), [128, SW], dt_)
                    d_[nm] = h_
                    d_[nm + "_t"] = Tl("%s%d" % (nm, i), h_[:, :])
                sets.append(d_)
            Hfh = alloc("Hf", [128, 2048], F32)
            Hft = Tl("Hf", Hfh[:, :])
            carh = alloc("car", [128, 2], F32)
            cart = Tl("car", carh[:, :])
            soh = alloc("so", [128, 8, 2, 2], F32)
            sot = Tl("so", soh[:, :, :, :])
            o_cw = PV["lcw"][0]
            o_cb = PV["lcb"][0]
            o_s0 = PV["st0"][0]

            def rev(ap_h, col_last, n):
                a_ = ap_h[:, col_last:col_last + 1]
                return bass.AP(a_.tensor, a_.offset, [list(a_.ap[0]), [-1, n]])

            steps = []
            for c in range(8):
                steps.append(dict(c=c, d=0, si=-1, sub=0, nsub=1, sl=512, rs0=1, tok0=0, slen=512))
                steps.append(dict(c=c, d=1, si=-1, sub=0, nsub=1, sl=512, rs0=1, tok0=0, slen=512))
                for si, (rs0, slen, tok0) in enumerate(rseg):
                    if si < 2:
                        continue
                    sl = min(slen, SW)
                    nsub = slen // sl
                    for sub in range(nsub):
                        steps.append(dict(c=c, d=0, si=si, sub=sub, nsub=nsub, sl=sl, rs0=rs0, tok0=tok0, slen=slen))
                    for sub in reversed(range(nsub)):
                        steps.append(dict(c=c, d=1, si=si, sub=sub, nsub=nsub, sl=sl, rs0=rs0, tok0=tok0, slen=slen))
            cur_c = [-1]

            def F(kk):
                st = steps[kk]
                c, d, n = st["c"], st["d"], st["sl"]
                r0 = st["rs0"] + st["sub"] * n
                if c != cur_c[0]:
                    cur_c[0] = c
                    for j in range(4):
                        k.ts(dgl[:, j, :], cmh[:, 1, :], pvh[:, o_cw + c * 4 + j:o_cw + c * 4 + j + 1], None, ALU.mult,
                             reads=[cmt, pvt], writes=[dglt])
                X = XS[kk % 3]
                pc = k.psum()
                for j in range(4):
                    if st["si"] == -1:
                        a_ = rech[:, c, r0 - 1 + j:r0 + j]
                        rhs_ = bass.AP(a_.tensor, a_.offset, [list(a_.ap[0]), [259, 2], [1, 256]])
                        out_ = pc.ap[:, 0:512].rearrange("p (s t) -> p s t", s=2)
                    else:
                        rhs_ = rech[:, c, r0 - 1 + j:r0 - 1 + j + n]
                        out_ = pc.ap[:, 0:n]
                    k.mm(out_, dgl[:, j, :], rhs_, j == 0, j == 3, reads=[dglt, rect], writes=[pc])
                k.ts(X["xc"][:, 0:n], pc.ap[:, 0:n], hbh_[:, 40:41], hbh_[:, 32 + c:33 + c], ALU.mult, ALU.add,
                     reads=[pc, hbit], writes=[X["xc_t"]])
                k.ts(X["xcb"][:, 0:n], pc.ap[:, 0:n], pvh[:, o_cb + c:o_cb + c + 1], None, ALU.add,
                     reads=[pc, pvt], writes=[X["xcb_t"]])
                pr = k.psum()
                pi_ = k.psum()
                k.mm(pr.ap[:, 0:n], wgh[:, (0 * 2 + d) * 8 + c, :], X["xcb"][:, 0:n], True, True,
                     reads=[wgt, X["xcb_t"]], writes=[pr])
                k.mm(pi_.ap[:, 0:n], wgh[:, (1 * 2 + d) * 8 + c, :], X["xcb"][:, 0:n], True, True,
                     reads=[wgt, X["xcb_t"]], writes=[pi_])
                st["pr"], st["pi"] = pr, pi_

            def M(kk):
                st = steps[kk]
                c, d, n = st["c"], st["d"], st["sl"]
                S = sets[kk % 2]
                pr, pi_ = st["pr"], st["pi"]
                dc = d * 8 + c
                k.act(S["A"][:, 0:n], pr.ap[:, 0:n], AF.Tanh, reads=[pr, hbit], writes=[S["A_t"]],
                      bias=hbh_[:, dc:dc + 1], scale=0.5)
                k.act(S["I"][:, 0:n], pi_.ap[:, 0:n], AF.Tanh, reads=[pi_, hbit], writes=[S["I_t"]],
                      bias=hbh_[:, 16 + dc:17 + dc], scale=0.5)
                k.act(S["T"][:, 0:n], S["A"][:, 0:n], AF.Exp, reads=[S["A_t"], sct], writes=[S["T_t"]],
                      scale=sch[:, 0, dc:dc + 1], bias=sch[:, 0, dc:dc + 1])
                k.act(S["A"][:, 0:n], S["A"][:, 0:n], AF.Exp, reads=[S["A_t"], sct], writes=[S["A_t"]],
                      scale=sch[:, 1, dc:dc + 1], bias=sch[:, 1, dc:dc + 1])

            def M2(kk):
                st = steps[kk]
                n = st["sl"]
                S = sets[kk % 2]
                k.act(S["T"][:, 0:n], S["T"][:, 0:n], AF.Sqrt, reads=[S["T_t"], pvt], writes=[S["T_t"]],
                      bias=pvs("one"), scale=-1.0)

            def scan(out_ap, a_ap, b_ap, init, reads, writes):
                k.add("dve", lambda e: e.tensor_tensor_scan(out_ap, a_ap, b_ap, init, ALU.mult, ALU.add),
                      reads=reads, writes=writes)

            def B(kk):
                st = steps[kk]
                c, d, n, si, sub, nsub = st["c"], st["d"], st["sl"], st["si"], st["sub"], st["nsub"]
                S = sets[kk % 2]
                X = XS[kk % 3]
                k.stt(S["I"][:, 0:n], S["I"][:, 0:n], 1.0, X["xc"][:, 0:n], ALU.add, ALU.mult,
                      reads=[S["I_t"], X["xc_t"]], writes=[S["I_t"]])
                k.tt(S["T"][:, 0:n], S["T"][:, 0:n], S["I"][:, 0:n], ALU.mult, reads=[S["T_t"], S["I_t"]], writes=[S["T_t"]])
                if si == -1 and d == 0:
                    for q_ in range(2):
                        cs_ = slice(q_ * 256, (q_ + 1) * 256)
                        scan(Hfh[:, cs_], S["A"][:, cs_], S["T"][:, cs_], 0.0, [S["A_t"], S["T_t"], Hft], [Hft])
                        k.copy(soh[:, c, q_, 0:1], Hfh[:, q_ * 256 + 255:q_ * 256 + 256], reads=[Hft], writes=[sot])
                elif si == -1:
                    for q_ in range(2):
                        lo = q_ * 256
                        scan(rev(S["hbk"], lo + 255, 256), rev(S["A"], lo + 255, 256), rev(S["T"], lo + 255, 256), 0.0,
                             [S["A_t"], S["T_t"]], [S["hbk_t"]])
                        k.copy(soh[:, c, q_, 1:2], S["hbk"][:, lo:lo + 1], reads=[S["hbk_t"]], writes=[sot])
                    k.tt(S["hbk"][:, 0:512], S["hbk"][:, 0:512], Hfh[:, 0:512], ALU.add,
                         reads=[S["hbk_t"], Hft], writes=[S["hbk_t"]])
                    k.tt(ggh[:, c, 0:512], S["hbk"][:, 0:512], ggh[:, c, 0:512], ALU.mult,
                         reads=[S["hbk_t"], ggt[c]], writes=[ggt[c]])
                elif d == 0:
                    if sub == 0:
                        init = pvh[:, o_s0 + c * 2:o_s0 + c * 2 + 1] if si == 2 else 0.0
                    else:
                        init = Hfh[:, sub * n - 1:sub * n]
                    scan(Hfh[:, sub * n:(sub + 1) * n], S["A"][:, 0:n], S["T"][:, 0:n], init,
                         [S["A_t"], S["T_t"], pvt, Hft], [Hft])
                    if si < 2 and sub == nsub - 1:
                        k.copy(soh[:, c, si, 0:1], Hfh[:, st["slen"] - 1:st["slen"]], reads=[Hft], writes=[sot])
                else:
                    if sub == nsub - 1:
                        init = pvh[:, o_s0 + c * 2 + 1:o_s0 + c * 2 + 2] if si == 2 else 0.0
                    else:
                        init = carh[:, 0:1]
                    scan(rev(S["hbk"], n - 1, n), rev(S["A"], n - 1, n), rev(S["T"], n - 1, n), init,
                         [S["A_t"], S["T_t"], pvt, cart], [S["hbk_t"]])
                    k.copy(carh[:, 0:1], S["hbk"][:, 0:1], reads=[S["hbk_t"]], writes=[cart])
                    if si < 2:
                        k.copy(soh[:, c, si, 1:2], S["hbk"][:, 0:1], reads=[S["hbk_t"]], writes=[sot])
                    k.tt(S["hbk"][:, 0:n], S["hbk"][:, 0:n], Hfh[:, sub * n:(sub + 1) * n], ALU.add,
                         reads=[S["hbk_t"], Hft], writes=[S["hbk_t"]])
                    t0 = st["tok0"] + sub * n
                    k.tt(ggh[:, c, t0:t0 + n], S["hbk"][:, 0:n], ggh[:, c, t0:t0 + n], ALU.mult,
                         reads=[S["hbk_t"], ggt[c]], writes=[ggt[c]])

            NS = len(steps)
            F(0)
            for kk in range(0, NS, 2):
                if kk + 1 < NS:
                    F(kk + 1)
                M(kk)
                if kk + 2 < NS:
                    F(kk + 2)
                if kk + 1 < NS:
                    M(kk + 1)
                M2(kk)
                if kk + 1 < NS:
                    M2(kk + 1)
                B(kk)
                if kk + 1 < NS:
                    B(kk + 1)
            k.dma("sp", so, soh[:, :, :, :].rearrange("p a b c -> p (a b c)"), reads=[sot], tile=sot)
            k.barrier()
            off[0] = PH3
            wo = alloc("wlo", [128, 8, 1024], BF16)
            wot = Tl("wlo", wo[:, :, :])
            k.dma("pool", wo[:, :, :], lru_out.rearrange("(c p) n -> p c n", p=128), writes=[wot], tile=wot)
            for b in range(NB):
                r = 0 if b == 0 else 1
                for oc in range(8):
                    ps = k.psum()
                    for kc in range(8):
                        k.mm(ps.ap, wo[:, kc, oc * 128:(oc + 1) * 128], ggh[:, kc, b * 512:(b + 1) * 512],
                             kc == 0, kc == 7, reads=[wot, ggt[kc]], writes=[ps])
                    k.stt(Yt[oc][b].ap, ps.ap, modp(1, 2, oc, r), Yt[oc][b].ap, ALU.mult, ALU.add,
                          reads=[ps, modt, Yt[oc][b]], writes=[Yt[oc][b]])
            k.barrier()
        if stage >= 8:
            ffn(1)
        off[0] = PH
        sq_h, sq_t, rs_t, tmp_h, tmp_t = norm_temps()
        o_f = PV["final"][0]
        for b in range(NB):
            if stage >= 9:
                rms_stats(b, rs_t, sq_h, sq_t)
            for c in range(8):
                if stage >= 9:
                    k.stt(Yt[c][b].ap, Yt[c][b].ap, pvh[:, o_f + c:o_f + c + 1], rs_t.ap, ALU.mult, ALU.mult,
                          reads=[Yt[c][b], pvt, rs_t], writes=[Yt[c][b]])
                k.dma("sp", yT[c * 128:(c + 1) * 128, b * 512:(b + 1) * 512], Yt[c][b].ap,
                      reads=[Yt[c][b]], tile=Yt[c][b])
        k.finalize()
    return nc


def _pp(v, nch):
    return np.ascontiguousarray(np.asarray(v, np.float32).reshape(nch, 128).T)


def _rope_tables():
    t = np.arange(2048)
    row = (t // 64).astype(np.float32)
    col = (t % 64).astype(np.float32)
    inv = (10000.0 ** (-np.arange(16, dtype=np.float32) / 16)).astype(np.float32)
    cos = np.zeros((128, 2048), np.float32)
    sin = np.zeros((128, 2048), np.float32)
    for p in range(128):
        d = p % 64
        pos = row if d < 32 else col
        ang = (pos * inv[d % 16]).astype(np.float32)
        cos[p] = np.cos(ang)
        sin[p] = np.sin(ang)
    return np.stack([cos, sin], axis=1)


def _const_mats():
    rot = np.zeros((128, 128), np.float32)
    for fp in range(128):
        d = fp % 32
        if d < 16:
            rot[fp + 16, fp] = -1.0
        else:
            rot[fp - 16, fp] = 1.0
    ident = np.eye(128, dtype=np.float32)
    p = np.arange(128)[:, None]
    c = np.arange(128)[None, :]
    mprev = (c <= p).astype(np.float32)
    mnext = (p <= c).astype(np.float32)
    return np.stack([rot, ident, mprev, mnext], axis=1)


_NC_CACHE = {}


def kernel(x_prompt, x_sample, c, cache_k, cache_v, state_lru, c_ctx,
           w_mod, b_mod, norm_mix, norm_ffn, w_ff1, w_ff2,
           att_in, att_out, att_sink, conv_w, conv_b, conv_norm_g, conv_norm_b,
           lru_in, lru_out, lru_conv_w, lru_conv_b, lru_wa, lru_ba, lru_wx, lru_bx, lru_lam,
           final_norm, _stage=None):
    stage = STAGE if _stage is None else _stage
    f = lambda a: np.asarray(a, np.float32)
    x_prompt, x_sample, c, cache_k, cache_v, state_lru, c_ctx = map(f, (x_prompt, x_sample, c, cache_k, cache_v, state_lru, c_ctx))
    att_in = f(att_in)[0]
    zc = np.zeros((1024, 64), np.float32)
    k0, k1 = att_in[:, 512:576], att_in[:, 576:640]
    w_qkv = np.ascontiguousarray(np.concatenate(
        [att_in[:, 0:512], k0, zc, zc, k0, k1, zc, zc, k1, att_in[:, 640:768]], axis=1))
    w_u = np.ascontiguousarray(att_in[:, 768:1792])
    lru_g = np.ascontiguousarray(np.stack([f(lru_wa)[0], f(lru_wx)[0]], 0).reshape(32, 128, 128).transpose(1, 0, 2))
    tabs = _rope_tables()
    cm = _const_mats()
    shared = {
        "w_mod": f(w_mod), "w_qkv": w_qkv, "w_u": w_u, "w_ao": f(att_out)[0], "w_ff1": f(w_ff1), "w_ff2": f(w_ff2),
        "lru_in": f(lru_in)[0], "lru_out": f(lru_out)[0], "lru_g": lru_g, "tabs": tabs, "cmats": cm,
    }
    pv_common = {
        "b_mod": np.concatenate([_pp(f(b_mod)[l], 48) for l in range(2)], 1),
        "norm_mix": np.concatenate([_pp(f(norm_mix)[l], 8) for l in range(2)], 1),
        "norm_ffn": np.concatenate([_pp(f(norm_ffn)[l], 8) for l in range(2)], 1),
        "final": _pp(f(final_norm), 8),
        "conv_w": np.ascontiguousarray(f(conv_w)[0].reshape(31, 4, 128).transpose(2, 1, 0)).reshape(128, 124),
        "conv_b": _pp(f(conv_b)[0], 4), "cng": _pp(f(conv_norm_g)[0], 4), "cnb": _pp(f(conv_norm_b)[0], 4),
        "lcw": np.ascontiguousarray(f(lru_conv_w)[0].reshape(4, 8, 128).transpose(2, 1, 0)).reshape(128, 32),
        "lcb": _pp(f(lru_conv_b)[0], 8),
        "lba": np.concatenate([_pp(f(lru_ba)[0, d], 8) for d in range(2)], 1),
        "lbx": np.concatenate([_pp(f(lru_bx)[0, d], 8) for d in range(2)], 1),
        "lam": np.concatenate([_pp(f(lru_lam)[0, d], 8) for d in range(2)], 1),
        "sink": np.broadcast_to(f(att_sink)[0][None, :], (128, 8)),
        "eps": np.full((128, 1), EPS, np.float32), "one": np.ones((128, 1), np.float32),
    }
    in_maps = []
    for core in range(8):
        s = core // 2
        xcat = np.concatenate([x_prompt[2 * core], x_prompt[2 * core + 1], x_sample[s]], axis=0)
        pv = np.zeros((128, NPV), np.float32)
        for name, val in pv_common.items():
            o, w = PV[name]
            pv[:, o:o + w] = val
        o, w = PV["st0"]
        st = state_lru[s, 0]
        pv[:, o:o + w] = np.stack([_pp(st[0], 8), _pp(st[1], 8)], axis=2).reshape(128, 16)
        o, w = PV["cond"]
        pv[:, o:o + w] = np.stack([_pp(c_ctx, 8), _pp(c[s], 8)], axis=2).reshape(128, 16)
        ck = cache_k[s, 0]
        zk = np.zeros((64, 256), np.float32)
        ckT = np.stack([np.concatenate([ck[:, 0, :].T, zk], 0), np.concatenate([zk, ck[:, 0, :].T], 0),
                        np.concatenate([ck[:, 1, :].T, zk], 0), np.concatenate([zk, ck[:, 1, :].T], 0)], axis=1)
        m = dict(shared)
        m.update({"xT": np.ascontiguousarray(xcat.T), "pv": pv, "ckT": np.ascontiguousarray(ckT),
                  "cV": np.ascontiguousarray(cache_v[s, 0].reshape(256, 128))})
        in_maps.append(m)
    if stage not in _NC_CACHE:
        _NC_CACHE[stage] = build_program(stage)
    nc = _NC_CACHE[stage]
    res = run_bass_kernel_spmd(nc, in_maps, core_ids=list(range(8)))
    y_prompt = np.zeros((16, 256, 1024), np.float32)
    y_sample = np.zeros((4, 2048, 1024), np.float32)
    nk = np.zeros((16, 1, 256, 2, 64), np.float32)
    nv = np.zeros((16, 1, 256, 2, 64), np.float32)
    nh = np.zeros((16, 1, 2, 1024), np.float32)
    for core in range(8):
        r = res.results[core]
        y = r["yT"].T
        y_prompt[2 * core] = y[0:256]
        y_prompt[2 * core + 1] = y[256:512]
        if core % 2 == 0:
            y_sample[core // 2] = y[512:]
        ko = r["ko"]
        vo = r["vo"]
        so = r["so"].reshape(128, 8, 2, 2)
        for j in range(2):
            for g in range(2):
                nk[2 * core + j, 0, :, g, :] = ko[0:64, g, j * 256:(j + 1) * 256].T
            nv[2 * core + j, 0] = vo[j * 256:(j + 1) * 256].reshape(256, 2, 64)
            for d in range(2):
                nh[2 * core + j, 0, d] = so[:, :, j, d].T.reshape(1024)
    return (y_prompt, y_sample, nk, nv, nh)
```

```python
import numpy as np
from contextlib import ExitStack
import concourse.bass as bass
import concourse.mybir as mybir
from concourse.bass_utils import run_bass_kernel_spmd

F32 = mybir.dt.float32
BF16 = mybir.dt.bfloat16
AF = mybir.ActivationFunctionType
ALU = mybir.AluOpType

NT = 2560
NB = 5
EPS = 1e-6
STAGE = 99
import os as _os
SUB = _os.environ.get('MK_SUB', 'ABCDEK')
ATT = _os.environ.get('MK_ATT', 'PSQXMVN')
DEBUG = {}


class Tl:
    __slots__ = ("name", "ap", "lw", "rd", "sem", "cnt", "excl")

    def __init__(self, name, ap):
        self.excl = False
        self.name = name
        self.ap = ap
        self.lw = None
        self.rd = []
        self.sem = None
        self.cnt = 0


class Op:
    __slots__ = ("eng", "fn", "needs", "sig", "val", "dma", "dsem", "dval")

    def __init__(self, eng, fn):
        self.eng = eng
        self.fn = fn
        self.needs = []
        self.sig = False
        self.val = 0
        self.dma = False
        self.dsem = None
        self.dval = 0


class KB:
    ENGS = ("pe", "act", "dve", "pool", "sp")

    def __init__(self, nc, es):
        self.nc = nc
        self.es = es
        self.q = {e: [] for e in self.ENGS}
        self.esem = {e: es.enter_context(nc.semaphore("s_" + e)) for e in self.ENGS}
        self.dma_tiles = []
        self.nsem = 0
        self.psum_tiles = []
        self.psum_i = 0
        self.acc_i = 0
        self.nalloc = 0

    def at(self, name, shape, dt, off):
        self.nalloc += 1
        h = self.nc.alloc_sbuf_tensor_at("%s_%d" % (name, self.nalloc), list(shape), dt, offset=off)
        return h

    def init_psum(self):
        for i in range(8):
            h = self.es.enter_context(self.nc.psum_tensor("ps%d" % i, [128, 512], F32))
            self.psum_tiles.append(Tl("ps%d" % i, h[:, :]))
            self.psum_tiles[-1].excl = True

    def psum(self):
        t = self.psum_tiles[self.psum_i % 6]
        self.psum_i += 1
        return t

    def psum_acc(self):
        t = self.psum_tiles[6 + self.acc_i % 2]
        self.acc_i += 1
        return t

    def add(self, eng, fn, reads=(), writes=(), dma_tile=None):
        op = Op(eng, fn)
        deps = {}
        for t in reads:
            if t.lw is not None:
                deps[id(t.lw)] = (t.lw, "raw")
            if t.excl:
                for r in t.rd:
                    if r.eng != eng and id(r) not in deps:
                        deps[id(r)] = (r, "rar")
        for t in writes:
            if t.lw is not None and id(t.lw) not in deps:
                deps[id(t.lw)] = (t.lw, "waw")
            for r in t.rd:
                if id(r) not in deps:
                    deps[id(r)] = (r, "war")
        for p, kind in deps.values():
            if p.dma or p.eng != eng or eng != "pe":
                op.needs.append(p)
        if dma_tile is not None:
            op.dma = True
            if dma_tile.sem is None:
                dma_tile.sem = self.es.enter_context(self.nc.semaphore("d%d" % self.nsem))
                self.nsem += 1
                self.dma_tiles.append(dma_tile)
            dma_tile.cnt += 16
            op.dsem = dma_tile.sem
            op.dval = dma_tile.cnt
        for t in reads:
            t.rd.append(op)
        for t in writes:
            t.lw = op
            t.rd = []
        self.q[eng].append(op)
        return op

    def barrier(self):
        lasts = [self.q[e][-1] for e in self.ENGS if self.q[e] and self.q[e][-1].dma is not None]
        pend = [(t.sem, t.cnt) for t in self.dma_tiles]
        for e in self.ENGS:
            op = Op(e, None)
            op.needs = [p for p in lasts if (not p.dma) and p.eng != e]
            op.val = list(pend)
            op.dma = None
            self.q[e].append(op)

    def finalize(self):
        nc = self.nc
        for e in self.ENGS:
            for i, op in enumerate(self.q[e]):
                op.dval = op.dval if op.dma else i
        for e in self.ENGS:
            for op in self.q[e]:
                last = {}
                keep = []
                for p in op.needs:
                    if p.dma:
                        keep.append(p)
                    elif p.eng not in last or last[p.eng].dval < p.dval:
                        last[p.eng] = p
                for p in last.values():
                    p.sig = True
                    keep.append(p)
                op.needs = keep
        for e in self.ENGS:
            c = 0
            for op in self.q[e]:
                if op.dma is False and op.sig:
                    c += 1
                    op.val = c
        final = [(t.sem, t.cnt) for t in self.dma_tiles]
        esem = self.esem
        q = self.q

        def run(ename, eng):
            known = {}

            def wait(s, v):
                if known.get(id(s), 0) >= v:
                    return
                eng.wait_ge(s, v)
                known[id(s)] = v

            for op in q[ename]:
                waits = {}
                for p in op.needs:
                    if p.dma:
                        s, v = p.dsem, p.dval
                    else:
                        s, v = esem[p.eng], p.val
                    if id(s) not in waits or waits[id(s)][1] < v:
                        waits[id(s)] = (s, v)
                for s, v in waits.values():
                    wait(s, v)
                if op.dma is None:
                    for s, v in op.val:
                        wait(s, v)
                    continue
                ins = op.fn(eng)
                if op.dma:
                    ins.then_inc(op.dsem, 16)
                elif op.sig:
                    ins.then_inc(esem[ename], 1)
            if ename == "sp":
                for s, v in final:
                    wait(s, v)

        with nc.Block() as block:
            @block.tensor
            def _(eng):
                run("pe", eng)

            @block.scalar
            def _(eng):
                run("act", eng)

            @block.vector
            def _(eng):
                run("dve", eng)

            @block.gpsimd
            def _(eng):
                run("pool", eng)

            @block.sync
            def _(eng):
                run("sp", eng)

    def dma(self, eng, out_ap, in_ap, reads=(), writes=(), tile=None, **kw):
        return self.add(eng, lambda e: e.dma_start(out=out_ap, in_=in_ap, **kw),
                        reads=reads, writes=writes, dma_tile=tile)

    def mm(self, out_ap, lhsT, rhs, start, stop, reads=(), writes=()):
        return self.add("pe", lambda e: e.matmul(out_ap, lhsT, rhs, start=start, stop=stop),
                        reads=reads, writes=writes)

    def act(self, out_ap, in_ap, func, reads=(), writes=(), bias=None, scale=None):
        kw = {}
        if bias is not None:
            kw["bias"] = bias
        if scale is not None:
            kw["scale"] = scale
        return self.add("act", lambda e: e.activation(out_ap, in_ap, func, **kw),
                        reads=reads, writes=writes)

    def tt(self, out_ap, in0, in1, op, reads=(), writes=(), eng="dve"):
        return self.add(eng, lambda e: e.tensor_tensor(out_ap, in0, in1, op),
                        reads=reads, writes=writes)

    def ts(self, out_ap, in0, s1, s2, op0, op1=None, reads=(), writes=(), eng="dve"):
        if op1 is None:
            return self.add(eng, lambda e: e.tensor_scalar(out_ap, in0, s1, None, op0),
                            reads=reads, writes=writes)
        return self.add(eng, lambda e: e.tensor_scalar(out_ap, in0, s1, s2, op0, op1),
                        reads=reads, writes=writes)

    def stt(self, out_ap, in0, scalar, in1, op0, op1, reads=(), writes=(), eng="dve"):
        return self.add(eng, lambda e: e.scalar_tensor_tensor(out_ap, in0, scalar, in1, op0, op1),
                        reads=reads, writes=writes)

    def copy(self, out_ap, in_ap, reads=(), writes=(), eng="dve"):
        return self.add(eng, lambda e: e.tensor_copy(out_ap, in_ap), reads=reads, writes=writes)

    def memset(self, out_ap, val, writes=(), eng="dve"):
        return self.add(eng, lambda e: e.memset(out_ap, val), writes=writes)

    def recip(self, out_ap, in_ap, reads=(), writes=()):
        return self.add("dve", lambda e: e.reciprocal(out_ap, in_ap), reads=reads, writes=writes)


PV = {}
_o = 0
for _n, _w in [("b_mod", 96), ("norm_mix", 16), ("norm_ffn", 16), ("final", 8), ("conv_w", 124),
               ("conv_b", 4), ("cng", 4), ("cnb", 4), ("lcw", 32), ("lcb", 8), ("lba", 16),
               ("lbx", 16), ("lam", 16), ("st0", 16), ("sink", 8), ("cond", 16), ("eps", 1),
               ("one", 1)]:
    PV[_n] = (_o, _w)
    _o += _w
NPV = _o

BASE = 16640
TOP = 229376


def build_program(stage=99):
    nc = bass.Bass("TRN2", target_bir_lowering=False)

    def din(name, shape):
        return nc.dram_tensor(name, list(shape), F32, kind="ExternalInput").ap()

    def dout(name, shape):
        return nc.dram_tensor(name, list(shape), F32, kind="ExternalOutput").ap()

    xT = din("xT", [1024, NT])
    pv_d = din("pv", [128, NPV])
    w_mod = din("w_mod", [2, 1024, 6144])
    w_qkv = din("w_qkv", [1024, 1152])
    w_u = din("w_u", [1024, 1024])
    w_ao = din("w_ao", [1024, 1024])
    w_ff1 = din("w_ff1", [2, 1024, 4096])
    w_ff2 = din("w_ff2", [2, 4096, 1024])
    lru_in = din("lru_in", [1024, 2048])
    lru_out = din("lru_out", [1024, 1024])
    lru_g = din("lru_g", [128, 32, 128])
    ckT = din("ckT", [128, 4, 256])
    cV = din("cV", [256, 128])
    tabs = din("tabs", [128, 2, 2048])
    cmats = din("cmats", [128, 4, 128])
    yT = dout("yT", [1024, NT])
    ko = dout("ko", [128, 2, 512])
    vo = dout("vo", [512, 128])
    so = dout("so", [128, 32])
    dbg = {}

    with ExitStack() as es:
        k = KB(nc, es)
        k.init_psum()
        off = [BASE]

        def alloc(name, shape, dt, base=None):
            nb = int(np.prod(shape[1:])) * (2 if dt == BF16 else 4)
            nb = (nb + 63) // 64 * 64
            if base is None:
                o = off[0]
                off[0] += nb
                assert off[0] <= TOP, (name, off[0])
                return k.at(name, shape, dt, o)
            return k.at(name, shape, dt, base)

        Yh = alloc("Y", [128, 8, NT], F32)
        Yt = [[Tl("Y%d_%d" % (c, b), Yh[:, c, b * 512:(b + 1) * 512]) for b in range(NB)] for c in range(8)]
        pvh = alloc("pv", [128, NPV], F32)
        pvt = Tl("pv", pvh[:, :])
        modh = alloc("mod", [128, 2, 48, 2], F32)
        modt = Tl("mod", modh[:, :, :, :])
        ABh = alloc("AB", [128, 2, 2, 8, 2], F32)
        ABt = Tl("AB", ABh[:, :, :, :, :])
        cmh = alloc("cm", [128, 4, 128], BF16)
        cmt = Tl("cm", cmh[:, :, :])
        onesh = alloc("ones", [128, 2, 128], BF16)
        onest = Tl("ones", onesh[:, :, :])
        esh = alloc("es", [128, 8], F32)
        est = Tl("es", esh[:, :])
        scb_h = alloc("scb", [128, 8, 2], BF16)
        scbt = Tl("scb", scb_h[:, :, :])
        PH = off[0]

        def pvs(name, i=0, n=1):
            o, w = PV[name]
            return pvh[:, o + i:o + i + n]

        k.dma("sp", pvh[:, :], pv_d, writes=[pvt], tile=pvt)
        for c in range(8):
            for b in range(NB):
                k.dma("sp", Yt[c][b].ap, xT[c * 128:(c + 1) * 128, b * 512:(b + 1) * 512],
                      writes=[Yt[c][b]], tile=Yt[c][b])
        k.dma("pool", cmh[:, :, :], cmats, writes=[cmt], tile=cmt)
        k.memset(onesh[:, 0, :], 1.0 / 1024, writes=[onest])
        k.memset(onesh[:, 1, :], 1.0 / 512, writes=[onest])
        k.act(esh[:, :], pvs("sink", 0, 8), AF.Exp, reads=[pvt], writes=[est])
        o_c = PV["cond"][0]
        k.act(scb_h[:, :, :], pvh[:, o_c:o_c + 16].rearrange("p (c r) -> p c r", r=2), AF.Silu,
              reads=[pvt], writes=[scbt])

        off[0] = PH
        wm = [alloc("wm%d" % i, [128, 8, 1024], BF16) for i in range(2)]
        wmt = [Tl("wm%d" % i, wm[i][:, :, :]) for i in range(2)]
        def adaln_finish(l, ps):
            ob = PV["b_mod"][0] + l * 48
            k.tt(modh[:, l, :, :], ps.ap[:, 0:96].rearrange("p (f r) -> p f r", r=2),
                 pvh[:, ob:ob + 48].unsqueeze(2).to_broadcast([128, 48, 2]), ALU.add,
                 reads=[ps, pvt], writes=[modt])
            for m, (gname, sj) in enumerate([("norm_mix", 1), ("norm_ffn", 4)]):
                og = PV[gname][0] + l * 8
                k.stt(ABh[:, l, m, :, :], modh[:, l, sj * 8:(sj + 1) * 8, :], 1.0,
                      pvh[:, og:og + 8].unsqueeze(2).to_broadcast([128, 8, 2]), ALU.add, ALU.mult,
                      reads=[modt, pvt], writes=[ABt])

        def adaln_evac(l, j, ps, c0):
            ob = PV["b_mod"][0] + l * 48 + j * 8
            k.tt(modh[:, l, j * 8:(j + 1) * 8, :], ps.ap[:, c0:c0 + 16].rearrange("p (f r) -> p f r", r=2),
                 pvh[:, ob:ob + 8].unsqueeze(2).to_broadcast([128, 8, 2]), ALU.add,
                 reads=[ps, pvt], writes=[modt])

        def adaln_AB(l, m):
            gname, sj = [("norm_mix", 1), ("norm_ffn", 4)][m]
            og = PV[gname][0] + l * 8
            k.stt(ABh[:, l, m, :, :], modh[:, l, sj * 8:(sj + 1) * 8, :], 1.0,
                  pvh[:, og:og + 8].unsqueeze(2).to_broadcast([128, 8, 2]), ALU.add, ALU.mult,
                  reads=[modt, pvt], writes=[ABt])

        ps = k.psum()
        for j in range(2):
            k.dma("pool", wm[j][:, :, :],
                  w_mod[0, :, j * 1024:(j + 1) * 1024].rearrange("(c p) n -> p c n", p=128),
                  writes=[wmt[j]], tile=wmt[j])
            for fc in range(8):
                col = (j * 8 + fc) * 2
                for kc in range(8):
                    k.mm(ps.ap[:, col:col + 2], wm[j][:, kc, fc * 128:(fc + 1) * 128],
                         scb_h[:, kc, :], kc == 0, kc == 7, reads=[wmt[j], scbt], writes=[ps])
        for j in range(2):
            adaln_evac(0, j, ps, j * 16)
        adaln_AB(0, 0)

        def modp(l, j, c, r):
            return modh[:, l, j * 8 + c, r:r + 1]

        def Ap(l, m, c, r):
            return ABh[:, l, m, c, r:r + 1]

        k.barrier()

        def rms_stats(b, rs_t, sq_h, sq_t):
            ps = k.psum()
            for c in range(8):
                k.act(sq_h[:, c % 2, :], Yt[c][b].ap, AF.Square, reads=[Yt[c][b]], writes=[sq_t[c % 2]])
                k.mm(ps.ap, onesh[:, 0, :], sq_h[:, c % 2, :], c == 0, c == 7,
                     reads=[onest, sq_t[c % 2]], writes=[ps])
            k.act(rs_t.ap, ps.ap, AF.Ln, reads=[ps, pvt], writes=[rs_t], bias=pvs("eps"), scale=1.0)
            k.act(rs_t.ap, rs_t.ap, AF.Exp, reads=[rs_t], writes=[rs_t], scale=-0.5)

        def modulate(b, l, m, rs_t, tmp_h, tmp_t, hb_h, hb_t):
            r = 0 if b == 0 else 1
            jb = 0 if m == 0 else 3
            for c in range(8):
                i = c % 2
                k.tt(tmp_h[:, i, :], Yt[c][b].ap, rs_t.ap, ALU.mult, reads=[Yt[c][b], rs_t], writes=[tmp_t[i]])
                k.act(hb_h[:, c, :], tmp_h[:, i, :], AF.Identity, reads=[tmp_t[i], ABt, modt], writes=[hb_t],
                      scale=Ap(l, m, c, r), bias=modp(l, jb, c, r))

        def norm_temps():
            sq_h = alloc("sq", [128, 2, 512], BF16)
            sq_t = [Tl("sq%d" % i, sq_h[:, i, :]) for i in range(2)]
            rs_h = alloc("rs", [128, 512], F32)
            rs_t = Tl("rs", rs_h[:, :])
            tmp_h = alloc("ntmp", [128, 2, 512], F32)
            tmp_t = [Tl("nt%d" % i, tmp_h[:, i, :]) for i in range(2)]
            return sq_h, sq_t, rs_t, tmp_h, tmp_t

        def ffn(l):
            off[0] = PH
            h2h = alloc("h2", [128, 8, NT], BF16)
            h2t = [Tl("h2_%d" % b, h2h[:, :, b * 512:(b + 1) * 512]) for b in range(NB)]
            hidh = alloc("hid", [128, 4, NT], BF16)
            hidt = [Tl("hid_%d" % b, hidh[:, :, b * 512:(b + 1) * 512]) for b in range(NB)]
            w1 = [alloc("w1_%d" % i, [128, 8, 512], BF16) for i in range(2)]
            w1t = [Tl("w1_%d" % i, w1[i][:, :, :]) for i in range(2)]
            w2 = [alloc("w2_%d" % i, [128, 4, 1024], BF16) for i in range(2)]
            w2t = [Tl("w2_%d" % i, w2[i][:, :, :]) for i in range(2)]
            rl_h = alloc("rl", [128, 2, 512], BF16)
            rl_t = [Tl("rl%d" % i, rl_h[:, i, :]) for i in range(2)]
            sq_h, sq_t, rs_t, tmp_h, tmp_t = norm_temps()
            if l == 0:
                wm1 = alloc("wm1x", [128, 8, 1024], BF16)
                wm1t = Tl("wm1x", wm1[:, :, :])
                pa = k.psum_tiles[6]
            def _normf(b):
                rms_stats(b, rs_t, sq_h, sq_t)
                modulate(b, l, 1, rs_t, tmp_h, tmp_t, h2h[:, :, b * 512:(b + 1) * 512], h2t[b])
            _normf(0)
            n = 0
            for j in range(8):
                i = j % 2
                k.dma("pool", w1[i][:, :, :],
                      w_ff1[l, :, j * 512:(j + 1) * 512].rearrange("(c p) n -> p c n", p=128),
                      writes=[w1t[i]], tile=w1t[i])
                k.dma("pool", w2[i][:, :, :],
                      w_ff2[l, j * 512:(j + 1) * 512, :].rearrange("(c p) n -> p c n", p=128),
                      writes=[w2t[i]], tile=w2t[i])
                if l == 0 and j < 6:
                    k.dma("pool", wm1[:, :, :],
                          w_mod[1, :, j * 1024:(j + 1) * 1024].rearrange("(c p) n -> p c n", p=128),
                          writes=[wm1t], tile=wm1t)
                for b in range(NB):
                    if j == 0 and b + 1 < NB:
                        _normf(b + 1)
                    for hc in range(4):
                        ps = k.psum()
                        for kc in range(8):
                            k.mm(ps.ap, w1[i][:, kc, hc * 128:(hc + 1) * 128], h2h[:, kc, b * 512:(b + 1) * 512],
                                 kc == 0, kc == 7, reads=[w1t[i], h2t[b]], writes=[ps])
                        ri = n % 2
                        n += 1
                        k.act(rl_h[:, ri, :], ps.ap, AF.Relu, reads=[ps], writes=[rl_t[ri]])
                        k.tt(hidh[:, hc, b * 512:(b + 1) * 512], rl_h[:, ri, :], rl_h[:, ri, :], ALU.mult,
                             reads=[rl_t[ri]], writes=[hidt[b]])
                for b in range(NB):
                    r = 0 if b == 0 else 1
                    for oc in range(8):
                        ps = k.psum()
                        for kc in range(4):
                            k.mm(ps.ap, w2[i][:, kc, oc * 128:(oc + 1) * 128], hidh[:, kc, b * 512:(b + 1) * 512],
                                 kc == 0, kc == 3, reads=[w2t[i], hidt[b]], writes=[ps])
                        k.stt(Yt[oc][b].ap, ps.ap, modp(l, 5, oc, r), Yt[oc][b].ap, ALU.mult, ALU.add,
                              reads=[ps, modt, Yt[oc][b]], writes=[Yt[oc][b]])
                if l == 0 and j < 6:
                    for fc in range(8):
                        col = (j * 8 + fc) * 2
                        for kc in range(8):
                            k.mm(pa.ap[:, col:col + 2], wm1[:, kc, fc * 128:(fc + 1) * 128],
                                 scb_h[:, kc, :], kc == 0, kc == 7, reads=[wm1t, scbt], writes=[pa])
                    if j == 5:
                        adaln_finish(1, pa)
            k.barrier()

        off[0] = PH
        GW = 286 + 286 + 2078
        OFF_QKV = off[0]
        qTh = alloc("qT", [128, 4, NT], BF16)
        qTt = [Tl("qT%d" % b, qTh[:, :, b * 512:(b + 1) * 512]) for b in range(NB)]
        kTh = alloc("kT", [128, 4, NT], BF16)
        kTt = [Tl("kT%d" % b, kTh[:, :, b * 512:(b + 1) * 512]) for b in range(NB)]
        Vh = alloc("Vaug", [128, 20, 2, 192], BF16)
        Vt = [Tl("V%d" % b, Vh[:, b * 4:(b + 1) * 4, :, :]) for b in range(NB)]
        OFF_ATT = off[0]
        attTh = alloc("attT", [128, 4, NT], BF16)
        attt = [Tl("att%d" % b, attTh[:, :, b * 512:(b + 1) * 512]) for b in range(NB)]
        gluh = alloc("glu", [128, 4, GW], BF16)
        OFF_T = off[0]
        off[0] = OFF_QKV
        cvTh = alloc("cvT", [128, 4, NT], BF16)
        cvt = [Tl("cv%d" % b, cvTh[:, :, b * 512:(b + 1) * 512]) for b in range(NB)]
        OFF_CV = off[0]

        def gcol(t):
            if t < 256:
                return 15 + t
            if t < 512:
                return 286 + 15 + (t - 256)
            return 572 + 15 + (t - 512)
        glut = [Tl("glu%d" % b, gluh[:, :, gcol(b * 512):gcol(b * 512) + 512]) for b in range(NB)]
        glupad = Tl("glupad", gluh[:, :, :])
        off[0] = OFF_ATT
        k.memset(Vh[:, :, :, 64:128], 1.0, writes=Vt)

        wq = alloc("wqkv", [128, 8, 1152], BF16)
        wqt = Tl("wqkv", wq[:, :, :])
        k.dma("pool", wq[:, :, :], w_qkv.rearrange("(c p) n -> p c n", p=128), writes=[wqt], tile=wqt)
        tabh = alloc("tabs", [128, 2, 2048], BF16)
        tabt = Tl("tabs", tabh[:, :, :])
        k.dma("pool", tabh[:, :, :], tabs, writes=[tabt], tile=tabt)
        hbs = [alloc("hb%d" % i, [128, 8, 512], BF16) for i in range(2)]
        hbts = [Tl("hb%d" % i, hbs[i][:, :, :]) for i in range(2)]
        sq_h, sq_t, rs_t, tmp_h, tmp_t = norm_temps()
        xbh = alloc("xb", [128, 512], BF16)
        xbt = Tl("xb", xbh[:, :])
        t1h = alloc("t1", [128, 2, 512], F32)
        t1t = [Tl("t1_%d" % i, t1h[:, i, :]) for i in range(2)]
        koh = alloc("koS", [128, 2, 512], F32)
        kot = Tl("koS", koh[:, :, :])
        voh = alloc("voS", [128, 4, 128], F32)
        vot = Tl("voS", voh[:, :, :])
        def _norm1(b):
            rms_stats(b, rs_t, sq_h, sq_t)
            modulate(b, 0, 0, rs_t, tmp_h, tmp_t, hbs[b % 2], hbts[b % 2])
        nb1 = NB if stage >= 1 else 0
        if nb1:
            _norm1(0)
        for b in range(nb1):
            hbh, hbt = hbs[b % 2], hbts[b % 2]
            if b + 1 < nb1:
                _norm1(b + 1)
            cols = slice(b * 512, (b + 1) * 512)
            for oc in range(8 if 'B' in SUB else 0):
                if b > 0 and 'C' not in SUB:
                    continue
                ps = k.psum()
                for kc in range(8):
                    k.mm(ps.ap, wq[:, kc, oc * 128:(oc + 1) * 128], hbh[:, kc, :], kc == 0, kc == 7,
                         reads=[wqt, hbt], writes=[ps])
                if oc < 4:
                    dst_ap, dst_t = qTh[:, oc, cols], qTt[b]
                else:
                    dst_ap, dst_t = kTh[:, oc - 4, cols], kTt[b]
                if b == 0:
                    if 'E' in SUB:
                        k.act(dst_ap, ps.ap, AF.Copy, reads=[ps], writes=[dst_t])
                    if oc in (4, 6) and 'K' in SUB:
                        k.copy(koh[:, (oc - 4) // 2, :], ps.ap, reads=[ps], writes=[kot])
                else:
                    tc_ = slice((b - 1) * 512, b * 512)
                    k.act(xbh[:, :], ps.ap, AF.Copy, reads=[ps], writes=[xbt])
                    ps2 = k.psum()
                    k.mm(ps2.ap, cmh[:, 0, :], xbh[:, :], True, True, reads=[cmt, xbt], writes=[ps2])
                    k.tt(t1h[:, 0, :], ps.ap, tabh[:, 0, tc_], ALU.mult, reads=[ps, tabt], writes=[t1t[0]])
                    k.tt(t1h[:, 1, :], ps2.ap, tabh[:, 1, tc_], ALU.mult, reads=[ps2, tabt], writes=[t1t[1]])
                    k.tt(dst_ap, t1h[:, 0, :], t1h[:, 1, :], ALU.add, reads=[t1t[0], t1t[1]], writes=[dst_t])
            for tt_ in range(4 if 'D' in SUB else 0):
                ps = k.psum()
                for kc in range(8):
                    k.mm(ps.ap[:, 0:128], hbh[:, kc, tt_ * 128:(tt_ + 1) * 128], wq[:, kc, 1024:1152],
                         kc == 0, kc == 7, reads=[wqt, hbt], writes=[ps])
                k.act(Vh[:, b * 4 + tt_, :, 0:64], ps.ap[:, 0:128].rearrange("p (a d) -> p a d", a=2), AF.Copy,
                      reads=[ps], writes=[Vt[b]])
                k.act(Vh[:, b * 4 + tt_, :, 128:192], ps.ap[:, 0:128].rearrange("p (a d) -> p a d", a=2), AF.Copy,
                      reads=[ps], writes=[Vt[b]])
                if b == 0:
                    k.copy(voh[:, tt_, :], ps.ap[:, 0:128], reads=[ps], writes=[vot])
            if b == 0 and 'D' in SUB:
                k.dma("sp", ko, koh[:, :, :], reads=[kot], tile=kot)
                k.dma("sp", vo.rearrange("(t p) d -> p t d", p=128), voh[:, :, :], reads=[vot], tile=vot)
        k.barrier()
        if stage >= 2:
            k.memset(gluh[:, :, :], 0.0, writes=[glupad] + glut)
            off[0] = OFF_ATT
            wu = alloc("wu", [128, 8, 1024], BF16)
            wut = Tl("wu", wu[:, :, :])
            k.dma("pool", wu[:, :, :], w_u.rearrange("(c p) n -> p c n", p=128), writes=[wut], tile=wut)
            sgh = alloc("sg", [128, 2, 512], F32)
            sgt = [Tl("sg%d" % i, sgh[:, i, :]) for i in range(2)]
            assert off[0] <= OFF_ATT + 20480, off[0]
            off[0] = OFF_T
            hbs = [alloc("hb%d" % i, [128, 8, 512], BF16) for i in range(2)]
            hbts = [Tl("hb%d" % i, hbs[i][:, :, :]) for i in range(2)]
            sq_h, sq_t, rs_t, tmp_h, tmp_t = norm_temps()
            def _norm2(b):
                rms_stats(b, rs_t, sq_h, sq_t)
                modulate(b, 0, 0, rs_t, tmp_h, tmp_t, hbs[b % 2], hbts[b % 2])
            _norm2(0)
            for b in range(NB):
                hbh, hbt = hbs[b % 2], hbts[b % 2]
                if b + 1 < NB:
                    _norm2(b + 1)
                for c in range(4):
                    psa = k.psum()
                    psg = k.psum()
                    for kc in range(8):
                        k.mm(psa.ap, wu[:, kc, c * 128:(c + 1) * 128], hbh[:, kc, :], kc == 0, kc == 7,
                             reads=[wut, hbt], writes=[psa])
                    for kc in range(8):
                        k.mm(psg.ap, wu[:, kc, 512 + c * 128:512 + (c + 1) * 128], hbh[:, kc, :], kc == 0, kc == 7,
                             reads=[wut, hbt], writes=[psg])
                    k.act(sgh[:, c % 2, :], psg.ap, AF.Sigmoid, reads=[psg], writes=[sgt[c % 2]])
                    if b == 0:
                        for s_ in range(2):
                            g0 = gcol(s_ * 256)
                            k.tt(gluh[:, c, g0:g0 + 256], psa.ap[:, s_ * 256:(s_ + 1) * 256],
                                 sgh[:, c % 2, s_ * 256:(s_ + 1) * 256], ALU.mult,
                                 reads=[psa, sgt[c % 2]], writes=[glut[b]])
                    else:
                        g0 = gcol(b * 512)
                        k.tt(gluh[:, c, g0:g0 + 512], psa.ap, sgh[:, c % 2, :], ALU.mult,
                             reads=[psa, sgt[c % 2]], writes=[glut[b]])
            k.barrier()
        if stage >= 3:
            off[0] = OFF_T
            ckh = alloc("ckT", [128, 4, 256], BF16)
            ckt = Tl("ckT", ckh[:, :, :])
            k.dma("pool", ckh[:, :, :], ckT, writes=[ckt], tile=ckt)
            cVh = alloc("cV", [128, 2, 2, 192], BF16)
            cVt = Tl("cV", cVh[:, :, :, :])
            k.memset(cVh[:, :, :, 64:128], 1.0, writes=[cVt])
            for t_ in range(2):
                for a_ in range(2):
                    for c0 in (0, 128):
                        k.dma("pool", cVh[:, t_, a_, c0:c0 + 64], cV[t_ * 128:(t_ + 1) * 128, a_ * 64:(a_ + 1) * 64],
                              writes=[cVt], tile=cVt)
            pth = alloc("pT", [128, 3, 512], BF16)
            ptt = [Tl("pT%d" % i, pth[:, i, :]) for i in range(3)]
            dnh = alloc("dn", [128, 2, 512], F32)
            dnt = [Tl("dn%d" % i, dnh[:, i, :]) for i in range(2)]
            k.memset(dnh[:, :, :], 1.0, writes=dnt)
            wmx = alloc("wm0x", [128, 8, 1024], BF16)
            wmxt = Tl("wm0x", wmx[:, :, :])

            def adaln0_dma(j):
                k.dma("pool", wmx[:, :, :],
                      w_mod[0, :, j * 1024:(j + 1) * 1024].rearrange("(c p) n -> p c n", p=128),
                      writes=[wmxt], tile=wmxt)

            def adaln0_compute(j):
                pa_ = k.psum()
                for fc in range(8):
                    for kc in range(8):
                        k.mm(pa_.ap[:, fc * 2:fc * 2 + 2], wmx[:, kc, fc * 128:(fc + 1) * 128],
                             scb_h[:, kc, :], kc == 0, kc == 7, reads=[wmxt, scbt], writes=[pa_])
                adaln_evac(0, j, pa_, 0)
                if j == 4:
                    adaln_AB(0, 1)
                if j < 5:
                    adaln0_dma(j + 1)
            adaln0_dma(2)
            pi = [0]
            units = []

            def attend(qcols, g, keylist, b_out):
                units.append(dict(qcols=qcols, g=g, kl=keylist, b_out=b_out))

            def S_stage(u, ki):
                qcols, g = u["qcols"], u["g"]
                kind, idx, mk = u["kl"][ki]
                ps = k.psum()
                for hl in range(4):
                    h = g * 4 + hl
                    if kind == "tok":
                        lhs = kTh[:, g * 2 + h % 2, idx * 128:(idx + 1) * 128]
                        rd = [kTt[idx // 4]]
                    else:
                        lhs = ckh[:, g * 2 + h % 2, idx * 128:(idx + 1) * 128]
                        rd = [ckt]
                    k.mm(ps.ap[:, hl * 128:(hl + 1) * 128], lhs, qTh[:, h // 2, qcols], True, True,
                         reads=rd + [qTt[qcols.start // 512]], writes=[ps])
                return ps

            def XP_stage(u, ki, ps):
                g = u["g"]
                kind, idx, mk = u["kl"][ki]
                nk = len(u["kl"])
                po = u["po"]
                i = pi[0] % 3
                pi[0] += 1
                k.act(pth[:, i, :], ps.ap, AF.Exp, reads=[ps], writes=[ptt[i]], scale=0.125)
                if mk is not None:
                    k.tt(pth[:, i, :].rearrange("p (h q) -> p h q", h=4),
                         pth[:, i, :].rearrange("p (h q) -> p h q", h=4),
                         cmh[:, mk, :].unsqueeze(1).to_broadcast([128, 4, 128]), ALU.mult,
                         reads=[ptt[i], cmt], writes=[ptt[i]])
                for hl in range(4):
                    vo_ = 0 if hl % 2 == 0 else 64
                    if kind == "tok":
                        lhs = Vh[:, idx, g, vo_:vo_ + 128]
                        rd = [Vt[idx // 4]]
                    else:
                        lhs = cVh[:, idx, g, vo_:vo_ + 128]
                        rd = [cVt]
                    k.mm(po.ap[:, hl * 128:(hl + 1) * 128], lhs, pth[:, i, hl * 128:(hl + 1) * 128],
                         ki == 0 and hl == 0, ki == nk - 1 and hl == 3, reads=rd + [ptt[i]], writes=[po])

            def A1(u):
                u["po"] = k.psum_acc()
                nk = len(u["kl"])
                pss = [S_stage(u, 0)]
                if nk > 1:
                    pss.append(S_stage(u, 1))
                for ki in range(nk):
                    if ki + 2 < nk:
                        pss.append(S_stage(u, ki + 2))
                    XP_stage(u, ki, pss[ki])

            def A2(u, ui):
                qcols, g, b_out, po = u["qcols"], u["g"], u["b_out"], u["po"]
                di = ui % 2
                for hl in range(4):
                    h = g * 4 + hl
                    ro = (h % 2) * 64
                    rd_ = 64 - ro
                    cs = slice(hl * 128, (hl + 1) * 128)
                    k.ts(dnh[ro:ro + 64, di, cs], po.ap[rd_:rd_ + 64, cs], esh[rd_:rd_ + 64, h:h + 1], None, ALU.add,
                         reads=[po, est], writes=[dnt[di]])
                k.act(dnh[:, di, :], dnh[:, di, :], AF.Ln, reads=[dnt[di]], writes=[dnt[di]])
                k.act(dnh[:, di, :], dnh[:, di, :], AF.Exp, reads=[dnt[di]], writes=[dnt[di]], scale=-1.0)
                for hl in range(4):
                    h = g * 4 + hl
                    ro = (h % 2) * 64
                    cs = slice(hl * 128, (hl + 1) * 128)
                    k.tt(attTh[ro:ro + 64, h // 2, qcols], po.ap[ro:ro + 64, cs], dnh[ro:ro + 64, di, cs], ALU.mult,
                         reads=[po, dnt[di]], writes=[attt[b_out]])

            for s_ in range(2):
                for qb in range(2):
                    t0 = s_ * 256 + qb * 128
                    for g in range(2):
                        attend(slice(t0, t0 + 128), g, [("tok", s_ * 2, None), ("tok", s_ * 2 + 1, None)], 0)
            for qb in range(16):
                t0 = 512 + qb * 128
                kl = []
                if qb > 0:
                    kl.append(("tok", 4 + qb - 1, 2))
                kl.append(("tok", 4 + qb, None))
                if qb < 15:
                    kl.append(("tok", 4 + qb + 1, 3))
                kl += [("ctx", 0, None), ("ctx", 1, None)]
                for g in range(2):
                    attend(slice(t0, t0 + 128), g, kl, 1 + qb // 4)
            A1(units[0])
            for ui in range(len(units)):
                if ui + 1 < len(units):
                    A1(units[ui + 1])
                A2(units[ui], ui)
                if ui in (7, 15, 23, 31):
                    adaln0_compute(2 + (ui - 7) // 8)
            k.barrier()
        if stage >= 4:
            off[0] = OFF_CV
            dgh = alloc("diag", [128, 4, 31, 128], BF16)
            assert off[0] <= OFF_ATT, off[0]
            off[0] = OFF_T
            dgts = [Tl("diag%d" % c, dgh[:, c, :, :]) for c in range(4)]
            o_w = PV["conv_w"][0]
            for c in range(4):
                for kk in range(31):
                    k.ts(dgh[:, c, kk, :], cmh[:, 1, :], pvh[:, o_w + c * 31 + kk:o_w + c * 31 + kk + 1], None, ALU.mult,
                         reads=[cmt, pvt], writes=[dgts[c]])
            zs = [alloc("z%d" % i, [128, 4, 512], F32) for i in range(2)]
            zts = [Tl("z%d" % i, zs[i][:, :, :]) for i in range(2)]
            zbh = alloc("zb", [128, 4, 512], BF16)
            zbt = Tl("zb", zbh[:, :, :])
            zqh = alloc("zq", [128, 4, 512], BF16)
            zqt = Tl("zq", zqh[:, :, :])
            assert off[0] <= TOP
            off[0] = OFF_CV + 31744
            muh = alloc("mu", [128, 512], F32)
            mut = Tl("mu", muh[:, :])
            vrh = alloc("vr", [128, 512], F32)
            vrt = Tl("vr", vrh[:, :])
            assert off[0] <= OFF_ATT, off[0]
            ob_ = PV["conv_b"][0]
            og_ = PV["cng"][0]
            obb = PV["cnb"][0]

            def convmm(b):
                zh, zt = zs[b % 2], zts[b % 2]
                subs = [(0, 256), (256, 256)] if b == 0 else [(0, 512)]
                for c in range(4):
                    ps = k.psum()
                    for (s0, n) in subs:
                        g0 = gcol(b * 512 + s0) - 15
                        for kk in range(31):
                            k.mm(ps.ap[:, s0:s0 + n], dgh[:, c, kk, :], gluh[:, c, g0 + kk:g0 + kk + n],
                                 kk == 0, kk == 30, reads=[dgts[c], glupad] + glut, writes=[ps])
                    k.act(zh[:, c, :], ps.ap, AF.Identity, reads=[ps, pvt], writes=[zt],
                          bias=pvh[:, ob_ + c:ob_ + c + 1], scale=1.0)

            def convln(b):
                zh, zt = zs[b % 2], zts[b % 2]
                k.copy(zbh[:, :, :], zh[:, :, :], reads=[zt], writes=[zbt])
                k.tt(zqh[:, :, :], zh[:, :, :], zh[:, :, :], ALU.mult, reads=[zt], writes=[zqt])
                pm = k.psum()
                pq = k.psum()
                for c in range(4):
                    k.mm(pm.ap, onesh[:, 1, :], zbh[:, c, :], c == 0, c == 3, reads=[onest, zbt], writes=[pm])
                for c in range(4):
                    k.mm(pq.ap, onesh[:, 1, :], zqh[:, c, :], c == 0, c == 3, reads=[onest, zqt], writes=[pq])
                k.copy(muh[:, :], pm.ap, reads=[pm], writes=[mut])
                k.tt(vrh[:, :], muh[:, :], muh[:, :], ALU.mult, reads=[mut], writes=[vrt])
                k.tt(vrh[:, :], pq.ap, vrh[:, :], ALU.subtract, reads=[pq, vrt], writes=[vrt])
                k.act(vrh[:, :], vrh[:, :], AF.Ln, reads=[vrt, pvt], writes=[vrt], bias=pvs("eps"), scale=1.0)
                k.act(vrh[:, :], vrh[:, :], AF.Exp, reads=[vrt], writes=[vrt], scale=-0.5)
                for c in range(4):
                    k.tt(zh[:, c, :], zh[:, c, :], muh[:, :], ALU.subtract, reads=[zt, mut], writes=[zt])
                    k.tt(zh[:, c, :], zh[:, c, :], vrh[:, :], ALU.mult, reads=[zt, vrt], writes=[zt])
                    k.act(cvTh[:, c, b * 512:(b + 1) * 512], zh[:, c, :], AF.Silu, reads=[zt, pvt], writes=[cvt[b]],
                          scale=pvh[:, og_ + c:og_ + c + 1], bias=pvh[:, obb + c:obb + c + 1])

            convmm(0)
            for b in range(NB):
                if b + 1 < NB:
                    convmm(b + 1)
                convln(b)
            k.barrier()
        if stage >= 5:
            off[0] = OFF_T
            wo = alloc("wo", [128, 8, 1024], BF16)
            wot = Tl("wo", wo[:, :, :])
            k.dma("pool", wo[:, :, :], w_ao.rearrange("(c p) n -> p c n", p=128), writes=[wot], tile=wot)
            for b in range(NB):
                r = 0 if b == 0 else 1
                cols = slice(b * 512, (b + 1) * 512)
                for oc in range(8):
                    ps = k.psum()
                    for kc in range(8):
                        rhs = attTh[:, kc, cols] if kc < 4 else cvTh[:, kc - 4, cols]
                        k.mm(ps.ap, wo[:, kc, oc * 128:(oc + 1) * 128], rhs, kc == 0, kc == 7,
                             reads=[wot, attt[b], cvt[b]], writes=[ps])
                    k.stt(Yt[oc][b].ap, ps.ap, modp(0, 2, oc, r), Yt[oc][b].ap, ALU.mult, ALU.add,
                          reads=[ps, modt, Yt[oc][b]], writes=[Yt[oc][b]])
            k.barrier()
        if stage >= 6:
            ffn(0)

        if stage >= 7:
            off[0] = PH
            RW = 259 + 259 + 2051
            rseg = [(1, 256, 0), (260, 256, 256), (519, 2048, 512)]
            rech = alloc("rec", [128, 8, RW], BF16)
            rect = Tl("rec", rech[:, :, :])
            ggh = alloc("gg", [128, 8, NT], BF16)
            ggt = [Tl("gg%d" % c, ggh[:, c, :]) for c in range(8)]
            PH3 = off[0]
            k.memset(rech[:, :, :], 0.0, writes=[rect])

            def rcol(t):
                if t < 256:
                    return 1 + t
                if t < 512:
                    return 260 + (t - 256)
                return 519 + (t - 512)
            for half in range(2):
                off[0] = PH3
                wl = alloc("wl", [128, 8, 1024], BF16)
                wlt = Tl("wl", wl[:, :, :])
                k.dma("pool", wl[:, :, :], lru_in[:, half * 1024:(half + 1) * 1024].rearrange("(c p) n -> p c n", p=128),
                      writes=[wlt], tile=wlt)
                hbs = [alloc("hb%d" % i, [128, 8, 512], BF16) for i in range(2)]
                hbts = [Tl("hb%d" % i, hbs[i][:, :, :]) for i in range(2)]
                sq_h, sq_t, rs_t, tmp_h, tmp_t = norm_temps()
                def _norm3(b, hbs=hbs, hbts=hbts, rs_t=rs_t, sq_h=sq_h, sq_t=sq_t, tmp_h=tmp_h, tmp_t=tmp_t):
                    rms_stats(b, rs_t, sq_h, sq_t)
                    modulate(b, 1, 0, rs_t, tmp_h, tmp_t, hbs[b % 2], hbts[b % 2])
                _norm3(0)
                for b in range(NB):
                    hbh, hbt = hbs[b % 2], hbts[b % 2]
                    if b + 1 < NB:
                        _norm3(b + 1)
                    for oc in range(8):
                        ps = k.psum()
                        for kc in range(8):
                            k.mm(ps.ap, wl[:, kc, oc * 128:(oc + 1) * 128], hbh[:, kc, :], kc == 0, kc == 7,
                                 reads=[wlt, hbt], writes=[ps])
                        if half == 0:
                            k.act(ggh[:, oc, b * 512:(b + 1) * 512], ps.ap, AF.Gelu_apprx_tanh, reads=[ps], writes=[ggt[oc]])
                        elif b == 0:
                            for s_ in range(2):
                                r0 = rcol(s_ * 256)
                                k.act(rech[:, oc, r0:r0 + 256], ps.ap[:, s_ * 256:(s_ + 1) * 256], AF.Copy,
                                      reads=[ps], writes=[rect])
                        else:
                            r0 = rcol(b * 512)
                            k.act(rech[:, oc, r0:r0 + 512], ps.ap, AF.Copy, reads=[ps], writes=[rect])
                k.barrier()
            off[0] = PH3
            wgh = alloc("wg", [128, 32, 128], BF16)
            wgt = Tl("wg", wgh[:, :, :])
            k.dma("pool", wgh[:, :, :], lru_g, writes=[wgt], tile=wgt)
            sch = alloc("sc", [128, 2, 16], F32)
            sct = Tl("sc", sch[:, :, :])
            o_l = PV["lam"][0]
            k.act(sch[:, 0, :], pvh[:, o_l:o_l + 16], AF.Exp, reads=[pvt], writes=[sct], scale=-1.0)
            k.act(sch[:, 0, :], sch[:, 0, :], AF.Ln, reads=[sct, pvt], writes=[sct], bias=pvs("one"), scale=1.0)
            k.ts(sch[:, 1, :], sch[:, 0, :], -4.0, None, ALU.mult, reads=[sct], writes=[sct])
            k.ts(sch[:, 0, :], sch[:, 0, :], -8.0, None, ALU.mult, reads=[sct], writes=[sct])
            hbh_ = alloc("hbias", [128, 48], F32)
            hbit = Tl("hbias", hbh_[:, :])
            k.ts(hbh_[:, 0:16], pvh[:, PV["lba"][0]:PV["lba"][0] + 16], 0.5, None, ALU.mult, reads=[pvt], writes=[hbit])
            k.ts(hbh_[:, 16:32], pvh[:, PV["lbx"][0]:PV["lbx"][0] + 16], 0.5, None, ALU.mult, reads=[pvt], writes=[hbit])
            k.ts(hbh_[:, 32:40], pvh[:, PV["lcb"][0]:PV["lcb"][0] + 8], 0.5, None, ALU.mult, reads=[pvt], writes=[hbit])
            k.memset(hbh_[:, 40:48], 0.5, writes=[hbit])
            dgl = alloc("dgl", [128, 4, 128], BF16)
            dglt = Tl("dgl", dgl[:, :, :])
            SW = 512
            XS = []
            for i in range(3):
                d_ = {}
                for nm, dt_ in (("xc", F32), ("xcb", BF16)):
                    h_ = alloc("%s%d" % (nm, i), [128, SW], dt_)
                    d_[nm] = h_
                    d_[nm + "_t"] = Tl("%s%d" % (nm, i), h_[:, :])
                XS.append(d_)
            sets = []
            for i in range(2):
                d_ = {}
                for nm, dt_ in (("A", F32), ("I", F32), ("T", F32), ("hbk", F32)):
                    h_ = alloc("%s%d" % (nm, i), [128, SW], dt_)
                    d_[nm] = h_
                    d_[nm + "_t"] = Tl("%s%d" % (nm, i), h_[:, :])
                sets.append(d_)
            Hfh = alloc("Hf", [128, 2048], F32)
            Hft = Tl("Hf", Hfh[:, :])
            carh = alloc("car", [128, 2], F32)
            cart = Tl("car", carh[:, :])
            soh = alloc("so", [128, 8, 2, 2], F32)
            sot = Tl("so", soh[:, :, :, :])
            o_cw = PV["lcw"][0]
            o_cb = PV["lcb"][0]
            o_s0 = PV["st0"][0]

            def rev(ap_h, col_last, n):
                a_ = ap_h[:, col_last:col_last + 1]
                return bass.AP(a_.tensor, a_.offset, [list(a_.ap[0]), [-1, n]])

            steps = []
            for c in range(8):
                steps.append(dict(c=c, d=0, si=-1, sub=0, nsub=1, sl=512, rs0=1, tok0=0, slen=512))
                steps.append(dict(c=c, d=1, si=-1, sub=0, nsub=1, sl=512, rs0=1, tok0=0, slen=512))
                for si, (rs0, slen, tok0) in enumerate(rseg):
                    if si < 2:
                        continue
                    sl = min(slen, SW)
                    nsub = slen // sl
                    for sub in range(nsub):
                        steps.append(dict(c=c, d=0, si=si, sub=sub, nsub=nsub, sl=sl, rs0=rs0, tok0=tok0, slen=slen))
                    for sub in reversed(range(nsub)):
                        steps.append(dict(c=c, d=1, si=si, sub=sub, nsub=nsub, sl=sl, rs0=rs0, tok0=tok0, slen=slen))
            cur_c = [-1]

            def F(kk):
                st = steps[kk]
                c, d, n = st["c"], st["d"], st["sl"]
                r0 = st["rs0"] + st["sub"] * n
                if c != cur_c[0]:
                    cur_c[0] = c
                    for j in range(4):
                        k.ts(dgl[:, j, :], cmh[:, 1, :], pvh[:, o_cw + c * 4 + j:o_cw + c * 4 + j + 1], None, ALU.mult,
                             reads=[cmt, pvt], writes=[dglt])
                X = XS[kk % 3]
                pc = k.psum()
                for j in range(4):
                    if st["si"] == -1:
                        a_ = rech[:, c, r0 - 1 + j:r0 + j]
                        rhs_ = bass.AP(a_.tensor, a_.offset, [list(a_.ap[0]), [259, 2], [1, 256]])
                        out_ = pc.ap[:, 0:512].rearrange("p (s t) -> p s t", s=2)
                    else:
                        rhs_ = rech[:, c, r0 - 1 + j:r0 - 1 + j + n]
                        out_ = pc.ap[:, 0:n]
                    k.mm(out_, dgl[:, j, :], rhs_, j == 0, j == 3, reads=[dglt, rect], writes=[pc])
                k.ts(X["xc"][:, 0:n], pc.ap[:, 0:n], hbh_[:, 40:41], hbh_[:, 32 + c:33 + c], ALU.mult, ALU.add,
                     reads=[pc, hbit], writes=[X["xc_t"]])
                k.ts(X["xcb"][:, 0:n], pc.ap[:, 0:n], pvh[:, o_cb + c:o_cb + c + 1], None, ALU.add,
                     reads=[pc, pvt], writes=[X["xcb_t"]])
                pr = k.psum()
                pi_ = k.psum()
                k.mm(pr.ap[:, 0:n], wgh[:, (0 * 2 + d) * 8 + c, :], X["xcb"][:, 0:n], True, True,
                     reads=[wgt, X["xcb_t"]], writes=[pr])
                k.mm(pi_.ap[:, 0:n], wgh[:, (1 * 2 + d) * 8 + c, :], X["xcb"][:, 0:n], True, True,
                     reads=[wgt, X["xcb_t"]], writes=[pi_])
                st["pr"], st["pi"] = pr, pi_

            def M(kk):
                st = steps[kk]
                c, d, n = st["c"], st["d"], st["sl"]
                S = sets[kk % 2]
                pr, pi_ = st["pr"], st["pi"]
                dc = d * 8 + c
                k.act(S["A"][:, 0:n], pr.ap[:, 0:n], AF.Tanh, reads=[pr, hbit], writes=[S["A_t"]],
                      bias=hbh_[:, dc:dc + 1], scale=0.5)
                k.act(S["I"][:, 0:n], pi_.ap[:, 0:n], AF.Tanh, reads=[pi_, hbit], writes=[S["I_t"]],
                      bias=hbh_[:, 16 + dc:17 + dc], scale=0.5)
                k.act(S["T"][:, 0:n], S["A"][:, 0:n], AF.Exp, reads=[S["A_t"], sct], writes=[S["T_t"]],
                      scale=sch[:, 0, dc:dc + 1], bias=sch[:, 0, dc:dc + 1])
                k.act(S["A"][:, 0:n], S["A"][:, 0:n], AF.Exp, reads=[S["A_t"], sct], writes=[S["A_t"]],
                      scale=sch[:, 1, dc:dc + 1], bias=sch[:, 1, dc:dc + 1])

            def M2(kk):
                st = steps[kk]
                n = st["sl"]
                S = sets[kk % 2]
                k.act(S["T"][:, 0:n], S["T"][:, 0:n], AF.Sqrt, reads=[S["T_t"], pvt], writes=[S["T_t"]],
                      bias=pvs("one"), scale=-1.0)

            def scan(out_ap, a_ap, b_ap, init, reads, writes):
                k.add("dve", lambda e: e.tensor_tensor_scan(out_ap, a_ap, b_ap, init, ALU.mult, ALU.add),
                      reads=reads, writes=writes)

            def B(kk):
                st = steps[kk]
                c, d, n, si, sub, nsub = st["c"], st["d"], st["sl"], st["si"], st["sub"], st["nsub"]
                S = sets[kk % 2]
                X = XS[kk % 3]
                k.stt(S["I"][:, 0:n], S["I"][:, 0:n], 1.0, X["xc"][:, 0:n], ALU.add, ALU.mult,
                      reads=[S["I_t"], X["xc_t"]], writes=[S["I_t"]])
                k.tt(S["T"][:, 0:n], S["T"][:, 0:n], S["I"][:, 0:n], ALU.mult, reads=[S["T_t"], S["I_t"]], writes=[S["T_t"]])
                if si == -1 and d == 0:
                    for q_ in range(2):
                        cs_ = slice(q_ * 256, (q_ + 1) * 256)
                        scan(Hfh[:, cs_], S["A"][:, cs_], S["T"][:, cs_], 0.0, [S["A_t"], S["T_t"], Hft], [Hft])
                        k.copy(soh[:, c, q_, 0:1], Hfh[:, q_ * 256 + 255:q_ * 256 + 256], reads=[Hft], writes=[sot])
                elif si == -1:
                    for q_ in range(2):
                        lo = q_ * 256
                        scan(rev(S["hbk"], lo + 255, 256), rev(S["A"], lo + 255, 256), rev(S["T"], lo + 255, 256), 0.0,
                             [S["A_t"], S["T_t"]], [S["hbk_t"]])
                        k.copy(soh[:, c, q_, 1:2], S["hbk"][:, lo:lo + 1], reads=[S["hbk_t"]], writes=[sot])
                    k.tt(S["hbk"][:, 0:512], S["hbk"][:, 0:512], Hfh[:, 0:512], ALU.add,
                         reads=[S["hbk_t"], Hft], writes=[S["hbk_t"]])
                    k.tt(ggh[:, c, 0:512], S["hbk"][:, 0:512], ggh[:, c, 0:512], ALU.mult,
                         reads=[S["hbk_t"], ggt[c]], writes=[ggt[c]])
                elif d == 0:
                    if sub == 0:
                        init = pvh[:, o_s0 + c * 2:o_s0 + c * 2 + 1] if si == 2 else 0.0
                    else:
                        init = Hfh[:, sub * n - 1:sub * n]
                    scan(Hfh[:, sub * n:(sub + 1) * n], S["A"][:, 0:n], S["T"][:, 0:n], init,
                         [S["A_t"], S["T_t"], pvt, Hft], [Hft])
                    if si < 2 and sub == nsub - 1:
                        k.copy(soh[:, c, si, 0:1], Hfh[:, st["slen"] - 1:st["slen"]], reads=[Hft], writes=[sot])
                else:
                    if sub == nsub - 1:
                        init = pvh[:, o_s0 + c * 2 + 1:o_s0 + c * 2 + 2] if si == 2 else 0.0
                    else:
                        init = carh[:, 0:1]
                    scan(rev(S["hbk"], n - 1, n), rev(S["A"], n - 1, n), rev(S["T"], n - 1, n), init,
                         [S["A_t"], S["T_t"], pvt, cart], [S["hbk_t"]])
                    k.copy(carh[:, 0:1], S["hbk"][:, 0:1], reads=[S["hbk_t"]], writes=[cart])
                    if si < 2:
                        k.copy(soh[:, c, si, 1:2], S["hbk"][:, 0:1], reads=[S["hbk_t"]], writes=[sot])
                    k.tt(S["hbk"][:, 0:n], S["hbk"][:, 0:n], Hfh[:, sub * n:(sub + 1) * n], ALU.add,
                         reads=[S["hbk_t"], Hft], writes=[S["hbk_t"]])
                    t0 = st["tok0"] + sub * n
                    k.tt(ggh[:, c, t0:t0 + n], S["hbk"][:, 0:n], ggh[:, c, t0:t0 + n], ALU.mult,
                         reads=[S["hbk_t"], ggt[c]], writes=[ggt[c]])

            NS = len(steps)
            F(0)
            for kk in range(0, NS, 2):
                if kk + 1 < NS:
                    F(kk + 1)
                M(kk)
                if kk + 2 < NS:
                    F(kk + 2)
                if kk + 1 < NS:
                    M(kk + 1)
                M2(kk)
                if kk + 1 < NS:
                    M2(kk + 1)
                B(kk)
                if kk + 1 < NS:
                    B(kk + 1)
            k.dma("sp", so, soh[:, :, :, :].rearrange("p a b c -> p (a b c)"), reads=[sot], tile=sot)
            k.barrier()
            off[0] = PH3
            wo = alloc("wlo", [128, 8, 1024], BF16)
            wot = Tl("wlo", wo[:, :, :])
            k.dma("pool", wo[:, :, :], lru_out.rearrange("(c p) n -> p c n", p=128), writes=[wot], tile=wot)
            for b in range(NB):
                r = 0 if b == 0 else 1
                for oc in range(8):
                    ps = k.psum()
                    for kc in range(8):
                        k.mm(ps.ap, wo[:, kc, oc * 128:(oc + 1) * 128], ggh[:, kc, b * 512:(b + 1) * 512],
                             kc == 0, kc == 7, reads=[wot, ggt[kc]], writes=[ps])
                    k.stt(Yt[oc][b].ap, ps.ap, modp(1, 2, oc, r), Yt[oc][b].ap, ALU.mult, ALU.add,
                          reads=[ps, modt, Yt[oc][b]], writes=[Yt[oc][b]])
            k.barrier()
        if stage >= 8:
            ffn(1)
        off[0] = PH
        sq_h, sq_t, rs_t, tmp_h, tmp_t = norm_temps()
        o_f = PV["final"][0]
        for b in range(NB):
            if stage >= 9:
                rms_stats(b, rs_t, sq_h, sq_t)
            for c in range(8):
                if stage >= 9:
                    k.stt(Yt[c][b].ap, Yt[c][b].ap, pvh[:, o_f + c:o_f + c + 1], rs_t.ap, ALU.mult, ALU.mult,
                          reads=[Yt[c][b], pvt, rs_t], writes=[Yt[c][b]])
                k.dma("sp", yT[c * 128:(c + 1) * 128, b * 512:(b + 1) * 512], Yt[c][b].ap,
                      reads=[Yt[c][b]], tile=Yt[c][b])
        k.finalize()
    return nc


def _pp(v, nch):
    return np.ascontiguousarray(np.asarray(v, np.float32).reshape(nch, 128).T)


def _rope_tables():
    t = np.arange(2048)
    row = (t // 64).astype(np.float32)
    col = (t % 64).astype(np.float32)
    inv = (10000.0 ** (-np.arange(16, dtype=np.float32) / 16)).astype(np.float32)
    cos = np.zeros((128, 2048), np.float32)
    sin = np.zeros((128, 2048), np.float32)
    for p in range(128):
        d = p % 64
        pos = row if d < 32 else col
        ang = (pos * inv[d % 16]).astype(np.float32)
        cos[p] = np.cos(ang)
        sin[p] = np.sin(ang)
    return np.stack([cos, sin], axis=1)


def _const_mats():
    rot = np.zeros((128, 128), np.float32)
    for fp in range(128):
        d = fp % 32
        if d < 16:
            rot[fp + 16, fp] = -1.0
        else:
            rot[fp - 16, fp] = 1.0
    ident = np.eye(128, dtype=np.float32)
    p = np.arange(128)[:, None]
    c = np.arange(128)[None, :]
    mprev = (c <= p).astype(np.float32)
    mnext = (p <= c).astype(np.float32)
    return np.stack([rot, ident, mprev, mnext], axis=1)


_NC_CACHE = {}


def kernel(x_prompt, x_sample, c, cache_k, cache_v, state_lru, c_ctx,
           w_mod, b_mod, norm_mix, norm_ffn, w_ff1, w_ff2,
           att_in, att_out, att_sink, conv_w, conv_b, conv_norm_g, conv_norm_b,
           lru_in, lru_out, lru_conv_w, lru_conv_b, lru_wa, lru_ba, lru_wx, lru_bx, lru_lam,
           final_norm, _stage=None):
    stage = STAGE if _stage is None else _stage
    f = lambda a: np.asarray(a, np.float32)
    x_prompt, x_sample, c, cache_k, cache_v, state_lru, c_ctx = map(f, (x_prompt, x_sample, c, cache_k, cache_v, state_lru, c_ctx))
    att_in = f(att_in)[0]
    zc = np.zeros((1024, 64), np.float32)
    k0, k1 = att_in[:, 512:576], att_in[:, 576:640]
    w_qkv = np.ascontiguousarray(np.concatenate(
        [att_in[:, 0:512], k0, zc, zc, k0, k1, zc, zc, k1, att_in[:, 640:768]], axis=1))
    w_u = np.ascontiguousarray(att_in[:, 768:1792])
    lru_g = np.ascontiguousarray(np.stack([f(lru_wa)[0], f(lru_wx)[0]], 0).reshape(32, 128, 128).transpose(1, 0, 2))
    tabs = _rope_tables()
    cm = _const_mats()
    shared = {
        "w_mod": f(w_mod), "w_qkv": w_qkv, "w_u": w_u, "w_ao": f(att_out)[0], "w_ff1": f(w_ff1), "w_ff2": f(w_ff2),
        "lru_in": f(lru_in)[0], "lru_out": f(lru_out)[0], "lru_g": lru_g, "tabs": tabs, "cmats": cm,
    }
    pv_common = {
        "b_mod": np.concatenate([_pp(f(b_mod)[l], 48) for l in range(2)], 1),
        "norm_mix": np.concatenate([_pp(f(norm_mix)[l], 8) for l in range(2)], 1),
        "norm_ffn": np.concatenate([_pp(f(norm_ffn)[l], 8) for l in range(2)], 1),
        "final": _pp(f(final_norm), 8),
        "conv_w": np.ascontiguousarray(f(conv_w)[0].reshape(31, 4, 128).transpose(2, 1, 0)).reshape(128, 124),
        "conv_b": _pp(f(conv_b)[0], 4), "cng": _pp(f(conv_norm_g)[0], 4), "cnb": _pp(f(conv_norm_b)[0], 4),
        "lcw": np.ascontiguousarray(f(lru_conv_w)[0].reshape(4, 8, 128).transpose(2, 1, 0)).reshape(128, 32),
        "lcb": _pp(f(lru_conv_b)[0], 8),
        "lba": np.concatenate([_pp(f(lru_ba)[0, d], 8) for d in range(2)], 1),
        "lbx": np.concatenate([_pp(f(lru_bx)[0, d], 8) for d in range(2)], 1),
        "lam": np.concatenate([_pp(f(lru_lam)[0, d], 8) for d in range(2)], 1),
        "sink": np.broadcast_to(f(att_sink)[0][None, :], (128, 8)),
        "eps": np.full((128, 1), EPS, np.float32), "one": np.ones((128, 1), np.float32),
    }
    in_maps = []
    for core in range(8):
        s = core // 2
        xcat = np.concatenate([x_prompt[2 * core], x_prompt[2 * core + 1], x_sample[s]], axis=0)
        pv = np.zeros((128, NPV), np.float32)
        for name, val in pv_common.items():
            o, w = PV[name]
            pv[:, o:o + w] = val
        o, w = PV["st0"]
        st = state_lru[s, 0]
        pv[:, o:o + w] = np.stack([_pp(st[0], 8), _pp(st[1], 8)], axis=2).reshape(128, 16)
        o, w = PV["cond"]
        pv[:, o:o + w] = np.stack([_pp(c_ctx, 8), _pp(c[s], 8)], axis=2).reshape(128, 16)
        ck = cache_k[s, 0]
        zk = np.zeros((64, 256), np.float32)
        ckT = np.stack([np.concatenate([ck[:, 0, :].T, zk], 0), np.concatenate([zk, ck[:, 0, :].T], 0),
                        np.concatenate([ck[:, 1, :].T, zk], 0), np.concatenate([zk, ck[:, 1, :].T], 0)], axis=1)
        m = dict(shared)
        m.update({"xT": np.ascontiguousarray(xcat.T), "pv": pv, "ckT": np.ascontiguousarray(ckT),
                  "cV": np.ascontiguousarray(cache_v[s, 0].reshape(256, 128))})
        in_maps.append(m)
    if stage not in _NC_CACHE:
        _NC_CACHE[stage] = build_program(stage)
    nc = _NC_CACHE[stage]
    res = run_bass_kernel_spmd(nc, in_maps, core_ids=list(range(8)))
    y_prompt = np.zeros((16, 256, 1024), np.float32)
    y_sample = np.zeros((4, 2048, 1024), np.float32)
    nk = np.zeros((16, 1, 256, 2, 64), np.float32)
    nv = np.zeros((16, 1, 256, 2, 64), np.float32)
    nh = np.zeros((16, 1, 2, 1024), np.float32)
    for core in range(8):
        r = res.results[core]
        y = r["yT"].T
        y_prompt[2 * core] = y[0:256]
        y_prompt[2 * core + 1] = y[256:512]
        if core % 2 == 0:
            y_sample[core // 2] = y[512:]
        ko = r["ko"]
        vo = r["vo"]
        so = r["so"].reshape(128, 8, 2, 2)
        for j in range(2):
            for g in range(2):
                nk[2 * core + j, 0, :, g, :] = ko[0:64, g, j * 256:(j + 1) * 256].T
            nv[2 * core + j, 0] = vo[j * 256:(j + 1) * 256].reshape(256, 2, 64)
            for d in range(2):
                nh[2 * core + j, 0, d] = so[:, :, j, d].T.reshape(1024)
    return (y_prompt, y_sample, nk, nv, nh)
```
